# Optimizing a Trainium2 kernel written in Bass

```python
import math
import jax
import jax.numpy as jnp
from jax import lax
import numpy as np

D_MODEL = 1024
BATCH = 4
SEQ = 4096
DEPTH = 2

GRID_W = 64
CTX_LEN = 256
EPS = 1e-6
D_MIX = D_MODEL
D_FF = 4 * D_MODEL
N_MOD = 6

S5_WIDTH = D_MIX // 4
S5_GROUP = 16
S5_GROUPS = S5_WIDTH // S5_GROUP
S5_STATE = 64

GDN_HEAD = 64
GDN_HEADS = (D_MIX - S5_WIDTH) // (2 * GDN_HEAD)
GDN_WIDTH = GDN_HEADS * GDN_HEAD
GDN_CONV = 5
GDN_CHUNK = 64

RWKV_HEAD = 64
RWKV_WIDTH = D_MIX - S5_WIDTH - GDN_WIDTH
RWKV_HEADS = RWKV_WIDTH // RWKV_HEAD
DECAY_LORA = 64
ICLR_LORA = 64
GATE_LORA = 128
RWKV_GN_EPS = 64e-5

S5_COLS = S5_WIDTH
GDN_COLS = 4 * GDN_WIDTH + 4 * GDN_HEADS
RWKV_COLS = 3 * RWKV_WIDTH + 2 * DECAY_LORA + 2 * ICLR_LORA + GATE_LORA
IN_COLS = S5_COLS + GDN_COLS + RWKV_COLS

kernel_name = 'hybrid_s5_gdn_rwkv7_prefix_dit'

f32 = jnp.float32


def to_heads(t, n):
    return t.reshape(t.shape[:-1] + (n, t.shape[-1] // n))


def rmsnorm(t, gain):
    t = t.astype(f32)
    return t * lax.rsqrt(jnp.mean(t * t, axis=-1, keepdims=True) + EPS) * gain.astype(f32)


def l2norm(t):
    return t * lax.rsqrt(jnp.sum(t * t, axis=-1, keepdims=True) + EPS)


def sq_relu_mlp(h, w1, w2):
    return jnp.square(jax.nn.relu(h @ w1)) @ w2


def centred_dwconv(t, w):
    pad = w.shape[0] // 2
    return lax.conv_general_dilated(t, w[:, None, :].astype(t.dtype), window_strides=(1,), padding=[(pad, pad)], dimension_numbers=('NWC', 'WIO', 'NWC'), feature_group_count=t.shape[-1])


def grid_qshift(t):
    B, L, C = t.shape
    rows = L // GRID_W
    gp = jnp.pad(t.reshape(B, rows, GRID_W, C), ((0, 0), (1, 1), (1, 1), (0, 0)))
    left, right = gp[:, 1:-1, :-2], gp[:, 1:-1, 2:]
    up, down = gp[:, :-2, 1:-1], gp[:, 2:, 1:-1]
    d = jnp.arange(C) % 4
    out = jnp.where(d == 0, left, jnp.where(d == 1, right, jnp.where(d == 2, up, down)))
    return out.reshape(B, L, C)


def seq_shift(t):
    tp = jnp.pad(t, ((0, 0), (1, 1), (0, 0)))
    even = jnp.arange(t.shape[-1]) % 2 == 0
    return jnp.where(even, tp[:, :-2], tp[:, 2:])


def _linear_op(e1, e2):
    a1, b1 = e1
    a2, b2 = e2
    return a2 * a1, a2 * b1 + b2


def s5_discretise(a_re, a_im, log_dt):
    A = lax.complex(a_re.astype(f32), a_im.astype(f32))
    dt = jnp.exp(log_dt.astype(f32))[:, None]
    a_bar = jnp.exp(A * dt)
    return a_bar, (a_bar - 1.0) / A


def s5_scan(a_bar, bu, h0, reverse):
    if h0 is not None:
        edge = -1 if reverse else 0
        bu = bu.at[:, edge].add(a_bar * h0)
    a = jnp.broadcast_to(a_bar, bu.shape)
    _, h = lax.associative_scan(_linear_op, (a, bu), reverse=reverse, axis=1)
    return h


def s5_mixer(u_ctx, u_lat, b_re, b_im, c_re, c_im, d, a_re, a_im, log_dt, glu_w, glu_b, ctx_out):
    b_re, b_im, c_re, c_im = (t.astype(f32) for t in (b_re, b_im, c_re, c_im))
    (abar_f, coef_f), (abar_b, coef_b) = (s5_discretise(a_re[i], a_im[i], log_dt[i]) for i in range(2))

    def drive(u, coef):
        ug = to_heads(u, S5_GROUPS)
        bu = lax.complex(jnp.einsum('gpc,blgc->blgp', b_re, ug), jnp.einsum('gpc,blgc->blgp', b_im, ug))
        return coef * bu

    def readout(h, u):
        ug = to_heads(u, S5_GROUPS)
        y = (jnp.einsum('gcp,blgp->blgc', c_re, jnp.real(h)) - jnp.einsum('gcp,blgp->blgc', c_im, jnp.imag(h)) + to_heads(d, S5_GROUPS) * ug)
        z = jax.nn.gelu(y.reshape(u.shape))
        return z * jax.nn.sigmoid(z @ glu_w + glu_b)

    hc_f = s5_scan(abar_f, drive(u_ctx, coef_f), None, False)
    hc_b = s5_scan(abar_b, drive(u_ctx, coef_b), None, True)
    hl_f = s5_scan(abar_f, drive(u_lat, coef_f), hc_f[:, -1], False)
    hl_b = s5_scan(abar_b, drive(u_lat, coef_b), hc_b[:, 0], True)
    y_lat = readout(hl_f + hl_b, u_lat)
    y_ctx = readout(hc_f + hc_b, u_ctx) if ctx_out else None
    return y_ctx, y_lat


def chunk_gated_delta(q, k, v, g, beta, s0, emit):
    B, L, H, K = q.shape
    V = v.shape[-1]
    C = GDN_CHUNK
    N = L // C

    def chunks(t):
        return jnp.moveaxis(t.reshape((B, N, C) + t.shape[2:]), 3, 1)

    q, k, v, beta = chunks(q), chunks(k), chunks(v), chunks(beta)
    g = jnp.cumsum(chunks(g), axis=-1)
    causal = jnp.tril(jnp.ones((C, C), bool))
    strict = jnp.tril(jnp.ones((C, C), bool), k=-1)
    decay = jnp.exp(jnp.where(causal, g[..., :, None] - g[..., None, :], -jnp.inf))
    kb = k * beta[..., None]
    m = jnp.where(strict, jnp.einsum('bhnik,bhnjk->bhnij', kb, k) * decay, 0.0)
    rhs = jnp.concatenate([v * beta[..., None], kb * jnp.exp(g)[..., None]], axis=-1)
    sol = lax.linalg.triangular_solve(m + jnp.eye(C, dtype=m.dtype), rhs, left_side=True, lower=True, unit_diagonal=True)
    u, w = sol[..., :V], sol[..., V:]
    g_last = g[..., -1]
    kd = k * jnp.exp(g_last[..., None] - g)[..., None]
    to_front = lambda t: jnp.moveaxis(t, 2, 0)
    if emit:
        qg = q * jnp.exp(g)[..., None]
        attn = jnp.einsum('bhnik,bhnjk->bhnij', q, k) * decay
        xs = tuple(to_front(t) for t in (kd, u, w, g_last, qg, attn))
    else:
        xs = tuple(to_front(t) for t in (kd, u, w, g_last))

    def step(s, inp):
        kd_n, u_n, w_n, gl_n = inp[:4]
        v_new = u_n - jnp.einsum('bhck,bhkv->bhcv', w_n, s)
        s_next = s * jnp.exp(gl_n)[..., None, None] + jnp.einsum('bhck,bhcv->bhkv', kd_n, v_new)
        if emit:
            qg_n, attn_n = inp[4:]
            o = jnp.einsum('bhck,bhkv->bhcv', qg_n, s) + jnp.einsum('bhij,bhjv->bhiv', attn_n, v_new)
            return s_next, o
        return s_next, None

    s_final, o = lax.scan(step, s0, xs)
    if emit:
        o = jnp.moveaxis(jnp.moveaxis(o, 0, 2), 1, 3).reshape(B, L, H, V)
    return o, s_final


def gdn_prep(p, conv_w, a_log, dt_bias):
    W, H = GDN_WIDTH, GDN_HEADS
    qkv = jax.nn.silu(centred_dwconv(p[..., :3 * W], conv_w))
    q = l2norm(to_heads(qkv[..., :W], H)) * GDN_HEAD ** -0.5
    k = l2norm(to_heads(qkv[..., W:2 * W], H))
    v = to_heads(qkv[..., 2 * W:], H)
    z = to_heads(p[..., 3 * W:4 * W], H)
    beta = jax.nn.sigmoid(to_heads(p[..., 4 * W:4 * W + 2 * H], 2))
    g = -jnp.exp(a_log) * jax.nn.softplus(to_heads(p[..., 4 * W + 2 * H:], 2) + dt_bias)
    return q, k, v, z, beta, g


def gdn_mixer(p_ctx, p_lat, conv_w, a_log, dt_bias, norm_w, ctx_out):
    qc, kc, vc, zc, bc, gc = gdn_prep(p_ctx, conv_w, a_log, dt_bias)
    ql, kl, vl, zl, bl, gl = gdn_prep(p_lat, conv_w, a_log, dt_bias)
    s0 = jnp.zeros((p_lat.shape[0], GDN_HEADS, GDN_HEAD, GDN_HEAD), f32)
    rev = lambda t: jnp.flip(t, axis=1)
    oc_f, sc_f = chunk_gated_delta(qc, kc, vc, gc[:, :, 0], bc[:, :, 0], s0, ctx_out)
    oc_b, sc_b = chunk_gated_delta(rev(qc), rev(kc), rev(vc), rev(gc[:, :, 1]), rev(bc[:, :, 1]), s0, ctx_out)
    ol_f, _ = chunk_gated_delta(ql, kl, vl, gl[:, :, 0], bl[:, :, 0], sc_f, True)
    ol_b, _ = chunk_gated_delta(rev(ql), rev(kl), rev(vl), rev(gl[:, :, 1]), rev(bl[:, :, 1]), sc_b, True)

    def readout(o, z):
        return (rmsnorm(o, norm_w) * jax.nn.silu(z)).reshape(z.shape[0], z.shape[1], GDN_WIDTH)

    y_lat = readout(ol_f + rev(ol_b), zl)
    y_ctx = readout(oc_f + rev(oc_b), zc) if ctx_out else None
    return y_ctx, y_lat


def rwkv_prep(p, shifted, mu, w0, w_up, a0, a_up, k_k, k_a):
    B, L, _ = p.shape
    W, H = RWKV_WIDTH, RWKV_HEADS
    p = p + (shifted - p) * mu
    r, k, v = p[..., :W], p[..., W:2 * W], p[..., 2 * W:3 * W]
    o1 = 3 * W
    o2 = o1 + 2 * DECAY_LORA
    o3 = o2 + 2 * ICLR_LORA
    w_dn = p[..., o1:o2].reshape(B, L, 2, DECAY_LORA)
    a_dn = p[..., o2:o3].reshape(B, L, 2, ICLR_LORA)
    g_dn = p[..., o3:]
    w_log = -jax.nn.softplus(-(w0 + jnp.einsum('bldr,drw->bldw', jnp.tanh(w_dn), w_up))) - 0.5
    decay = jnp.exp(-jnp.exp(w_log))
    iclr = jax.nn.sigmoid(a0 + jnp.einsum('bldr,drw->bldw', a_dn, a_up))
    kk = l2norm(to_heads(k * k_k, H))
    k_dir = k[:, :, None, :] * (1.0 + (iclr - 1.0) * k_a)
    b_dir = kk.reshape(B, L, 1, W) * iclr
    heads = lambda t: to_heads(t, H)
    return heads(r), heads(v), kk, heads(decay), heads(k_dir), heads(b_dir), g_dn


def rwkv7_scan(r, decay, k, v, kk, b, s0, reverse, emit):
    def step(s, inp):
        r_t, w_t, k_t, v_t, kk_t, b_t = inp
        sa = jnp.einsum('bhvk,bhk->bhv', s, kk_t)
        s = s * w_t[:, :, None, :] - sa[..., None] * b_t[:, :, None, :] + v_t[..., None] * k_t[:, :, None, :]
        return s, (jnp.einsum('bhvk,bhk->bhv', s, r_t) if emit else None)

    xs = tuple(jnp.moveaxis(t, 1, 0) for t in (r, decay, k, v, kk, b))
    s_final, y = lax.scan(step, s0, xs, reverse=reverse)
    return (jnp.moveaxis(y, 0, 1) if emit else None), s_final


def rwkv_mixer(p_ctx, p_lat, mu, w0, w_up, a0, a_up, g_up, k_k, k_a, r_k, ln_w, ln_b, ctx_out):
    args = (mu, w0, w_up, a0, a_up, k_k, k_a)
    ctx_t = rwkv_prep(p_ctx, seq_shift(p_ctx), *args)
    lat_t = rwkv_prep(p_lat, grid_qshift(p_lat), *args)
    s0 = jnp.zeros((p_lat.shape[0], RWKV_HEADS, RWKV_HEAD, RWKV_HEAD), f32)

    def bidir(t, s0_f, s0_b, emit):
        r, v, kk, decay, k_dir, b_dir, _ = t
        y_f, s_f = rwkv7_scan(r, decay[:, :, 0], k_dir[:, :, 0], v, kk, b_dir[:, :, 0], s0_f, False, emit)
        y_b, s_b = rwkv7_scan(r, decay[:, :, 1], k_dir[:, :, 1], v, kk, b_dir[:, :, 1], s0_b, True, emit)
        return y_f, y_b, s_f, s_b

    def readout(y, t):
        r, v, _, _, k_dir, _, g_dn = t
        B, L = y.shape[:2]
        mean = jnp.mean(y, axis=-1, keepdims=True)
        var = jnp.mean(jnp.square(y - mean), axis=-1, keepdims=True)
        yn = ((y - mean) * lax.rsqrt(var + RWKV_GN_EPS)).reshape(B, L, RWKV_WIDTH) * ln_w + ln_b
        bonus = jnp.sum(r[:, :, None] * k_dir * r_k, axis=(2, 4))[..., None] * v
        gate = jax.nn.sigmoid(g_dn) @ g_up
        return (yn + bonus.reshape(B, L, RWKV_WIDTH)) * gate

    yc_f, yc_b, sc_f, sc_b = bidir(ctx_t, s0, s0, ctx_out)
    yl_f, yl_b, _, _ = bidir(lat_t, sc_f, sc_b, True)
    y_lat = readout(yl_f + yl_b, lat_t)
    y_ctx = readout(yc_f + yc_b, ctx_t) if ctx_out else None
    return y_ctx, y_lat


def setup_inputs(seed: int = 0) -> dict:
    key = jax.random.key(seed)
    ks = iter(jax.random.split(key, 64))

    def nrm(shape, scale):
        return scale * jax.random.normal(next(ks), shape, f32)

    def unif(shape, lo, hi):
        return jax.random.uniform(next(ks), shape, f32, lo, hi)

    G, P, H = S5_GROUPS, S5_STATE, GDN_HEADS
    gdn_dt = unif((DEPTH, 2, H), 1e-3, 1e-1)
    return {
        'x': nrm((BATCH, SEQ, D_MODEL), 1.0),
        'c': nrm((BATCH, D_MODEL), 1.0),
        'ctx': nrm((BATCH, CTX_LEN, D_MODEL), 1.0),
        'c_ctx': nrm((D_MODEL,), 1.0),
        'mod_w': nrm((DEPTH, D_MODEL, N_MOD * D_MODEL), 0.5 * D_MODEL ** -0.5),
        'mod_b': nrm((DEPTH, N_MOD * D_MODEL), 0.01),
        'norm_mix': 1.0 + nrm((DEPTH, D_MODEL), 0.02),
        'norm_mlp': 1.0 + nrm((DEPTH, D_MODEL), 0.02),
        'norm_final': 1.0 + nrm((D_MODEL,), 0.02),
        'w_in': nrm((DEPTH, D_MODEL, IN_COLS), D_MODEL ** -0.5),
        'w_out': nrm((DEPTH, D_MIX, D_MODEL), D_MIX ** -0.5),
        's5_b_re': nrm((DEPTH, G, P, S5_GROUP), (2 * S5_GROUP) ** -0.5),
        's5_b_im': nrm((DEPTH, G, P, S5_GROUP), (2 * S5_GROUP) ** -0.5),
        's5_c_re': nrm((DEPTH, G, S5_GROUP, P), P ** -0.5),
        's5_c_im': nrm((DEPTH, G, S5_GROUP, P), P ** -0.5),
        's5_d': nrm((DEPTH, S5_WIDTH), 0.5),
        's5_a_re': -0.5 + nrm((DEPTH, 2, G, P), 0.01),
        's5_a_im': jnp.pi * jnp.arange(P, dtype=f32) + nrm((DEPTH, 2, G, P), 0.01),
        's5_log_dt': unif((DEPTH, 2, G), math.log(1e-3), math.log(1e-1)),
        's5_glu_w': nrm((DEPTH, S5_WIDTH, S5_WIDTH), S5_WIDTH ** -0.5),
        's5_glu_b': nrm((DEPTH, S5_WIDTH), 0.01),
        'gdn_conv': nrm((DEPTH, GDN_CONV, 3 * GDN_WIDTH), GDN_CONV ** -0.5),
        'gdn_a_log': jnp.log(unif((DEPTH, 2, H), 1.0, 16.0)),
        'gdn_dt_bias': gdn_dt + jnp.log(-jnp.expm1(-gdn_dt)),
        'gdn_norm': 1.0 + nrm((DEPTH, GDN_HEAD), 0.02),
        'rwkv_mu': unif((DEPTH, RWKV_COLS), 0.0, 1.0),
        'rwkv_w0': unif((DEPTH, 2, RWKV_WIDTH), -5.0, 1.0),
        'rwkv_w_up': nrm((DEPTH, 2, DECAY_LORA, RWKV_WIDTH), 0.1),
        'rwkv_a0': nrm((DEPTH, 2, RWKV_WIDTH), 0.1),
        'rwkv_a_up': nrm((DEPTH, 2, ICLR_LORA, RWKV_WIDTH), 0.1),
        'rwkv_g_up': nrm((DEPTH, GATE_LORA, RWKV_WIDTH), GATE_LORA ** -0.5),
        'rwkv_k_k': 0.85 + nrm((DEPTH, RWKV_WIDTH), 0.02),
        'rwkv_k_a': 1.0 + nrm((DEPTH, RWKV_WIDTH), 0.02),
        'rwkv_r_k': nrm((DEPTH, RWKV_HEADS, RWKV_HEAD), 0.1),
        'rwkv_ln_w': 1.0 + nrm((DEPTH, RWKV_WIDTH), 0.02),
        'rwkv_ln_b': nrm((DEPTH, RWKV_WIDTH), 0.01),
        'mlp_w1': nrm((DEPTH, D_MODEL, D_FF), D_MODEL ** -0.5),
        'mlp_w2': nrm((DEPTH, D_FF, D_MODEL), D_FF ** -0.5),
    }


def reference(x, c, ctx, c_ctx, mod_w, mod_b, norm_mix, norm_mlp, norm_final, w_in, w_out,
              s5_b_re, s5_b_im, s5_c_re, s5_c_im, s5_d, s5_a_re, s5_a_im, s5_log_dt, s5_glu_w, s5_glu_b,
              gdn_conv, gdn_a_log, gdn_dt_bias, gdn_norm,
              rwkv_mu, rwkv_w0, rwkv_w_up, rwkv_a0, rwkv_a_up, rwkv_g_up, rwkv_k_k, rwkv_k_a, rwkv_r_k,
              rwkv_ln_w, rwkv_ln_b, mlp_w1, mlp_w2):
    o_gdn = S5_COLS
    o_rwkv = S5_COLS + GDN_COLS
    for layer in range(DEPTH):
        ctx_out = layer < DEPTH - 1
        mods = jax.nn.silu(c.astype(f32)) @ mod_w[layer] + mod_b[layer]
        sh1, sc1, gt1, sh2, sc2, gt2 = jnp.split(mods[:, None, :], N_MOD, axis=-1)
        cmods = jnp.split(jax.nn.silu(c_ctx.astype(f32)) @ mod_w[layer] + mod_b[layer], N_MOD)

        p_lat = (rmsnorm(x, norm_mix[layer]) * (1.0 + sc1) + sh1) @ w_in[layer]
        p_ctx = (rmsnorm(ctx, norm_mix[layer]) * (1.0 + cmods[1]) + cmods[0]) @ w_in[layer]
        ya_c, ya_l = s5_mixer(p_ctx[..., :o_gdn], p_lat[..., :o_gdn], s5_b_re[layer], s5_b_im[layer],
                              s5_c_re[layer], s5_c_im[layer], s5_d[layer], s5_a_re[layer], s5_a_im[layer],
                              s5_log_dt[layer], s5_glu_w[layer], s5_glu_b[layer], ctx_out)
        yb_c, yb_l = gdn_mixer(p_ctx[..., o_gdn:o_rwkv], p_lat[..., o_gdn:o_rwkv], gdn_conv[layer],
                               gdn_a_log[layer], gdn_dt_bias[layer], gdn_norm[layer], ctx_out)
        yc_c, yc_l = rwkv_mixer(p_ctx[..., o_rwkv:], p_lat[..., o_rwkv:], rwkv_mu[layer], rwkv_w0[layer],
                                rwkv_w_up[layer], rwkv_a0[layer], rwkv_a_up[layer], rwkv_g_up[layer],
                                rwkv_k_k[layer], rwkv_k_a[layer], rwkv_r_k[layer], rwkv_ln_w[layer],
                                rwkv_ln_b[layer], ctx_out)
        x = x + gt1 * (jnp.concatenate([ya_l, yb_l, yc_l], axis=-1) @ w_out[layer])

        h = rmsnorm(x, norm_mlp[layer]) * (1.0 + sc2) + sh2
        x = x + gt2 * sq_relu_mlp(h, mlp_w1[layer], mlp_w2[layer])

        if ctx_out:
            ctx = ctx + cmods[2] * (jnp.concatenate([ya_c, yb_c, yc_c], axis=-1) @ w_out[layer])
            hc = rmsnorm(ctx, norm_mlp[layer]) * (1.0 + cmods[4]) + cmods[3]
            ctx = ctx + cmods[5] * sq_relu_mlp(hc, mlp_w1[layer], mlp_w2[layer])
    return rmsnorm(x, norm_final)
```

```python
import numpy as np
import concourse.bass as bass
import concourse.mybir as mybir
from concourse.bass_utils import run_bass_kernel_spmd

F32 = mybir.dt.float32
BF16 = mybir.dt.bfloat16
ALU = mybir.AluOpType
AF = mybir.ActivationFunctionType

D = 1024
LC = 256
LL = 4096
T = LC + LL
DEPTH = 2
INC = 3352
EPS = 1e-6
NDMA = 16


class St:
    __slots__ = ("lw", "rd")

    def __init__(self):
        self.lw = None
        self.rd = {}


class Buf:
    def __init__(self, name, handle, dram=False):
        self.name = name
        self.h = handle
        self.dram = dram
        self.reg = {None: St()}

    def _ap(self, idx):
        base = self.h.ap() if self.dram else self.h
        return base[idx]

    def __getitem__(self, idx):
        return View(self, self._ap(idx), None)

    def k(self, key):
        return Keyed(self, key)

    def states(self, key):
        if key is None:
            return list(self.reg.values())
        if key not in self.reg:
            self.reg[key] = St()
        return [self.reg[key], self.reg[None]]


class Keyed:
    def __init__(self, buf, key):
        self.buf = buf
        self.key = key

    def __getitem__(self, idx):
        return View(self.buf, self.buf._ap(idx), self.key)


class View:
    def __init__(self, buf, ap, key):
        self.buf = buf
        self.ap = ap
        self.key = key

    def with_ap(self, ap):
        return View(self.buf, ap, self.key)

    def rearrange(self, pat, **kw):
        return View(self.buf, self.ap.rearrange(pat, **kw), self.key)

    def __getitem__(self, idx):
        return View(self.buf, self.ap[idx], self.key)

    def bcast(self, shape):
        return View(self.buf, self.ap.broadcast_to(shape), self.key)


class Scope:
    def __init__(self, p):
        self.p = p
        self.bufs = []

    def __enter__(self):
        from contextlib import ExitStack
        self.prev_stack = self.p.stack
        self.prev_scope = getattr(self.p, "cur_scope", None)
        self.es = ExitStack()
        self.es.__enter__()
        self.p.stack = self.es
        self.p.cur_scope = self
        return self

    def __exit__(self, *a):
        p = self.p
        freed = dict(getattr(p, "freed", {}))
        for b in self.bufs:
            for st in b.reg.values():
                toks = list(st.rd.items()) + ([st.lw] if st.lw is not None else [])
                for kk, vv in toks:
                    if freed.get(kk, 0) < vv:
                        freed[kk] = vv
        p.freed = freed
        p.stack = self.prev_stack
        p.cur_scope = self.prev_scope
        if self.prev_scope is not None:
            self.prev_scope.bufs.extend(self.bufs)
        self.es.__exit__(None, None, None)
        return False


class Prog:
    ENGS = ("pe", "dve", "act", "pool", "sp")

    def __init__(self, nc, stack):
        self.nc = nc
        self.stack = stack
        self.lists = {e: [] for e in self.ENGS}
        self.cnt = {e: 0 for e in self.ENGS}
        self.seen = {e: {} for e in self.ENGS}
        self.sems = {}
        for e in ("pe", "dve", "act", "pool"):
            self.sems[e] = stack.enter_context(nc.semaphore("s_" + e))
        self.dma_slots = {}
        for q in ("sp", "pool", "act"):
            self.dma_slots[q] = []
            for i in range(NDMA if q == "sp" else 4):
                key = "d_%s_%d" % (q, i)
                self.sems[key] = stack.enter_context(nc.semaphore(key))
                self.dma_slots[q].append([key, 0])
        self.dma_rr = {q: 0 for q in self.dma_slots}
        self.n_instr = 0

    def scope(self):
        return Scope(self)

    def uniq(self, name):
        self.n_names = getattr(self, "n_names", 0) + 1
        return "%s_%d" % (name, self.n_names)

    def sbuf(self, name, shape, dt):
        name = self.uniq(name)
        b = Buf(name, self.stack.enter_context(self.nc.sbuf_tensor(name, shape, dt)))
        b.reg[None].rd = dict(getattr(self, "freed", {}))
        if getattr(self, "cur_scope", None) is not None:
            self.cur_scope.bufs.append(b)
        return b

    def psum(self, name, shape, dt=F32):
        b = Buf(name, self.stack.enter_context(self.nc.psum_tensor(name, shape, dt)))
        b.is_psum = True
        return b

    def dram(self, name, shape, dt, kind="Internal"):
        return Buf(name, self.nc.dram_tensor(name, shape, dt, kind=kind), dram=True)

    def _need(self, eng, reads, writes):
        need = {}

        def add(tok):
            if tok is not None:
                if need.get(tok[0], 0) < tok[1]:
                    need[tok[0]] = tok[1]

        for v in reads:
            for st in v.buf.states(v.key):
                add(st.lw)
        for v in writes:
            for st in v.buf.states(v.key):
                add(st.lw)
                for kk, vv in st.rd.items():
                    add((kk, vv))
        out = []
        for kk, vv in need.items():
            if eng == "pe" and kk == "pe":
                continue
            if self.seen[eng].get(kk, 0) >= vv:
                continue
            self.seen[eng][kk] = vv
            out.append((kk, vv))
        return out

    def _mark(self, tok, reads, writes):
        for v in reads:
            st = v.buf.states(v.key)[0] if v.key is not None else v.buf.reg[None]
            st.rd[tok[0]] = tok[1]
        for v in writes:
            if v.key is None:
                for st in v.buf.reg.values():
                    st.lw = tok
                    st.rd = {}
            else:
                st = v.buf.states(v.key)[0]
                st.lw = tok
                st.rd = {}

    def op(self, eng, fn, reads, writes):
        writes = list(writes) + [v for v in reads if getattr(v.buf, "is_psum", False)]
        waits = self._need(eng, reads, writes)
        self.cnt[eng] += 1
        tok = (eng, self.cnt[eng])
        self.lists[eng].append((waits, fn, (eng, 1)))
        self._mark(tok, reads, writes)
        self.n_instr += 1

    def dma(self, out, in_, q="sp"):
        slots = self.dma_slots[q]
        i = self.dma_rr[q]
        self.dma_rr[q] = (i + 1) % len(slots)
        slot = slots[i]
        waits = self._need(q, [in_], [out])
        if slot[1] > 0 and self.seen[q].get(slot[0], 0) < slot[1]:
            self.seen[q][slot[0]] = slot[1]
            waits.append((slot[0], slot[1]))
        slot[1] += 16
        tok = (slot[0], slot[1])
        o_ap, i_ap = out.ap, in_.ap
        self.lists[q].append((waits, lambda e: e.dma_start(out=o_ap, in_=i_ap), (slot[0], 16)))
        self._mark(tok, [in_], [out])
        self.n_instr += 1

    def wait_all(self, eng, views):
        waits = self._need(eng, views, [])
        self.lists[eng].append((waits, None, None))

    def emit(self):
        nc = self.nc
        sems = self.sems
        with nc.Block() as block:
            def mk(lst):
                def body(e):
                    for waits, fn, inc in lst:
                        for kk, vv in waits:
                            e.wait_ge(sems[kk], vv)
                        if fn is not None:
                            fn(e).then_inc(sems[inc[0]], inc[1])
                return body

            block.tensor(mk(self.lists["pe"]))
            block.vector(mk(self.lists["dve"]))
            block.scalar(mk(self.lists["act"]))
            block.gpsimd(mk(self.lists["pool"]))
            block.sync(mk(self.lists["sp"]))

    def mm(self, out, lhsT, rhs, start=True, stop=True):
        self.op("pe", lambda e: e.matmul(out.ap, lhsT.ap, rhs.ap, start=start, stop=stop), [lhsT, rhs], [out])

    def tr(self, out, in_, ident):
        self.op("pe", lambda e: e.matmul(out.ap, in_.ap, ident.ap, start=True, stop=True), [in_, ident], [out])

    def tt(self, eng, out, a, b, op):
        self.op(eng, lambda e: e.tensor_tensor(out=out.ap, in0=a.ap, in1=b.ap, op=op), [a, b], [out])

    def ts(self, eng, out, a, s1, op0, s2=None, op1=None):
        reads = [a] + [s for s in (s1, s2) if isinstance(s, View)]
        s1a = s1.ap if isinstance(s1, View) else s1
        s2a = s2.ap if isinstance(s2, View) else s2
        if op1 is None:
            self.op(eng, lambda e: e.tensor_scalar(out=out.ap, in0=a.ap, scalar1=s1a, scalar2=None, op0=op0), reads, [out])
        else:
            self.op(eng, lambda e: e.tensor_scalar(out=out.ap, in0=a.ap, scalar1=s1a, scalar2=s2a, op0=op0, op1=op1), reads, [out])

    def stt(self, eng, out, a, s, b, op0, op1):
        reads = [a, b] + ([s] if isinstance(s, View) else [])
        sa = s.ap if isinstance(s, View) else s
        eng = "dve"
        self.op(eng, lambda e: e.scalar_tensor_tensor(out=out.ap, in0=a.ap, scalar=sa, in1=b.ap, op0=op0, op1=op1), reads, [out])

    def act(self, out, a, func, bias=None, scale=None):
        reads = [a] + [s for s in (bias, scale) if isinstance(s, View)]
        kw = {}
        if bias is not None:
            kw["bias"] = bias.ap if isinstance(bias, View) else bias
        if scale is not None:
            kw["scale"] = scale.ap if isinstance(scale, View) else scale
        self.op("act", lambda e: e.activation(out=out.ap, in_=a.ap, func=func, **kw), reads, [out])

    def copy(self, eng, out, a):
        if eng == "act":
            self.op("act", lambda e: e.activation(out=out.ap, in_=a.ap, func=AF.Copy), [a], [out])
        else:
            self.op(eng, lambda e: e.tensor_copy(out=out.ap, in_=a.ap), [a], [out])

    def recip(self, out, a):
        self.op("dve", lambda e: e.reciprocal(out=out.ap, in_=a.ap), [a], [out])

    def memset(self, eng, out, val):
        self.op(eng, lambda e: e.memset(out.ap, val), [], [out])

    def scan(self, out, d0, d1, init, op0=ALU.mult, op1=ALU.add):
        reads = [d0, d1] + ([init] if isinstance(init, View) else [])
        ia = init.ap if isinstance(init, View) else init
        self.op("dve", lambda e: e.tensor_tensor_scan(out=out.ap, data0=d0.ap, data1=d1.ap, initial=ia, op0=op0, op1=op1), reads, [out])


def rev_view(v):
    from concourse.ap import AP
    ap = v.ap
    l = [list(x) for x in ap.ap]
    assert l[-1][0] == 1, l
    off = ap.offset + (l[-1][1] - 1)
    l[-1][0] = -1
    return v.with_ap(AP(ap.tensor, off, l))


def token_blocks(n):
    nc_ = min(n, LC)
    blks = [(i, nc_, True) for i in range(0, LC, nc_)]
    for s in range(LC, T, n):
        blks.append((s, n, False))
    return blks


IN_TILES = [(0, 128), (128, 128)] + [(256 + 128 * i, 128) for i in range(12)] + [(1792, 24)] + \
           [(1816 + 128 * i, 128) for i in range(12)]


def build_program(debug=None):
    from contextlib import ExitStack
    if debug is None:
        debug = {"zero_mix": True, "s5": True, "gdn": True, "rwkv": True}
    nc = bass.Bass("TRN2", target_bir_lowering=False)
    with ExitStack() as stack:
        p = Prog(nc, stack)
        xT = p.dram("xT", [D, T], F32, kind="ExternalInput")
        cT = p.dram("cT", [128, 8, 2], F32, kind="ExternalInput")
        mod_w = p.dram("mod_w", [DEPTH, D, 6 * D], F32, kind="ExternalInput")
        mod_b = p.dram("mod_b", [128, DEPTH, 48], F32, kind="ExternalInput")
        nrm = p.dram("nrm", [128, 5, 8], F32, kind="ExternalInput")
        w_in = p.dram("w_in", [DEPTH, D, INC], F32, kind="ExternalInput")
        w_out = p.dram("w_out", [DEPTH, D, D], F32, kind="ExternalInput")
        w1 = p.dram("mlp_w1", [DEPTH, D, 4 * D], F32, kind="ExternalInput")
        w2 = p.dram("mlp_w2", [DEPTH, 4 * D, D], F32, kind="ExternalInput")
        s5c = p.dram("s5c", [DEPTH, 128, 3, 16], F32, kind="ExternalInput")
        s5B = p.dram("s5B", [DEPTH, 128, 2, 8, 128], F32, kind="ExternalInput")
        s5C = p.dram("s5C", [DEPTH, 128, 2, 8, 128], F32, kind="ExternalInput")
        s5v = p.dram("s5v", [DEPTH, 128, 2, 2], F32, kind="ExternalInput")
        s5g = p.dram("s5g", [DEPTH, 256, 256], F32, kind="ExternalInput")
        cmat_d = p.dram("cmat", [128, 8, 64], F32, kind="ExternalInput")
        gdn_cw = p.dram("gdn_cw", [128, DEPTH, 9, 5], F32, kind="ExternalInput")
        gdn_sc = p.dram("gdn_sc", [128, DEPTH, 2, 12], F32, kind="ExternalInput")
        gdn_nw = p.dram("gdn_nw", [128, DEPTH], F32, kind="ExternalInput")
        pBA = p.dram("pBA", [T, 24], F32)
        rw_mu = p.dram("rw_mu", [128, DEPTH, 12], F32, kind="ExternalInput")
        rw_vec = p.dram("rw_vec", [128, DEPTH, 3, 9], F32, kind="ExternalInput")
        rw_up = p.dram("rw_up", [128, DEPTH, 3, 384], F32, kind="ExternalInput")
        cmask = p.dram("cmask", [128, 6], F32, kind="ExternalInput")
        yrw = p.dram("yrw", [384, T], F32)
        outT = p.dram("outT", [D, LL], F32, kind="ExternalOutput")
        xres = p.dram("xres", [D, T], F32)
        pT = p.dram("pT", [INC, T], F32)
        yT = p.dram("yT", [D, T], BF16)
        dbg = None
        if debug and "shape" in debug:
            dbg = p.dram("dbg", list(debug["shape"]), F32, kind="ExternalOutput")

        ones_bf = p.sbuf("ones_bf", [128, 128], BF16)
        p.memset("dve", ones_bf[:], 1.0)
        modv = p.sbuf("modv", [128, DEPTH, 48, 2], F32)
        modb_sb = p.sbuf("modb_sb", [128, DEPTH, 48], F32)
        nrm_sb = p.sbuf("nrm_sb", [128, 5, 8], F32)
        sc_sb = p.sbuf("sc_sb", [128, 8, 2], F32)
        coef = p.sbuf("coef", [128, DEPTH, 2, 6, 8], F32)
        ps = [p.psum("ps%d" % i, [128, 512]) for i in range(8)]

        p.dma(modb_sb[:], mod_b[:])
        p.dma(nrm_sb[:], nrm[:])
        p.dma(sc_sb[:], cT[:])
        sg = p.sbuf("sg", [128, 8, 2], F32)
        p.act(sg[:], sc_sb[:], AF.Sigmoid)
        p.tt("dve", sc_sb[:], sc_sb[:], sg[:], ALU.mult)

        with p.scope():
            mw = [p.sbuf("mw_a", [128, 8, 512], F32), p.sbuf("mw_b", [128, 8, 512], F32)]
            it = 0
            for l in range(DEPTH):
                src = mod_w.h.ap()[l].rearrange("(k p) n -> p k n", p=128)
                for g in range(12):
                    b = mw[it % 2]
                    it += 1
                    p.dma(b[:], View(mod_w, src[:, :, g * 512:(g + 1) * 512], None))
                    pst = ps[g % 2]
                    for jj in range(4):
                        for k in range(8):
                            p.mm(pst[:, jj * 2:jj * 2 + 2], b[:, k, jj * 128:(jj + 1) * 128], sc_sb[:, k, :],
                                 start=(k == 0), stop=(k == 7))
                    for w in range(2):
                        p.tt("dve", modv[:, l, g * 4:(g + 1) * 4, w],
                             pst[:, 0:8].rearrange("p (j w) -> p j w", w=2)[:, :, w],
                             modb_sb[:, l, g * 4:(g + 1) * 4], ALU.add)
        for l in range(DEPTH):
            for w in range(2):
                mv = lambda j0: modv[:, l, j0:j0 + 8, w]
                p.stt("dve", coef[:, l, w, 0, :], mv(8), 1.0, nrm_sb[:, l, :], ALU.add, ALU.mult)
                p.copy("dve", coef[:, l, w, 1, :], mv(0))
                p.copy("dve", coef[:, l, w, 2, :], mv(16))
                p.stt("dve", coef[:, l, w, 3, :], mv(32), 1.0, nrm_sb[:, 2 + l, :], ALU.add, ALU.mult)
                p.copy("dve", coef[:, l, w, 4, :], mv(24))
                p.copy("dve", coef[:, l, w, 5, :], mv(40))

        for k in range(8):
            p.dma(xres[k * 128:(k + 1) * 128, :], xT[k * 128:(k + 1) * 128, :])

        xres_v = lambda s, n: xres.h.ap().rearrange("(k p) t -> p k t", p=128)[:, :, s:s + n]
        yT_v = lambda s, n: yT.h.ap().rearrange("(k p) t -> p k t", p=128)[:, :, s:s + n]

        def load_cast(dst, src_ap_fn, ncols, src_buf, stg, engs=("dve", "pool")):
            i = 0
            for c0 in range(0, ncols, 512):
                c1 = min(ncols, c0 + 512)
                s = stg[i % 2]
                p.dma(s[:, :, 0:c1 - c0], View(src_buf, src_ap_fn(c0, c1), None))
                p.copy(engs[i % 2], dst[:, :, c0:c1], s[:, :, 0:c1 - c0])
                i += 1

        eps_sb = p.sbuf("eps_sb", [128, 1], F32)
        p.memset("dve", eps_sb[:], EPS)

        def rstd_from(dst, src, scale=1.0 / D, eps=None):
            p.act(dst, src, AF.Sqrt, bias=(eps if eps is not None else eps_sb)[:, 0:1], scale=scale)
            p.recip(dst, dst)

        def norm_block(xb, xn, n, cf_A, cf_sh, sq, rstd, pst, ntmp):
            for k in range(8):
                p.act(sq[:, k, 0:n], xb[:, k, 0:n], AF.Square)
            for k in range(8):
                p.mm(pst[:, 0:n], ones_bf[:], sq[:, k, 0:n], start=(k == 0), stop=(k == 7))
            rstd_from(rstd[:, 0:n], pst[:, 0:n])
            for k in range(8):
                eng = "dve" if k % 2 == 0 else "pool"
                tmp = ntmp[k % 2]
                p.tt(eng, tmp[:, 0:n], xb[:, k, 0:n], rstd[:, 0:n], ALU.mult)
                p.act(xn[:, k, 0:n], tmp[:, 0:n], AF.Identity, bias=cf_sh[:, k:k + 1], scale=cf_A[:, k:k + 1])

        PI = float(np.pi)
        cst = p.sbuf("cst", [128, 4], F32)
        p.memset("dve", cst[:, 0:1], -3.1415915)
        p.memset("dve", cst[:, 1:2], 1.0)

        def sincos(dst_c, dst_s, th, shp, tmpf):
            t_y, t_k, t_m, t_f, t_i = tmpf
            for dst, shift in ((dst_s, 0.5), (dst_c, 0.75)):
                p.ts("dve", t_y, th, 1.0 / (2 * PI), ALU.mult, shift, ALU.add)
                p.copy("dve", t_i, t_y)
                p.copy("dve", t_k, t_i)
                p.tt("dve", t_m, t_k, t_y, ALU.is_gt)
                p.tt("dve", t_k, t_k, t_m, ALU.subtract)
                p.tt("dve", t_f, t_y, t_k, ALU.subtract)
                p.ts("dve", t_f, t_f, 2 * PI, ALU.mult, -PI, ALU.add)
                p.tt("dve", t_m, t_f, t_f, ALU.mult)
                coefs = [(-1.0) ** k / float(np.prod(np.arange(1, 2 * k + 2, dtype=np.float64))) for k in range(10)]
                p.memset("dve", t_k, coefs[9])
                for k in range(8, -1, -1):
                    p.tt("dve", t_k, t_k, t_m, ALU.mult)
                    p.ts("dve", t_k, t_k, coefs[k], ALU.add)
                p.tt("dve", dst, t_k, t_f, ALU.mult)

        S5TC = 256

        def s5_mixer(l):
            TC = S5TC
            NCH = T // TC
            with p.scope():
                I32 = mybir.dt.int32
                c_sb = p.sbuf("s5c_sb", [128, 3, 16], F32)
                p.dma(c_sb[:], s5c[l])
                sm = {nm: p.sbuf("s5_" + nm, [128, 16], F32) for nm in
                      ("dt", "th", "r", "c1", "s1", "x", "y", "den", "cre", "cim", "t0", "t1", "ty", "tk", "tm", "tf", "cT", "sT")}
                ti32 = p.sbuf("s5_ti", [128, 16], I32)
                V = lambda nm: sm[nm][:, :]
                def exp_acc(dst, src):
                    tq, e_ = V("ty"), V("tk")
                    p.ts("dve", tq, src, 1.0 / 16.0, ALU.mult)
                    p.memset("dve", e_, 1.0)
                    for k in range(10, 0, -1):
                        p.tt("dve", e_, e_, tq, ALU.mult)
                        p.ts("dve", e_, e_, 1.0 / k, ALU.mult, 1.0, ALU.add)
                    for _ in range(4):
                        p.tt("dve", e_, e_, e_, ALU.mult)
                    p.copy("dve", dst, e_)
                exp_acc(V("dt"), c_sb[:, 2, :])
                p.tt("dve", V("th"), V("dt"), c_sb[:, 1, :], ALU.mult)
                p.tt("dve", V("t0"), V("dt"), c_sb[:, 0, :], ALU.mult)
                exp_acc(V("r"), V("t0"))
                sincos(V("c1"), V("s1"), V("th"), None, (V("ty"), V("tk"), V("tm"), V("tf"), ti32[:, :]))
                p.tt("dve", V("x"), V("r"), V("c1"), ALU.mult)
                p.ts("dve", V("x"), V("x"), -1.0, ALU.add)
                p.tt("dve", V("y"), V("r"), V("s1"), ALU.mult)
                p.tt("dve", V("den"), c_sb[:, 0, :], c_sb[:, 0, :], ALU.mult)
                p.tt("dve", V("t0"), c_sb[:, 1, :], c_sb[:, 1, :], ALU.mult)
                p.tt("dve", V("den"), V("den"), V("t0"), ALU.add)
                p.recip(V("den"), V("den"))
                p.tt("dve", V("t0"), V("x"), c_sb[:, 0, :], ALU.mult)
                p.tt("dve", V("t1"), V("y"), c_sb[:, 1, :], ALU.mult)
                p.tt("dve", V("t0"), V("t0"), V("t1"), ALU.add)
                p.tt("dve", V("cre"), V("t0"), V("den"), ALU.mult)
                p.tt("dve", V("t0"), V("y"), c_sb[:, 0, :], ALU.mult)
                p.tt("dve", V("t1"), V("x"), c_sb[:, 1, :], ALU.mult)
                p.tt("dve", V("t0"), V("t0"), V("t1"), ALU.subtract)
                p.tt("dve", V("cim"), V("t0"), V("den"), ALU.mult)

                B_bf = p.sbuf("s5B_bf", [128, 2, 8, 128], BF16)
                C_bf = p.sbuf("s5C_bf", [128, 2, 8, 128], BF16)
                G_bf = p.sbuf("s5G_bf", [128, 2, 256], BF16)
                v_sb = p.sbuf("s5v_sb", [128, 2, 2], F32)
                p.dma(v_sb[:], s5v[l])
                with p.scope():
                    stg = p.sbuf("s5stg", [128, 2, 8, 128], F32)
                    p.dma(stg[:], s5B[l])
                    p.copy("dve", B_bf[:], stg[:])
                    stg2 = p.sbuf("s5stg2", [128, 2, 8, 128], F32)
                    p.dma(stg2[:], s5C[l])
                    p.copy("pool", C_bf[:], stg2[:])
                    stg3 = p.sbuf("s5stg3", [128, 2, 256], F32)
                    p.dma(stg3[:], View(s5g, s5g.h.ap()[l].rearrange("(k p) n -> p k n", p=128), None))
                    p.copy("dve", G_bf[:], stg3[:])

                u_b = p.sbuf("s5u_b", [128, 2, T], BF16)
                yacc = p.sbuf("s5yacc", [128, 2, T], F32)
                ustg = [p.sbuf("s5ustg%d" % q, [128, 1088], F32) for q in range(2)]
                uq = 0
                for i in range(2):
                    for c in range(0, T, 1088):
                        st_ = ustg[uq % 2]
                        uq += 1
                        p.dma(st_[:, :], pT[i * 128:(i + 1) * 128, c:c + 1088])
                        p.copy("pool" if i else "act", u_b.k((i, c))[:, i, c:c + 1088], st_[:, :])

                tab = {nm: p.sbuf("s5tab_" + nm, [128, 8, TC], F32) for nm in ("er", "ei", "mr", "mi", "rt")}
                wk = {nm: [p.sbuf("s5w_%s%d" % (nm, q), [128, TC], F32) for q in range(2)] for nm in
                      ("br", "bi", "a", "b", "c", "d", "dr", "di", "gr", "gi")}
                hb = {nm: p.sbuf("s5h_" + nm, [128, 8, TC], BF16) for nm in ("re", "im")}
                tA_buf = p.sbuf("s5tA", [128, 8, TC], F32)
                carry = p.sbuf("s5carry", [128, 8, 2], F32)
                sc8 = {nm: p.sbuf("s5s8_" + nm, [128, 8], F32) for nm in ("mc", "ms", "t0", "t1", "t2")}
                ctmp = p.sbuf("s5ctmp", [128, 4], F32)

                for d in range(2):
                    ds = slice(d * 8, d * 8 + 8)
                    p.memset("dve", tab["er"][:, :, 0:1], 1.0)
                    p.memset("dve", tab["ei"][:, :, 0:1], 0.0)
                    p.copy("dve", sc8["mc"][:, :], sm["c1"][:, ds])
                    p.copy("dve", sc8["ms"][:, :], sm["s1"][:, ds])
                    m = 1
                    while m < TC:
                        bc = lambda v: v[:, :].rearrange("p (j o) -> p j o", o=1).bcast([128, 8, m])
                        er0, ei0 = tab["er"][:, :, 0:m], tab["ei"][:, :, 0:m]
                        er1, ei1 = tab["er"][:, :, m:2 * m], tab["ei"][:, :, m:2 * m]
                        t_a, t_b = tab["mr"][:, :, 0:m], tab["mi"][:, :, 0:m]
                        p.tt("dve", t_a, er0, bc(sc8["mc"]), ALU.mult)
                        p.tt("pool", t_b, ei0, bc(sc8["ms"]), ALU.mult)
                        p.tt("dve", er1, t_a, t_b, ALU.subtract)
                        p.tt("dve", t_a, er0, bc(sc8["ms"]), ALU.mult)
                        p.tt("pool", t_b, ei0, bc(sc8["mc"]), ALU.mult)
                        p.tt("dve", ei1, t_a, t_b, ALU.add)
                        p.tt("dve", sc8["t0"][:, :], sc8["mc"][:, :], sc8["mc"][:, :], ALU.mult)
                        p.tt("dve", sc8["t1"][:, :], sc8["ms"][:, :], sc8["ms"][:, :], ALU.mult)
                        p.tt("dve", sc8["t2"][:, :], sc8["mc"][:, :], sc8["ms"][:, :], ALU.mult)
                        p.tt("dve", sc8["mc"][:, :], sc8["t0"][:, :], sc8["t1"][:, :], ALU.subtract)
                        p.ts("dve", sc8["ms"][:, :], sc8["t2"][:, :], 2.0, ALU.mult)
                        m *= 2
                    bcT = lambda v: v.rearrange("p (j o) -> p j o", o=1).bcast([128, 8, TC])
                    cre_b, cim_b = bcT(sm["cre"][:, ds]), bcT(sm["cim"][:, ds])
                    t_a = tA_buf
                    p.tt("dve", tab["mr"][:], tab["er"][:], cre_b, ALU.mult)
                    p.tt("pool", t_a[:], tab["ei"][:], cim_b, ALU.mult)
                    p.tt("dve", tab["mr"][:], tab["mr"][:], t_a[:], ALU.add)
                    p.tt("dve", tab["mi"][:], tab["er"][:], cim_b, ALU.mult)
                    p.tt("pool", t_a[:], tab["ei"][:], cre_b, ALU.mult)
                    p.tt("dve", tab["mi"][:], tab["mi"][:], t_a[:], ALU.subtract)
                    p.memset("pool", tab["rt"][:], 1.0)
                    p.tt("pool", tab["rt"][:], tab["rt"][:], bcT(sm["r"][:, ds]), ALU.mult)
                    p.memset("dve", carry[:], 0.0)

                    def ord_(v):
                        return v if d == 0 else rev_view(v)

                    chunks = list(range(NCH)) if d == 0 else [0] + list(range(NCH - 1, 0, -1))
                    for ci, ch in enumerate(chunks):
                        t0_ = ch * TC
                        for j in range(8):
                            q = j % 2
                            pr, pi_ = ps[(2 * j) % 6], ps[(2 * j + 1) % 6]
                            p.mm(pr[:, 0:TC], B_bf[:, 0, j, :], u_b[:, j // 4, t0_:t0_ + TC])
                            p.mm(pi_[:, 0:TC], B_bf[:, 1, j, :], u_b[:, j // 4, t0_:t0_ + TC])
                            br, bi = wk["br"][q][:, :], wk["bi"][q][:, :]
                            p.copy("act", br, pr[:, 0:TC])
                            p.copy("act", bi, pi_[:, 0:TC])
                            mr, mi = ord_(tab["mr"][:, j, :]), ord_(tab["mi"][:, j, :])
                            er, ei = ord_(tab["er"][:, j, :]), ord_(tab["ei"][:, j, :])
                            a_, b_, c_, d_ = (wk[nm][q][:, :] for nm in ("a", "b", "c", "d"))
                            dr, di = wk["dr"][q][:, :], wk["di"][q][:, :]
                            gr, gi = wk["gr"][q][:, :], wk["gi"][q][:, :]
                            p.tt("dve", a_, br, mr, ALU.mult)
                            p.tt("pool", b_, bi, mi, ALU.mult)
                            p.tt("pool", c_, br, mi, ALU.mult)
                            p.tt("dve", d_, bi, mr, ALU.mult)
                            p.tt("dve", dr, a_, b_, ALU.subtract)
                            p.tt("pool", di, c_, d_, ALU.add)
                            p.scan(ord_(gr), tab["rt"][:, j, :], ord_(dr), carry[:, j, 0:1])
                            p.scan(ord_(gi), tab["rt"][:, j, :], ord_(di), carry[:, j, 1:2])
                            lr = gr[:, TC - 1:TC] if d == 0 else gr[:, 0:1]
                            li = gi[:, TC - 1:TC] if d == 0 else gi[:, 0:1]
                            mc, ms_ = sc8["mc"][:, j:j + 1], sc8["ms"][:, j:j + 1]
                            p.tt("dve", ctmp[:, 0:1], lr, mc, ALU.mult)
                            p.tt("dve", ctmp[:, 1:2], li, ms_, ALU.mult)
                            p.tt("dve", ctmp[:, 2:3], lr, ms_, ALU.mult)
                            p.tt("dve", ctmp[:, 3:4], li, mc, ALU.mult)
                            p.tt("dve", carry[:, j, 0:1], ctmp[:, 0:1], ctmp[:, 1:2], ALU.subtract)
                            p.tt("dve", carry[:, j, 1:2], ctmp[:, 2:3], ctmp[:, 3:4], ALU.add)
                            p.tt("dve", a_, gr, er, ALU.mult)
                            p.tt("pool", b_, gi, ei, ALU.mult)
                            p.tt("pool", c_, gr, ei, ALU.mult)
                            p.tt("dve", d_, gi, er, ALU.mult)
                            p.tt("dve", hb["re"].k(j)[:, j, :], a_, b_, ALU.subtract)
                            p.stt("dve", hb["im"].k(j)[:, j, :], c_, -1.0, d_, ALU.mult, ALU.subtract)
                        for i in range(2):
                            py = ps[6 + i]
                            for jj in range(4):
                                j = 4 * i + jj
                                p.mm(py[:, 0:TC], C_bf[:, 0, j, :], hb["re"].k(j)[:, j, :], start=(jj == 0), stop=False)
                                p.mm(py[:, 0:TC], C_bf[:, 1, j, :], hb["im"].k(j)[:, j, :], start=False, stop=(jj == 3))
                            ya = yacc.k((i, ch))[:, i, t0_:t0_ + TC]
                            if d == 0:
                                p.copy("act", ya, py[:, 0:TC])
                            else:
                                p.tt("dve", ya, ya, py[:, 0:TC], ALU.add)
                NB = 512
                zb = [p.sbuf("s5z%d" % q, [128, 2, NB], BF16) for q in range(2)]
                zf = [p.sbuf("s5zf%d" % q, [128, 2, NB], F32) for q in range(2)]
                t1 = p.sbuf("s5t1", [128, NB], F32)
                t2 = p.sbuf("s5t2", [128, NB], F32)
                ob = [p.sbuf("s5ob%d" % q, [128, 2, NB], BF16) for q in range(2)]
                ufs = [p.sbuf("s5uf%d" % q, [128, 2, NB], F32) for q in range(2)]
                for bi_, (s_, n, isctx) in enumerate(token_blocks(NB)):
                    q = bi_ % 2
                    u_f = ufs[q]
                    p.dma(u_f[:, :, 0:n], View(pT, pT.h.ap()[0:256, :].rearrange("(k p) t -> p k t", p=128)[:, :, s_:s_ + n], None))
                    for i in range(2):
                        yv = t1[:, 0:n]
                        p.stt("dve", yv, u_f[:, i, 0:n], v_sb[:, i, 0:1], yacc[:, i, s_:s_ + n], ALU.mult, ALU.add)
                        p.tt("pool", t2[:, 0:n], yv, yv, ALU.mult)
                        p.ts("dve", t2[:, 0:n], t2[:, 0:n], 0.044715, ALU.mult, 1.0, ALU.add)
                        p.tt("pool", t2[:, 0:n], t2[:, 0:n], yv, ALU.mult)
                        p.act(t2[:, 0:n], t2[:, 0:n], AF.Sigmoid, scale=1.5957691216)
                        p.tt("dve", zf[q][:, i, 0:n], yv, t2[:, 0:n], ALU.mult)
                        p.copy("pool", zb[q][:, i, 0:n], zf[q][:, i, 0:n])
                    for i in range(2):
                        pg = ps[i]
                        for k in range(2):
                            p.mm(pg[:, 0:n], G_bf[:, k, i * 128:(i + 1) * 128], zb[q][:, k, 0:n], start=(k == 0), stop=(k == 1))
                        p.act(t2[:, 0:n], pg[:, 0:n], AF.Sigmoid, bias=v_sb[:, i, 1:2])
                        p.tt("dve", ob[q][:, i, 0:n], zf[q][:, i, 0:n], t2[:, 0:n], ALU.mult)
                    p.dma(View(yT, yT.h.ap()[0:256, :].rearrange("(k p) t -> p k t", p=128)[:, :, s_:s_ + n], ("s5", s_)), ob[q][:, :, 0:n])

        cm = p.sbuf("cmat_sb", [128, 8, 64], F32)
        p.dma(cm[:], cmat_d[:])
        one_c = p.sbuf("one_c", [128, 1], F32)
        p.memset("dve", one_c[:], 1.0)
        CI, CL, CLS, CU, CUS, CONES, CSEL63, CSEL0 = range(8)
        NCK = T // 64
        BWD_ORDER = [3, 2, 1, 0] + list(range(NCK - 1, 3, -1))

        class PsumSlots:
            def __init__(self, banks, half):
                self.banks = banks
                self.half = half
                self.bi = -1
                self.j = 0

            def group(self):
                self.bi = (self.bi + 1) % len(self.banks)
                self.j = 0

            def get(self, part0=None):
                b = self.banks[self.bi]
                j = self.j
                self.j += 1
                assert j < 8
                h = self.half
                return ps[b].k(("h", h))[h * 64:(h + 1) * 64, j * 64:(j + 1) * 64]

        def neumann(hs, M, Y0ps, W, pslots, part0):
            Ih = cm[hs, CI, :]
            X = [W("X0"), W("X1")]
            Y = [W("Y0"), W("Y1")]
            TT = [W("T0"), W("T1")]
            p.copy("dve", Y[0], Y0ps)
            p.tt("dve", TT[0], Ih, Y0ps, ALU.subtract)
            Xc, Yc, Tc = M, Y[0], TT[0]
            for k in range(1, 6):
                pslots.group()
                px = pslots.get(part0)
                p.mm(px, Yc, Xc)
                if k < 5:
                    py = pslots.get(part0)
                    p.mm(py, Xc, Yc)
                Xn = X[k % 2]
                p.copy("dve", Xn, px)
                if k < 5:
                    Yn = Y[k % 2]
                    p.copy("dve", Yn, py)
                pslots.group()
                pz = pslots.get(part0)
                p.mm(pz, Xn, Tc)
                Tn = TT[k % 2]
                p.tt("dve", Tn, Tc, pz, ALU.add)
                Xc, Tc = Xn, Tn
                if k < 5:
                    Yc = Yn
            return Tc

        def gdn_mixer(l):
            with p.scope():
                ba = p.sbuf("g_ba", [128, NCK, 24], F32)
                for hf in range(2):
                    for n0 in range(0, NCK, 17):
                        p.dma(ba.k((hf, n0))[hf * 64:(hf + 1) * 64, n0:n0 + 17, :],
                              View(pBA, pBA.h.ap().rearrange("(n t) r -> t n r", t=64)[:, n0:n0 + 17, :], None))
                sc = p.sbuf("g_sc", [128, 2, 12], F32)
                p.dma(sc[:], gdn_sc[:, l])
                nA = p.sbuf("g_nA", [128, 12], F32)
                p.act(nA[:], sc[:, 0, :], AF.Exp)
                p.ts("dve", nA[:], nA[:], -1.0, ALU.mult)
                if debug.get("gdn_stop") == 0:
                    return
                names = ("beta", "g", "gc", "gl", "egc", "egl", "ed", "nbe")
                tk = {nm: p.sbuf("g_" + nm, [128, NCK, 12], F32) for nm in names}
                bcn = lambda v: v.rearrange("p (o r) -> p o r", o=1).bcast([128, NCK, 12])
                p.act(tk["beta"][:], ba[:, :, 0:12], AF.Sigmoid)
                p.tt("dve", tk["g"][:], ba[:, :, 12:24], bcn(sc[:, 1, :]), ALU.add)
                p.act(tk["g"][:], tk["g"][:], AF.Exp)
                p.act(tk["g"][:], tk["g"][:], AF.Ln, bias=one_c[:, 0:1])
                p.tt("dve", tk["g"][:], tk["g"][:], bcn(nA[:, :]), ALU.mult)
                if debug.get("gdn_stop") == 5:
                    return
                for d in range(2):
                    for hf in range(2):
                        hsl = slice(hf * 64, hf * 64 + 64)
                        tri = cm[hsl, CU if d == 0 else CL, :]
                        sel = cm[hsl, CSEL63 if d == 0 else CSEL0, :]
                        for n0 in range(0, NCK, 34):
                            r3 = lambda v: v.rearrange("p (n r) -> p n r", r=6)
                            pg = ps[0]
                            p.mm(r3(pg[hsl, 0:204]), tri, tk["g"][hsl, n0:n0 + 34, d * 6:(d + 1) * 6])
                            p.copy("dve", tk["gc"][hsl, n0:n0 + 34, d * 6:(d + 1) * 6], r3(pg[hsl, 0:204]))
                            pl = ps[1]
                            p.mm(r3(pl[hsl, 0:204]), sel, tk["gc"][hsl, n0:n0 + 34, d * 6:(d + 1) * 6])
                            p.copy("dve", tk["gl"][hsl, n0:n0 + 34, d * 6:(d + 1) * 6], r3(pl[hsl, 0:204]))
                if debug.get("gdn_stop") == 6:
                    return
                p.ts("dve", tk["egc"][:], tk["gc"][:], -100.0, ALU.max)
                p.act(tk["egc"][:], tk["egc"][:], AF.Exp)
                if debug.get("gdn_stop") == 7:
                    if debug.get("gdn_dump"):
                        p.dma(dbg[:, :], tk[debug["gdn_dump"]][:, :, :].rearrange("p n r -> p (n r)"))
                    return
                p.ts("dve", tk["egl"][:], tk["gl"][:], -100.0, ALU.max)
                p.act(tk["egl"][:], tk["egl"][:], AF.Exp)
                if debug.get("gdn_stop") == 8:
                    return
                p.tt("dve", tk["ed"][:], tk["gl"][:], tk["gc"][:], ALU.subtract)
                p.ts("dve", tk["ed"][:], tk["ed"][:], -100.0, ALU.max)
                p.act(tk["ed"][:], tk["ed"][:], AF.Exp)
                p.tt("dve", tk["nbe"][:], tk["beta"][:], tk["egc"][:], ALU.mult)
                p.ts("dve", tk["nbe"][:], tk["nbe"][:], -1.0, ALU.mult)

                if debug.get("gdn_stop") == 1:
                    return
                cw = p.sbuf("g_cw", [128, 9, 5], F32)
                p.dma(cw[:], gdn_cw[:, l])
                nw = p.sbuf("g_nw", [128, DEPTH], F32)
                p.dma(nw[:], gdn_nw[:])
                ones128 = p.sbuf("g_ones", [128, 128], F32)
                p.memset("dve", ones128[:], 0.0)
                p.memset("dve", ones128[0:64, 0:64], 1.0)
                p.memset("dve", ones128[64:128, 64:128], 1.0)
                eps_g = p.sbuf("g_eps", [128, 1], F32)
                p.memset("dve", eps_g[:], EPS)

                raw = p.sbuf("g_raw", [128, T], F32)
                qkv = [p.sbuf("g_%s" % nm, [128, T], F32) for nm in ("q", "k", "v")]
                oacc = p.sbuf("g_oacc", [128, T], F32)
                tmpn = [p.sbuf("g_tmpn%d" % i, [128, 512], F32) for i in range(2)]
                obs = [p.sbuf("g_ob%d" % i, [128, 512], BF16) for i in range(2)]
                NU = 4
                wk = {nm: [p.sbuf("g_w%s%d" % (nm, u), [128, 64], F32) for u in range(NU)] for nm in
                      ("dg", "Dx", "Di", "Ds", "M", "X0", "X1", "Y0", "Y1", "T0", "T1", "At", "AtT", "bV", "RHS", "vn", "Qg",
                       "Kd", "Kt", "Qt", "S", "dg2")}

                for nm_ in wk:
                    for u in range(NU):
                        p.memset("pool", wk[nm_][u][:, :], 0.0)
                for hp in range(3):
                    for wi in range(3):
                        row0 = 256 + wi * 384 + hp * 128
                        tile_i = wi * 3 + hp
                        dst = qkv[wi]
                        for c in range(0, T, 1088):
                            p.dma(raw.k(c)[:, c:c + 1088], pT[row0:row0 + 128, c:c + 1088])
                        for (a_, b_) in ((0, LC), (LC, T)):
                            p.ts("dve", dst[:, a_:b_], raw[:, a_:b_], cw[:, tile_i, 2:3], ALU.mult)
                            for j in (0, 1, 3, 4):
                                sft = j - 2
                                lo, hi = max(a_, a_ - sft), min(b_, b_ - sft)
                                p.stt("dve", dst[:, lo:hi], raw[:, lo + sft:hi + sft], cw[:, tile_i, j:j + 1], dst[:, lo:hi], ALU.mult, ALU.add)
                        p.act(dst[:, :], dst[:, :], AF.Silu)
                        if wi < 2:
                            for bi_, (s_, n, isctx) in enumerate(token_blocks(512)):
                                tq_ = tmpn[bi_ % 2]
                                p.tt("pool", tq_[:, 0:n], dst[:, s_:s_ + n], dst[:, s_:s_ + n], ALU.mult)
                                pss = ps[2 + bi_ % 2]
                                p.mm(pss[:, 0:n], ones128[:, :], tq_[:, 0:n])
                                p.act(tq_[:, 0:n], pss[:, 0:n], AF.Sqrt, bias=eps_g[:, 0:1])
                                p.recip(tq_[:, 0:n], tq_[:, 0:n])
                                if wi == 0:
                                    p.stt("dve", dst[:, s_:s_ + n], dst[:, s_:s_ + n], 0.125, tq_[:, 0:n], ALU.mult, ALU.mult)
                                else:
                                    p.tt("dve", dst[:, s_:s_ + n], dst[:, s_:s_ + n], tq_[:, 0:n], ALU.mult)
                    if debug.get("gdn_stop") == 2:
                        return
                    qt, kt, vt = qkv
                    p.memset("pool", oacc[:], 0.0)
                    for u in range(NU):
                        p.memset("dve", wk["S"][u][:, :], 0.0)
                    pslots_u = {hh * 2 + d: PsumSlots([0, 1, 2, 3] if d == 0 else [4, 5, 6, 7], hh) for hh in range(2) for d in range(2)}
                    for step in range(debug.get("gdn_steps", NCK)):
                        units = []
                        for hh in range(2):
                            for d in range(2):
                                n = step if d == 0 else BWD_ORDER[step]
                                hs, cs = slice(hh * 64, hh * 64 + 64), slice(n * 64, n * 64 + 64)
                                ci = d * 6 + hp * 2 + hh
                                c = {"ps": pslots_u[hh * 2 + d], "u": hh * 2 + d, "hh": hh, "d": d, "n": n, "hs": hs, "cs": cs, "p0": hh * 64,
                                     "col": (lambda nm, hs=hs, n=n, ci=ci: tk[nm][hs, n, ci:ci + 1]),
                                     "W": (lambda nm, u=hh * 2 + d, hs=hs: wk[nm][u][hs, :])}
                                units.append(c)
                        for c in units:
                            hs, cs, W, col, p0 = c["hs"], c["cs"], c["W"], c["col"], c["p0"]
                            Ih = cm[hs, CI, :]
                            pslots = c["ps"]
                            pslots.group()
                            c["pK"], c["pQ"], c["pV"] = pslots.get(p0), pslots.get(p0), pslots.get(p0)
                            p.tr(c["pK"], kt[hs, cs], Ih)
                            p.tr(c["pQ"], qt[hs, cs], Ih)
                            p.tr(c["pV"], vt[hs, cs], Ih)
                            c["pKK"], c["pQK"] = pslots.get(p0), pslots.get(p0)
                            p.mm(c["pKK"], kt[hs, cs], kt[hs, cs])
                            p.mm(c["pQK"], qt[hs, cs], kt[hs, cs])
                            p.ts("dve", W("dg"), Ih, col("gc"), ALU.mult)
                            c["pG"] = pslots.get(p0)
                            p.mm(c["pG"], cm[hs, CONES, :], W("dg"))
                        if debug.get("gdn_stage", 9) <= 1:
                            continue
                        for c in units:
                            hs, cs, W, col, p0, d = c["hs"], c["cs"], c["W"], c["col"], c["p0"], c["d"]
                            mi, ms_ = (CL, CLS) if d == 0 else (CU, CUS)
                            s2 = debug.get("gdn_s2", 99)
                            e12 = "dve"
                            p.copy(e12, W("Kt"), c["pK"])
                            p.copy(e12, W("Qt"), c["pQ"])
                            if s2 >= 3:
                                p.ts("dve", W("bV"), c["pV"], col("beta"), ALU.mult)
                            if s2 >= 4:
                                p.ts("dve", W("Dx"), c["pG"], col("gc"), ALU.subtract, 0.0, ALU.max)
                            if s2 >= 5:
                                p.act(W("Dx"), W("Dx"), AF.Exp, scale=-1.0)
                            if s2 >= 6:
                                p.tt("pool", W("Di"), W("Dx"), cm[hs, mi, :], ALU.mult)
                                p.tt("pool", W("Ds"), W("Dx"), cm[hs, ms_, :], ALU.mult)
                            if s2 >= 8:
                                p.stt("dve", W("M"), c["pKK"], col("beta"), W("Ds"), ALU.mult, ALU.mult)
                            if s2 >= 9:
                                p.tt("dve", W("At"), c["pQK"], W("Di"), ALU.mult)
                            if s2 >= 10:
                                p.ts("dve", W("dg2"), cm[hs, CI, :], col("egc"), ALU.mult)
                            if s2 >= 11:
                                p.ts("pool", W("Kd"), W("Kt"), col("ed"), ALU.mult)
                        if debug.get("gdn_stage", 9) <= 2:
                            continue
                        for c in units:
                            hs, cs, W, col, p0 = c["hs"], c["cs"], c["W"], c["col"], c["p0"]
                            Ih = cm[hs, CI, :]
                            pslots = c["ps"]
                            pslots.group()
                            c["pY0"], c["pAT"], c["pKS"], c["pQg"] = pslots.get(p0), pslots.get(p0), pslots.get(p0), pslots.get(p0)
                            p.tr(c["pY0"], W("M"), Ih)
                            p.tr(c["pAT"], W("At"), Ih)
                            p.mm(c["pKS"], kt[hs, cs], W("S"))
                            p.mm(c["pQg"], W("Qt"), W("dg2"))
                        if debug.get("gdn_stage", 9) <= 3:
                            continue
                        for c in units:
                            W, col = c["W"], c["col"]
                            p.copy("dve", W("AtT"), c["pAT"])
                            p.stt("dve", W("RHS"), c["pKS"], col("nbe"), W("bV"), ALU.mult, ALU.add)
                            p.copy("dve", W("Qg"), c["pQg"])
                        if debug.get("gdn_stage", 9) <= 4.5:
                            continue
                        for c in units:
                            c["TT"] = neumann(c["hs"], c["W"]("M"), c["pY0"], c["W"], c["ps"], c["p0"])
                        if debug.get("gdn_stage", 9) <= 4:
                            continue
                        for c in units:
                            W = c["W"]
                            pslots = c["ps"]
                            pslots.group()
                            c["pvn"] = pslots.get(c["p0"])
                            p.mm(c["pvn"], c["TT"], W("RHS"))
                            p.copy("dve", W("vn"), c["pvn"])
                        for c in units:
                            hs, cs, W, col, p0, hh, n = c["hs"], c["cs"], c["W"], c["col"], c["p0"], c["hh"], c["n"]
                            pslots = c["ps"]
                            pslots.group()
                            po = pslots.get(p0)
                            p.mm(po, W("S"), W("Qg"), start=True, stop=False)
                            p.mm(po, W("vn"), W("AtT"), start=False, stop=True)
                            p.tt("dve", oacc.k((hh, n))[hs, cs], oacc.k((hh, n))[hs, cs], po, ALU.add)
                            pS = pslots.get(p0)
                            p.mm(pS, W("Kd"), W("vn"))
                            p.stt("dve", W("S"), W("S"), col("egl"), pS, ALU.mult, ALU.add)
                    zt = raw
                    rowz = 256 + 3 * 384 + hp * 128
                    for c_ in range(0, T, 1088):
                        p.dma(zt.k(c_)[:, c_:c_ + 1088], pT[rowz:rowz + 128, c_:c_ + 1088])
                    for bi_, (s_, n, isctx) in enumerate(token_blocks(512)):
                        tq_ = tmpn[bi_ % 2]
                        p.tt("pool", tq_[:, 0:n], oacc[:, s_:s_ + n], oacc[:, s_:s_ + n], ALU.mult)
                        pss = ps[2 + bi_ % 2]
                        p.mm(pss[:, 0:n], ones128[:, :], tq_[:, 0:n])
                        p.act(tq_[:, 0:n], pss[:, 0:n], AF.Sqrt, bias=eps_g[:, 0:1], scale=1.0 / 64.0)
                        p.recip(tq_[:, 0:n], tq_[:, 0:n])
                        p.stt("dve", tq_[:, 0:n], oacc[:, s_:s_ + n], nw[:, l:l + 1], tq_[:, 0:n], ALU.mult, ALU.mult)
                        p.act(zt[:, s_:s_ + n], zt[:, s_:s_ + n], AF.Silu)
                        p.tt("dve", obs[bi_ % 2][:, 0:n], tq_[:, 0:n], zt[:, s_:s_ + n], ALU.mult)
                        p.dma(yT.k(("g", hp, s_))[256 + hp * 128:256 + (hp + 1) * 128, s_:s_ + n], obs[bi_ % 2][:, 0:n])

        def rwkv_mixer(l):
            O_RW = 1816
            SB = 256
            NSB = T // SB
            CPS = SB // 64
            with p.scope():
                mu = p.sbuf("r_mu", [128, 12], F32)
                p.dma(mu[:], rw_mu[:, l])
                vec = p.sbuf("r_vec", [128, 3, 9], F32)
                p.dma(vec[:], rw_vec[:, l])
                ups = p.sbuf("r_ups", [128, 3, 384], F32)
                p.dma(ups[:], rw_up[:, l])
                cmk = p.sbuf("r_cmk", [128, 6], F32)
                p.dma(cmk[:], cmask[:])
                ones128 = p.sbuf("r_ones", [128, 128], F32)
                p.memset("dve", ones128[:], 0.0)
                p.memset("dve", ones128[0:64, 0:64], 1.0)
                p.memset("dve", ones128[64:128, 64:128], 1.0)
                epsr = p.sbuf("r_eps", [128, 2], F32)
                p.memset("dve", epsr[:, 0:1], EPS)
                p.memset("dve", epsr[:, 1:2], 64e-5)
                with p.scope():
                    raw = p.sbuf("r_raw", [128, T], F32)
                    sh = p.sbuf("r_sh", [128, T], F32)
                    g3 = lambda v: v.rearrange("p (r c) -> p r c", c=64)
                    for i in range(12):
                        rows = slice(O_RW + 128 * i, O_RW + 128 * (i + 1))
                        for c in range(0, T, 1088):
                            p.dma(raw.k(c)[:, c:c + 1088], pT[rows, c:c + 1088])
                        p.memset("pool", sh[:, :], 0.0)
                        X, S_ = g3(raw[:, LC:T]), g3(sh[:, LC:T])
                        p.stt("dve", S_[:, :, 1:64], X[:, :, 0:63], cmk[:, 0:1], S_[:, :, 1:64], ALU.mult, ALU.add)
                        p.stt("dve", S_[:, :, 0:63], X[:, :, 1:64], cmk[:, 1:2], S_[:, :, 0:63], ALU.mult, ALU.add)
                        p.stt("dve", S_[:, 1:64, :], X[:, 0:63, :], cmk[:, 2:3], S_[:, 1:64, :], ALU.mult, ALU.add)
                        p.stt("dve", S_[:, 0:63, :], X[:, 1:64, :], cmk[:, 3:4], S_[:, 0:63, :], ALU.mult, ALU.add)
                        p.stt("dve", sh[:, 1:LC], raw[:, 0:LC - 1], cmk[:, 4:5], sh[:, 1:LC], ALU.mult, ALU.add)
                        p.stt("dve", sh[:, 0:LC - 1], raw[:, 1:LC], cmk[:, 5:6], sh[:, 0:LC - 1], ALU.mult, ALU.add)
                        p.tt("pool", sh[:, :], sh[:, :], raw[:, :], ALU.subtract)
                        p.stt("dve", sh[:, :], sh[:, :], mu[:, i:i + 1], raw[:, :], ALU.mult, ALU.add)
                        for c in range(0, T, 1088):
                            p.dma(pT.k(("rw", i, c))[rows, c:c + 1088], sh[:, c:c + 1088])

                rmask = p.sbuf("r_rmask", [128, SB], F32)
                p.memset("dve", rmask[:, :], 1.0)
                p.memset("dve", rmask[:, :].rearrange("p (n t) -> p n t", t=64)[:, :, 0:1], 0.0)
                shared = {nm: p.sbuf("r_" + nm, [128, SB], F32) for nm in ("wdn", "adn", "th")}
                NU = 6
                wk = {nm: [p.sbuf("r_w%s%d" % (nm, u), [128, 64], F32) for u in range(NU)] for nm in
                      ("M", "MT", "X0", "X1", "Y0", "Y1", "T0", "T1", "LakT", "LrkT", "LrbT", "Vt", "Kt", "Bt", "RHS", "Z", "nZ", "S")}
                for nm_ in wk:
                    for u in range(NU):
                        p.memset("pool", wk[nm_][u][:, :], 0.0)
                pt = {(hp, nm): p.sbuf("r_%s%d" % (nm, hp), [128, SB], F32) for hp in range(3) for nm in
                      ("r", "k", "v", "kk", "lw", "lp", "E", "Ei", "ic", "A", "B", "K", "R", "ob")}
                tmp = p.sbuf("r_tmp", [128, SB], F32)
                colblocks = [(0, 256)]
                bank_sets = {0: [0, 1, 2], 1: [3, 4, 5], 2: [6, 7]}

                def lora_sig(dst, hp, d, which, src, bias_col):
                    dsl = slice(d * 64, d * 64 + 64)
                    for bi_, (c0, nb) in enumerate(colblocks):
                        pz_ = ps[bank_sets[hp][bi_ % len(bank_sets[hp])]]
                        p.mm(pz_[:, 0:nb], ups[dsl, which, hp * 128:(hp + 1) * 128], src[dsl, c0:c0 + nb])
                        p.act(dst[:, c0:c0 + nb], pz_[:, 0:nb], AF.Sigmoid, bias=bias_col)

                def blocksum(dst_fn, src, hp):
                    for bi_, (c0, nb) in enumerate(colblocks):
                        pz_ = ps[bank_sets[hp][bi_ % len(bank_sets[hp])]]
                        p.mm(pz_[:, 0:nb], ones128[:, :], src[:, c0:c0 + nb])
                        dst_fn(pz_[:, 0:nb], c0, nb)

                for d in range(2):
                    if d == 0:
                        groups = [(sb, list(range(sb * CPS, (sb + 1) * CPS))) for sb in range(NSB)]
                    else:
                        groups = [(0, [3, 2, 1, 0])] + [(sb, list(range((sb + 1) * CPS - 1, sb * CPS - 1, -1))) for sb in range(NSB - 1, 0, -1)] \
                                 + [(0, list(range(CPS - 1, 3, -1)))]
                    ordv = (lambda v: v) if d == 0 else rev_view
                    for u in range(NU):
                        p.memset("dve", wk["S"][u][:, :], 0.0)
                    pslots_u = {hp * 2 + hh: PsumSlots(bank_sets[hp], hh) for hp in range(3) for hh in range(2)}
                    for (sb, chunks) in groups:
                        if not chunks:
                            continue
                        t0 = sb * SB
                        p.dma(shared["wdn"][:, :], pT[O_RW + 1152:O_RW + 1280, t0:t0 + SB])
                        p.dma(shared["adn"][:, :], pT[O_RW + 1280:O_RW + 1408, t0:t0 + SB])
                        p.act(shared["th"][:, :], shared["wdn"][:, :], AF.Tanh)
                        for hp in range(3):
                            G = lambda nm, hp=hp: pt[(hp, nm)]
                            for wi, nm in enumerate(("r", "k", "v")):
                                row0 = O_RW + wi * 384 + hp * 128
                                p.dma(G(nm)[:, :], pT[row0:row0 + 128, t0:t0 + SB])
                            if d == 1:
                                p.dma(G("ob")[:, :], yrw[hp * 128:(hp + 1) * 128, t0:t0 + SB])
                            lora_sig(G("lw"), hp, d, 0, shared["th"], vec[:, hp, 5 + d:6 + d])
                            p.ts("dve", G("lw")[:, :], G("lw")[:, :], -0.6065306597, ALU.mult)
                            lora_sig(G("ic"), hp, d, 1, shared["adn"], vec[:, hp, 7 + d:8 + d])
                            p.ts("dve", G("kk")[:, :], G("k")[:, :], vec[:, hp, 0:1], ALU.mult)
                            p.tt("pool", tmp[:, :], G("kk")[:, :], G("kk")[:, :], ALU.mult)
                            def kkfin(pv, c0, nb, G=G):
                                p.act(tmp[:, c0:c0 + nb], pv, AF.Sqrt, bias=epsr[:, 0:1])
                            blocksum(kkfin, tmp, hp)
                            p.recip(tmp[:, :], tmp[:, :])
                            p.tt("dve", G("kk")[:, :], G("kk")[:, :], tmp[:, :], ALU.mult)
                            p.scan(ordv(G("lp")[:, :]), ordv(rmask[:, :]) if d == 0 else rmask[:, :], ordv(G("lw")[:, :]), 0.0)
                            p.ts("dve", G("E")[:, :], G("lp")[:, :], 1.0, ALU.mult)
                            p.act(G("E")[:, :], G("E")[:, :], AF.Exp)
                            p.act(G("Ei")[:, :], G("lp")[:, :], AF.Exp, scale=-1.0)
                            p.tt("pool", tmp[:, :], G("lp")[:, :], G("lw")[:, :], ALU.subtract)
                            p.act(tmp[:, :], tmp[:, :], AF.Exp)
                            p.tt("dve", G("A")[:, :], G("kk")[:, :], tmp[:, :], ALU.mult)
                            p.tt("pool", G("R")[:, :], G("r")[:, :], G("E")[:, :], ALU.mult)
                            p.tt("pool", tmp[:, :], G("kk")[:, :], G("ic")[:, :], ALU.mult)
                            p.tt("dve", G("B")[:, :], tmp[:, :], G("Ei")[:, :], ALU.mult)
                            p.ts("dve", tmp[:, :], G("ic")[:, :], -1.0, ALU.add, vec[:, hp, 1:2], ALU.mult)
                            p.ts("dve", tmp[:, :], tmp[:, :], 1.0, ALU.add)
                            p.tt("pool", tmp[:, :], tmp[:, :], G("k")[:, :], ALU.mult)
                            p.tt("dve", G("K")[:, :], tmp[:, :], G("Ei")[:, :], ALU.mult)
                            if d == 0:
                                p.memset("pool", G("ob")[:, :], 0.0)
                        ms_, msT, miT = (CLS, CUS, CU) if d == 0 else (CUS, CLS, CL)
                        for n in chunks:
                            j = n - sb * CPS
                            cs = slice(j * 64, j * 64 + 64)
                            lastcol = j * 64 + (63 if d == 0 else 0)
                            units = []
                            for hp in range(3):
                                for hh in range(2):
                                    hs = slice(hh * 64, hh * 64 + 64)
                                    u = hp * 2 + hh
                                    units.append({"u": u, "hp": hp, "hs": hs, "p0": hh * 64, "ps": pslots_u[u],
                                                  "W": (lambda nm, u=u, hs=hs: wk[nm][u][hs, :]),
                                                  "F": (lambda nm, hp=hp, hs=hs, cs=cs: pt[(hp, nm)][hs, cs])})
                            for c in units:
                                W, F, hs, q = c["W"], c["F"], c["hs"], c["ps"]
                                Ih = cm[hs, CI, :]
                                q.group()
                                c["pM"], c["pMT"], c["pLak"], c["pLrk"], c["pLrb"] = (q.get() for _ in range(5))
                                c["pV"], c["pK"], c["pB"] = (q.get() for _ in range(3))
                                p.mm(c["pM"], F("A"), F("B"))
                                p.mm(c["pMT"], F("B"), F("A"))
                                p.mm(c["pLak"], F("K"), F("A"))
                                p.mm(c["pLrk"], F("K"), F("R"))
                                p.mm(c["pLrb"], F("B"), F("R"))
                                p.tr(c["pV"], F("v"), Ih)
                                p.tr(c["pK"], F("K"), Ih)
                                p.tr(c["pB"], F("B"), Ih)
                            for c in units:
                                W, hs = c["W"], c["hs"]
                                p.tt("dve", W("M"), c["pM"], cm[hs, ms_, :], ALU.mult)
                                p.tt("dve", W("MT"), c["pMT"], cm[hs, msT, :], ALU.mult)
                                p.tt("dve", W("LakT"), c["pLak"], cm[hs, msT, :], ALU.mult)
                                p.tt("dve", W("LrkT"), c["pLrk"], cm[hs, miT, :], ALU.mult)
                                p.tt("dve", W("LrbT"), c["pLrb"], cm[hs, miT, :], ALU.mult)
                                p.copy("dve", W("Vt"), c["pV"])
                                p.copy("dve", W("Kt"), c["pK"])
                                p.copy("dve", W("Bt"), c["pB"])
                            for c in units:
                                W, F, q = c["W"], c["F"], c["ps"]
                                q.group()
                                pr_ = q.get()
                                p.mm(pr_, F("A"), W("S"), start=True, stop=False)
                                p.mm(pr_, W("LakT"), W("Vt"), start=False, stop=True)
                                p.copy("dve", W("RHS"), pr_)
                            for c in units:
                                c["TT"] = neumann(c["hs"], c["W"]("M"), c["W"]("MT"), c["W"], c["ps"], c["p0"])
                            for c in units:
                                W, q = c["W"], c["ps"]
                                q.group()
                                pz_ = q.get()
                                p.mm(pz_, c["TT"], W("RHS"))
                                p.copy("dve", W("Z"), pz_)
                                p.ts("dve", W("nZ"), W("Z"), -1.0, ALU.mult)
                            for c in units:
                                W, F, q, hs, hp = c["W"], c["F"], c["ps"], c["hs"], c["hp"]
                                q.group()
                                po, pS = q.get(), q.get()
                                p.mm(po, W("S"), F("R"), start=True, stop=False)
                                p.mm(po, W("Vt"), W("LrkT"), start=False, stop=False)
                                p.mm(po, W("nZ"), W("LrbT"), start=False, stop=True)
                                p.mm(pS, W("Kt"), W("Vt"), start=True, stop=False)
                                p.mm(pS, W("Bt"), W("nZ"), start=False, stop=True)
                                obv = pt[(hp, "ob")].k((c["u"], n))[hs, cs]
                                p.tt("dve", obv, obv, po, ALU.add)
                                p.tt("dve", W("S"), W("S"), pS, ALU.add)
                                p.ts("dve", W("S"), W("S"), pt[(hp, "E")][hs, lastcol:lastcol + 1], ALU.mult)
                        for hp in range(3):
                            lo, hi = min(chunks) * 64 - t0, (max(chunks) + 1) * 64 - t0
                            p.dma(yrw.k((hp, sb, lo))[hp * 128:(hp + 1) * 128, t0 + lo:t0 + hi], pt[(hp, "ob")][:, lo:hi])

                rwo16 = [p.sbuf("r_o16_%d" % i, [128, SB], BF16) for i in range(2)]
                gdn_t = shared["wdn"]
                sgd = shared["th"]
                for sb in range(NSB):
                    t0 = sb * SB
                    p.dma(gdn_t[:, :], pT[O_RW + 1408:O_RW + 1536, t0:t0 + SB])
                    p.dma(shared["adn"][:, :], pT[O_RW + 1280:O_RW + 1408, t0:t0 + SB])
                    p.act(sgd[:, :], gdn_t[:, :], AF.Sigmoid)
                    for hp in range(3):
                        G = lambda nm, hp=hp: pt[(hp, nm)]
                        for wi, nm in enumerate(("r", "k", "v")):
                            row0 = O_RW + wi * 384 + hp * 128
                            p.dma(G(nm)[:, :], pT[row0:row0 + 128, t0:t0 + SB])
                        y = G("ob")
                        p.dma(y[:, :], yrw[hp * 128:(hp + 1) * 128, t0:t0 + SB])
                        def mfin(pv, c0, nb, y=y, G=G):
                            p.stt("dve", G("A")[:, c0:c0 + nb], pv, -1.0 / 64.0, y[:, c0:c0 + nb], ALU.mult, ALU.add)
                        blocksum(mfin, y, hp)
                        p.tt("pool", tmp[:, :], G("A")[:, :], G("A")[:, :], ALU.mult)
                        def vfin(pv, c0, nb, G=G):
                            p.act(G("B")[:, c0:c0 + nb], pv, AF.Sqrt, bias=epsr[:, 1:2], scale=1.0 / 64.0)
                        blocksum(vfin, tmp, hp)
                        p.recip(G("B")[:, :], G("B")[:, :])
                        p.tt("dve", G("A")[:, :], G("A")[:, :], G("B")[:, :], ALU.mult)
                        p.ts("dve", G("A")[:, :], G("A")[:, :], vec[:, hp, 3:4], ALU.mult, vec[:, hp, 4:5], ALU.add)
                        lora_sig(G("ic"), hp, 0, 1, shared["adn"], vec[:, hp, 7:8])
                        lora_sig(G("E"), hp, 1, 1, shared["adn"], vec[:, hp, 8:9])
                        p.tt("dve", G("ic")[:, :], G("ic")[:, :], G("E")[:, :], ALU.add)
                        p.ts("dve", G("ic")[:, :], G("ic")[:, :], -2.0, ALU.add, vec[:, hp, 1:2], ALU.mult)
                        p.ts("dve", G("ic")[:, :], G("ic")[:, :], 2.0, ALU.add)
                        p.tt("pool", tmp[:, :], G("r")[:, :], G("k")[:, :], ALU.mult)
                        p.stt("dve", tmp[:, :], tmp[:, :], vec[:, hp, 2:3], G("ic")[:, :], ALU.mult, ALU.mult)
                        def bfin(pv, c0, nb, G=G):
                            p.tt("dve", G("K")[:, c0:c0 + nb], pv, G("v")[:, c0:c0 + nb], ALU.mult)
                        blocksum(bfin, tmp, hp)
                        p.tt("dve", G("A")[:, :], G("A")[:, :], G("K")[:, :], ALU.add)
                        ob16 = G("R")
                        for bi_, (c0, nb) in enumerate(colblocks):
                            pz_ = ps[bank_sets[hp][bi_ % len(bank_sets[hp])]]
                            p.mm(pz_[:, 0:nb], ups[:, 2, hp * 128:(hp + 1) * 128], sgd[:, c0:c0 + nb])
                            p.tt("dve", G("B")[:, c0:c0 + nb], G("A")[:, c0:c0 + nb], pz_[:, 0:nb], ALU.mult)
                        o16 = rwo16[hp % 2]
                        p.copy("pool", o16[:, :], G("B")[:, :])
                        p.dma(yT.k(("rw", hp, sb))[640 + hp * 128:640 + (hp + 1) * 128, t0:t0 + SB], o16[:, :])

        for l in range(debug.get("layers", DEPTH)):
            with p.scope():
                win_sb = p.sbuf("win_sb", [128, 8, INC], BF16)
                with p.scope():
                    stg = [p.sbuf("stg_a", [128, 8, 512], F32), p.sbuf("stg_b", [128, 8, 512], F32)]
                    wsrc = w_in.h.ap()[l].rearrange("(k p) n -> p k n", p=128)
                    load_cast(win_sb, lambda c0, c1: wsrc[:, :, c0:c1], INC, w_in, stg)
                xbs = [p.sbuf("xb%d" % i, [128, 8, 512], F32) for i in range(2)]
                xns = [p.sbuf("xn%d" % i, [128, 8, 512], BF16) for i in range(2)]
                sq = p.sbuf("sq", [128, 8, 512], BF16)
                rstd = p.sbuf("rstd", [128, 512], F32)
                evs = [p.sbuf("ev%d" % i, [128, 512], F32) for i in range(4)]
                ntmp = [p.sbuf("ntmp%d" % i, [128, 512], F32) for i in range(2)]
                ei = 0
                for bi, (s, n, isctx) in enumerate(token_blocks(512)):
                    xb, xn = xbs[bi % 2], xns[bi % 2]
                    w = 1 if isctx else 0
                    p.dma(xb[:, :, 0:n], View(xres, xres_v(s, n), None))
                    norm_block(xb, xn, n, coef[:, l, w, 0, :], coef[:, l, w, 1, :], sq, rstd, ps[7], ntmp)
                    for ti, (c0, m) in enumerate(IN_TILES):
                        pst = ps[ti % 6]
                        for k in range(8):
                            p.mm(pst[0:m, 0:n], win_sb[:, k, c0:c0 + m], xn[:, k, 0:n], start=(k == 0), stop=(k == 7))
                        ev = evs[ei % 4]
                        ei += 1
                        p.copy("act" if ti % 2 == 0 else "dve", ev[0:m, 0:n], pst[0:m, 0:n])
                        p.dma(pT.k(("A", bi, ti))[c0:c0 + m, s:s + n], ev[0:m, 0:n])
                    for tq in range(n // 128):
                        pst = ps[6]
                        for k in range(8):
                            p.mm(pst[:, 0:24], xn[:, k, tq * 128:(tq + 1) * 128], win_sb[:, k, 1792:1816], start=(k == 0), stop=(k == 7))
                        ev = evs[ei % 4]
                        ei += 1
                        p.copy("dve", ev[:, 0:24], pst[:, 0:24])
                        p.dma(pBA.k(("A", bi, tq))[s + tq * 128:s + (tq + 1) * 128, :], ev[:, 0:24])


            if debug.get("s5"):
                s5_mixer(l)
            if debug.get("gdn"):
                gdn_mixer(l)
            if debug.get("rwkv"):
                rwkv_mixer(l)
            if debug and debug.get("zero_mix"):
                with p.scope():
                    zb = p.sbuf("zb", [128, 8, 512], BF16)
                    p.memset("pool", zb[:], 0.0)
                    for (s, n, isctx) in token_blocks(512):
                        for (flag, ka, kb) in (("s5", 0, 2), ("gdn", 2, 5), ("rwkv", 5, 8)):
                            if not debug.get(flag):
                                p.dma(View(yT, yT_v(s, n)[:, ka:kb, :], ("z", s, ka)), zb[:, ka:kb, 0:n])

            with p.scope():
                wo_sb = p.sbuf("wo_sb", [128, 8, D], BF16)
                w1_sb = p.sbuf("w1_sb", [128, 8, 4 * D], BF16)
                w2_sb = p.sbuf("w2_sb", [128, 32, D], BF16)
                with p.scope():
                    stg = [p.sbuf("stg_a", [128, 8, 512], F32), p.sbuf("stg_b", [128, 8, 512], F32)]
                    s0 = w_out.h.ap()[l].rearrange("(k p) n -> p k n", p=128)
                    load_cast(wo_sb, lambda c0, c1: s0[:, :, c0:c1], D, w_out, stg)
                    s1 = w1.h.ap()[l].rearrange("(k p) n -> p k n", p=128)
                    load_cast(w1_sb, lambda c0, c1: s1[:, :, c0:c1], 4 * D, w1, stg)
                    s2 = w2.h.ap()[l].rearrange("(k p) n -> p k n", p=128)
                    for kq in range(4):
                        i = 0
                        for c0 in range(0, D, 512):
                            sgb = stg[i % 2]
                            i += 1
                            p.dma(sgb[:], View(w2, s2[:, kq * 8:(kq + 1) * 8, c0:c0 + 512], None))
                            p.copy("dve" if i % 2 else "pool", w2_sb[:, kq * 8:(kq + 1) * 8, c0:c0 + 512], sgb[:])
                NB = 256
                xbs = [p.sbuf("xb%d" % i, [128, 8, NB], F32) for i in range(2)]
                ybs = [p.sbuf("yb%d" % i, [128, 8, NB], BF16) for i in range(2)]
                hn = p.sbuf("hn", [128, 8, NB], BF16)
                sq = p.sbuf("sq", [128, 8, NB], BF16)
                rstd = p.sbuf("rstd", [128, NB], F32)
                hid = p.sbuf("hid", [128, 32, NB], BF16)
                rls = [p.sbuf("rl%d" % i, [128, NB], F32) for i in range(2)]
                ntmp = [p.sbuf("ntmp%d" % i, [128, NB], F32) for i in range(2)]
                last = (l == DEPTH - 1)
                for bi, (s, n, isctx) in enumerate(token_blocks(NB)):
                    if last and isctx:
                        continue
                    w = 1 if isctx else 0
                    xb, yb = xbs[bi % 2], ybs[bi % 2]
                    p.dma(xb[:, :, 0:n], View(xres, xres_v(s, n), None))
                    p.dma(yb[:, :, 0:n], View(yT, yT_v(s, n), None))
                    for dt_ in range(8):
                        pst = ps[dt_ % 4]
                        for k in range(8):
                            p.mm(pst[:, 0:n], wo_sb[:, k, dt_ * 128:(dt_ + 1) * 128], yb[:, k, 0:n], start=(k == 0), stop=(k == 7))
                        p.stt("dve", xb[:, dt_, 0:n], pst[:, 0:n], coef[:, l, w, 2, dt_:dt_ + 1], xb[:, dt_, 0:n], ALU.mult, ALU.add)
                    norm_block(xb, hn, n, coef[:, l, w, 3, :], coef[:, l, w, 4, :], sq, rstd, ps[7], ntmp)
                    for ft in range(32):
                        pst = ps[4 + ft % 3]
                        for k in range(8):
                            p.mm(pst[:, 0:n], w1_sb[:, k, ft * 128:(ft + 1) * 128], hn[:, k, 0:n], start=(k == 0), stop=(k == 7))
                        rl = rls[ft % 2]
                        p.act(rl[:, 0:n], pst[:, 0:n], AF.Relu)
                        p.tt("pool" if ft % 2 == 0 else "dve", hid[:, ft, 0:n], rl[:, 0:n], rl[:, 0:n], ALU.mult)
                    for dt_ in range(8):
                        pst = ps[dt_ % 4]
                        for ft in range(32):
                            p.mm(pst[:, 0:n], w2_sb[:, ft, dt_ * 128:(dt_ + 1) * 128], hid[:, ft, 0:n], start=(ft == 0), stop=(ft == 31))
                        p.stt("dve", xb[:, dt_, 0:n], pst[:, 0:n], coef[:, l, w, 5, dt_:dt_ + 1], xb[:, dt_, 0:n], ALU.mult, ALU.add)
                    if not last:
                        p.dma(View(xres, xres_v(s, n), None), xb[:, :, 0:n])
                    else:
                        for k in range(8):
                            p.act(sq[:, k, 0:n], xb[:, k, 0:n], AF.Square)
                        for k in range(8):
                            p.mm(ps[7][:, 0:n], ones_bf[:], sq[:, k, 0:n], start=(k == 0), stop=(k == 7))
                        rstd_from(rstd[:, 0:n], ps[7][:, 0:n])
                        for k in range(8):
                            p.stt("dve", xb[:, k, 0:n], xb[:, k, 0:n], nrm_sb[:, 4, k:k + 1], rstd[:, 0:n], ALU.mult, ALU.mult)
                        p.dma(View(outT, outT.h.ap().rearrange("(k p) t -> p k t", p=128)[:, :, s - LC:s - LC + n], ("o", s)), xb[:, :, 0:n])

        if debug and debug.get("tap"):
            debug["tap"](p, locals())
        p.wait_all("sp", [outT[:, :]] + ([dbg[:]] if dbg is not None else []))
        p.emit()
    return nc, p


def host_layout(inputs, b):
    f = np.float32
    x = np.asarray(inputs["x"], f)
    ctx = np.asarray(inputs["ctx"], f)
    c = np.asarray(inputs["c"], f)
    c_ctx = np.asarray(inputs["c_ctx"], f)
    fm = lambda v: np.ascontiguousarray(v.reshape(-1, 128).T)
    m = {}
    m["xT"] = np.ascontiguousarray(np.concatenate([ctx[b].T, x[b].T], axis=1))
    m["cT"] = np.ascontiguousarray(np.stack([fm(c[b]), fm(c_ctx)], axis=-1))
    m["mod_w"] = np.asarray(inputs["mod_w"], f)
    mb = np.asarray(inputs["mod_b"], f)
    m["mod_b"] = np.ascontiguousarray(np.stack([fm(mb[l]) for l in range(DEPTH)], axis=1))
    nm = [np.asarray(inputs["norm_mix"], f)[0], np.asarray(inputs["norm_mix"], f)[1],
          np.asarray(inputs["norm_mlp"], f)[0], np.asarray(inputs["norm_mlp"], f)[1], np.asarray(inputs["norm_final"], f)]
    m["nrm"] = np.ascontiguousarray(np.stack([fm(v) for v in nm], axis=1))
    G, P_, Cg = 16, 64, 16
    s5c = np.zeros((DEPTH, 128, 3, 16), f)
    s5B = np.zeros((DEPTH, 128, 2, 8, 128), f)
    s5C = np.zeros((DEPTH, 128, 2, 8, 128), f)
    are, aim, ldt = (np.asarray(inputs[k], f) for k in ("s5_a_re", "s5_a_im", "s5_log_dt"))
    bre, bim, cre, cim = (np.asarray(inputs[k], f) for k in ("s5_b_re", "s5_b_im", "s5_c_re", "s5_c_im"))
    for l in range(DEPTH):
        for d in range(2):
            for j in range(8):
                for gl in range(2):
                    g = 2 * j + gl
                    s5c[l, gl * 64:(gl + 1) * 64, 0, d * 8 + j] = are[l, d, g]
                    s5c[l, gl * 64:(gl + 1) * 64, 1, d * 8 + j] = aim[l, d, g]
                    s5c[l, gl * 64:(gl + 1) * 64, 2, d * 8 + j] = ldt[l, d, g]
        for j in range(8):
            for gl in range(2):
                g = 2 * j + gl
                r0 = 32 * (j % 4) + 16 * gl
                s5B[l, r0:r0 + 16, 0, j, gl * 64:(gl + 1) * 64] = bre[l, g].T
                s5B[l, r0:r0 + 16, 1, j, gl * 64:(gl + 1) * 64] = bim[l, g].T
                s5C[l, gl * 64:(gl + 1) * 64, 0, j, r0:r0 + 16] = cre[l, g].T
                s5C[l, gl * 64:(gl + 1) * 64, 1, j, r0:r0 + 16] = cim[l, g].T
    m["s5c"], m["s5B"], m["s5C"] = s5c, s5B, s5C
    sd, gb = np.asarray(inputs["s5_d"], f), np.asarray(inputs["s5_glu_b"], f)
    m["s5v"] = np.ascontiguousarray(np.stack([np.stack([fm(sd[l]), fm(gb[l])], axis=-1) for l in range(DEPTH)], axis=0))
    m["s5g"] = np.asarray(inputs["s5_glu_w"], f)
    cmat = np.zeros((128, 8, 64), f)
    ii = np.arange(64)
    mats = [np.eye(64), (ii[None, :] <= ii[:, None]), (ii[None, :] < ii[:, None]), (ii[None, :] >= ii[:, None]),
            (ii[None, :] > ii[:, None]), np.ones((64, 64)), np.repeat((ii == 63)[:, None], 64, 1), np.repeat((ii == 0)[:, None], 64, 1)]
    for i_, mt in enumerate(mats):
        cmat[0:64, i_, :] = mt.astype(f)
        cmat[64:128, i_, :] = mt.astype(f)
    m["cmat"] = cmat
    gcv = np.asarray(inputs["gdn_conv"], f)
    m["gdn_cw"] = np.ascontiguousarray(gcv.reshape(DEPTH, 5, 9, 128).transpose(3, 0, 2, 1))
    gal, gdb = np.asarray(inputs["gdn_a_log"], f), np.asarray(inputs["gdn_dt_bias"], f)
    gsc = np.stack([gal.reshape(DEPTH, 12), gdb.reshape(DEPTH, 12)], axis=1)
    m["gdn_sc"] = np.ascontiguousarray(np.broadcast_to(gsc[None], (128, DEPTH, 2, 12)))
    gnw = np.asarray(inputs["gdn_norm"], f)
    m["gdn_nw"] = np.ascontiguousarray(np.concatenate([gnw.T, gnw.T], axis=0))
    g_ = lambda k: np.asarray(inputs[k], f)
    m["rw_mu"] = np.ascontiguousarray(np.stack([fm(g_("rwkv_mu")[l]) for l in range(DEPTH)], axis=1))
    vecs = np.zeros((128, DEPTH, 3, 9), f)
    for l in range(DEPTH):
        srcs = [g_("rwkv_k_k")[l], g_("rwkv_k_a")[l], g_("rwkv_r_k")[l].reshape(-1), g_("rwkv_ln_w")[l], g_("rwkv_ln_b")[l],
                g_("rwkv_w0")[l, 0], g_("rwkv_w0")[l, 1], g_("rwkv_a0")[l, 0], g_("rwkv_a0")[l, 1]]
        for i_, v_ in enumerate(srcs):
            vecs[:, l, :, i_] = fm(v_)
    m["rw_vec"] = vecs
    ups = np.zeros((128, DEPTH, 3, 384), f)
    for l in range(DEPTH):
        ups[:, l, 0] = g_("rwkv_w_up")[l].reshape(128, 384)
        ups[:, l, 1] = g_("rwkv_a_up")[l].reshape(128, 384)
        ups[:, l, 2] = g_("rwkv_g_up")[l]
    m["rw_up"] = ups
    pp = np.arange(128)
    m["cmask"] = np.stack([(pp % 4 == 0), (pp % 4 == 1), (pp % 4 == 2), (pp % 4 == 3), (pp % 2 == 0), (pp % 2 == 1)], axis=1).astype(f)
    m["w_in"] = np.asarray(inputs["w_in"], f)
    m["w_out"] = np.asarray(inputs["w_out"], f)
    m["mlp_w1"] = np.asarray(inputs["mlp_w1"], f)
    m["mlp_w2"] = np.asarray(inputs["mlp_w2"], f)
    return m


def kernel(**inputs):
    nc, _ = build_program()
    in_maps = [host_layout(inputs, core // 2) for core in range(8)]
    res = run_bass_kernel_spmd(nc, in_maps, core_ids=list(range(8)))
    out = np.stack([np.asarray(res.results[2 * b]["outT"]).T for b in range(4)], axis=0)
    return np.ascontiguousarray(out.astype(np.float32))
```

```python
import numpy as np
import concourse.bass as bass
import concourse.mybir as mybir
from concourse.bass_utils import run_bass_kernel_spmd

F32 = mybir.dt.float32
BF16 = mybir.dt.bfloat16
ALU = mybir.AluOpType
AF = mybir.ActivationFunctionType

D = 1024
LC = 256
LL = 4096
T = LC + LL
DEPTH = 2
INC = 3352
EPS = 1e-6
NDMA = 16


class St:
    __slots__ = ("lw", "rd")

    def __init__(self):
        self.lw = None
        self.rd = {}


class Buf:
    def __init__(self, name, handle, dram=False):
        self.name = name
        self.h = handle
        self.dram = dram
        self.reg = {None: St()}

    def _ap(self, idx):
        base = self.h.ap() if self.dram else self.h
        return base[idx]

    def __getitem__(self, idx):
        return View(self, self._ap(idx), None)

    def k(self, key):
        return Keyed(self, key)

    def states(self, key):
        if key is None:
            return list(self.reg.values())
        if key not in self.reg:
            self.reg[key] = St()
        return [self.reg[key], self.reg[None]]


class Keyed:
    def __init__(self, buf, key):
        self.buf = buf
        self.key = key

    def __getitem__(self, idx):
        return View(self.buf, self.buf._ap(idx), self.key)


class View:
    def __init__(self, buf, ap, key):
        self.buf = buf
        self.ap = ap
        self.key = key

    def with_ap(self, ap):
        return View(self.buf, ap, self.key)

    def rearrange(self, pat, **kw):
        return View(self.buf, self.ap.rearrange(pat, **kw), self.key)

    def __getitem__(self, idx):
        return View(self.buf, self.ap[idx], self.key)

    def bcast(self, shape):
        return View(self.buf, self.ap.broadcast_to(shape), self.key)


class Scope:
    def __init__(self, p):
        self.p = p
        self.bufs = []

    def __enter__(self):
        from contextlib import ExitStack
        self.prev_stack = self.p.stack
        self.prev_scope = getattr(self.p, "cur_scope", None)
        self.es = ExitStack()
        self.es.__enter__()
        self.p.stack = self.es
        self.p.cur_scope = self
        return self

    def __exit__(self, *a):
        p = self.p
        freed = dict(getattr(p, "freed", {}))
        for b in self.bufs:
            for st in b.reg.values():
                toks = list(st.rd.items()) + ([st.lw] if st.lw is not None else [])
                for kk, vv in toks:
                    if freed.get(kk, 0) < vv:
                        freed[kk] = vv
        p.freed = freed
        p.stack = self.prev_stack
        p.cur_scope = self.prev_scope
        if self.prev_scope is not None:
            self.prev_scope.bufs.extend(self.bufs)
        self.es.__exit__(None, None, None)
        return False


class Prog:
    ENGS = ("pe", "dve", "act", "pool", "sp")

    def __init__(self, nc, stack):
        self.nc = nc
        self.stack = stack
        self.lists = {e: [] for e in self.ENGS}
        self.cnt = {e: 0 for e in self.ENGS}
        self.seen = {e: {} for e in self.ENGS}
        self.sems = {}
        for e in ("pe", "dve", "act", "pool"):
            self.sems[e] = stack.enter_context(nc.semaphore("s_" + e))
        self.dma_slots = {}
        for q in ("sp", "pool", "act"):
            self.dma_slots[q] = []
            for i in range(NDMA if q == "sp" else 4):
                key = "d_%s_%d" % (q, i)
                self.sems[key] = stack.enter_context(nc.semaphore(key))
                self.dma_slots[q].append([key, 0])
        self.dma_rr = {q: 0 for q in self.dma_slots}
        self.n_instr = 0

    def scope(self):
        return Scope(self)

    def uniq(self, name):
        self.n_names = getattr(self, "n_names", 0) + 1
        return "%s_%d" % (name, self.n_names)

    def sbuf(self, name, shape, dt):
        name = self.uniq(name)
        b = Buf(name, self.stack.enter_context(self.nc.sbuf_tensor(name, shape, dt)))
        b.reg[None].rd = dict(getattr(self, "freed", {}))
        if getattr(self, "cur_scope", None) is not None:
            self.cur_scope.bufs.append(b)
        return b

    def psum(self, name, shape, dt=F32):
        b = Buf(name, self.stack.enter_context(self.nc.psum_tensor(name, shape, dt)))
        b.is_psum = True
        return b

    def dram(self, name, shape, dt, kind="Internal"):
        return Buf(name, self.nc.dram_tensor(name, shape, dt, kind=kind), dram=True)

    def _need(self, eng, reads, writes):
        need = {}

        def add(tok):
            if tok is not None:
                if need.get(tok[0], 0) < tok[1]:
                    need[tok[0]] = tok[1]

        for v in reads:
            for st in v.buf.states(v.key):
                add(st.lw)
        for v in writes:
            for st in v.buf.states(v.key):
                add(st.lw)
                for kk, vv in st.rd.items():
                    add((kk, vv))
        out = []
        for kk, vv in need.items():
            if eng == "pe" and kk == "pe":
                continue
            if self.seen[eng].get(kk, 0) >= vv:
                continue
            self.seen[eng][kk] = vv
            out.append((kk, vv))
        return out

    def _mark(self, tok, reads, writes):
        for v in reads:
            st = v.buf.states(v.key)[0] if v.key is not None else v.buf.reg[None]
            st.rd[tok[0]] = tok[1]
        for v in writes:
            if v.key is None:
                for st in v.buf.reg.values():
                    st.lw = tok
                    st.rd = {}
            else:
                st = v.buf.states(v.key)[0]
                st.lw = tok
                st.rd = {}

    def op(self, eng, fn, reads, writes):
        writes = list(writes) + [v for v in reads if getattr(v.buf, "is_psum", False)]
        waits = self._need(eng, reads, writes)
        self.cnt[eng] += 1
        tok = (eng, self.cnt[eng])
        self.lists[eng].append((waits, fn, (eng, 1)))
        self._mark(tok, reads, writes)
        self.n_instr += 1

    def dma(self, out, in_, q="sp"):
        slots = self.dma_slots[q]
        i = self.dma_rr[q]
        self.dma_rr[q] = (i + 1) % len(slots)
        slot = slots[i]
        waits = self._need(q, [in_], [out])
        if slot[1] > 0 and self.seen[q].get(slot[0], 0) < slot[1]:
            self.seen[q][slot[0]] = slot[1]
            waits.append((slot[0], slot[1]))
        slot[1] += 16
        tok = (slot[0], slot[1])
        o_ap, i_ap = out.ap, in_.ap
        self.lists[q].append((waits, lambda e: e.dma_start(out=o_ap, in_=i_ap), (slot[0], 16)))
        self._mark(tok, [in_], [out])
        self.n_instr += 1

    def wait_all(self, eng, views):
        waits = self._need(eng, views, [])
        self.lists[eng].append((waits, None, None))

    def emit(self):
        nc = self.nc
        sems = self.sems
        with nc.Block() as block:
            def mk(lst):
                def body(e):
                    for waits, fn, inc in lst:
                        for kk, vv in waits:
                            e.wait_ge(sems[kk], vv)
                        if fn is not None:
                            fn(e).then_inc(sems[inc[0]], inc[1])
                return body

            block.tensor(mk(self.lists["pe"]))
            block.vector(mk(self.lists["dve"]))
            block.scalar(mk(self.lists["act"]))
            block.gpsimd(mk(self.lists["pool"]))
            block.sync(mk(self.lists["sp"]))

    def mm(self, out, lhsT, rhs, start=True, stop=True):
        self.op("pe", lambda e: e.matmul(out.ap, lhsT.ap, rhs.ap, start=start, stop=stop), [lhsT, rhs], [out])

    def tr(self, out, in_, ident):
        self.op("pe", lambda e: e.matmul(out.ap, in_.ap, ident.ap, start=True, stop=True), [in_, ident], [out])

    def tt(self, eng, out, a, b, op):
        self.op(eng, lambda e: e.tensor_tensor(out=out.ap, in0=a.ap, in1=b.ap, op=op), [a, b], [out])

    def ts(self, eng, out, a, s1, op0, s2=None, op1=None):
        reads = [a] + [s for s in (s1, s2) if isinstance(s, View)]
        s1a = s1.ap if isinstance(s1, View) else s1
        s2a = s2.ap if isinstance(s2, View) else s2
        if op1 is None:
            self.op(eng, lambda e: e.tensor_scalar(out=out.ap, in0=a.ap, scalar1=s1a, scalar2=None, op0=op0), reads, [out])
        else:
            self.op(eng, lambda e: e.tensor_scalar(out=out.ap, in0=a.ap, scalar1=s1a, scalar2=s2a, op0=op0, op1=op1), reads, [out])

    def stt(self, eng, out, a, s, b, op0, op1):
        reads = [a, b] + ([s] if isinstance(s, View) else [])
        sa = s.ap if isinstance(s, View) else s
        eng = "dve"
        self.op(eng, lambda e: e.scalar_tensor_tensor(out=out.ap, in0=a.ap, scalar=sa, in1=b.ap, op0=op0, op1=op1), reads, [out])

    def act(self, out, a, func, bias=None, scale=None):
        reads = [a] + [s for s in (bias, scale) if isinstance(s, View)]
        kw = {}
        if bias is not None:
            kw["bias"] = bias.ap if isinstance(bias, View) else bias
        if scale is not None:
            kw["scale"] = scale.ap if isinstance(scale, View) else scale
        self.op("act", lambda e: e.activation(out=out.ap, in_=a.ap, func=func, **kw), reads, [out])

    def copy(self, eng, out, a):
        if eng == "act":
            self.op("act", lambda e: e.activation(out=out.ap, in_=a.ap, func=AF.Copy), [a], [out])
        else:
            self.op(eng, lambda e: e.tensor_copy(out=out.ap, in_=a.ap), [a], [out])

    def recip(self, out, a):
        self.op("dve", lambda e: e.reciprocal(out=out.ap, in_=a.ap), [a], [out])

    def memset(self, eng, out, val):
        self.op(eng, lambda e: e.memset(out.ap, val), [], [out])

    def scan(self, out, d0, d1, init, op0=ALU.mult, op1=ALU.add):
        reads = [d0, d1] + ([init] if isinstance(init, View) else [])
        ia = init.ap if isinstance(init, View) else init
        self.op("dve", lambda e: e.tensor_tensor_scan(out=out.ap, data0=d0.ap, data1=d1.ap, initial=ia, op0=op0, op1=op1), reads, [out])


def rev_view(v):
    from concourse.ap import AP
    ap = v.ap
    l = [list(x) for x in ap.ap]
    assert l[-1][0] == 1, l
    off = ap.offset + (l[-1][1] - 1)
    l[-1][0] = -1
    return v.with_ap(AP(ap.tensor, off, l))


def token_blocks(n):
    nc_ = min(n, LC)
    blks = [(i, nc_, True) for i in range(0, LC, nc_)]
    for s in range(LC, T, n):
        blks.append((s, n, False))
    return blks


IN_TILES = [(0, 128), (128, 128)] + [(256 + 128 * i, 128) for i in range(12)] + [(1792, 24)] + \
           [(1816 + 128 * i, 128) for i in range(12)]


def build_program(debug=None):
    from contextlib import ExitStack
    if debug is None:
        debug = {"zero_mix": True, "s5": True, "gdn": True, "rwkv": True}
    nc = bass.Bass("TRN2", target_bir_lowering=False)
    with ExitStack() as stack:
        p = Prog(nc, stack)
        xT = p.dram("xT", [D, T], F32, kind="ExternalInput")
        cT = p.dram("cT", [128, 8, 2], F32, kind="ExternalInput")
        mod_w = p.dram("mod_w", [DEPTH, D, 6 * D], F32, kind="ExternalInput")
        mod_b = p.dram("mod_b", [128, DEPTH, 48], F32, kind="ExternalInput")
        nrm = p.dram("nrm", [128, 5, 8], F32, kind="ExternalInput")
        w_in = p.dram("w_in", [DEPTH, D, INC], F32, kind="ExternalInput")
        w_out = p.dram("w_out", [DEPTH, D, D], F32, kind="ExternalInput")
        w1 = p.dram("mlp_w1", [DEPTH, D, 4 * D], F32, kind="ExternalInput")
        w2 = p.dram("mlp_w2", [DEPTH, 4 * D, D], F32, kind="ExternalInput")
        s5c = p.dram("s5c", [DEPTH, 128, 3, 16], F32, kind="ExternalInput")
        s5B = p.dram("s5B", [DEPTH, 128, 2, 8, 128], F32, kind="ExternalInput")
        s5C = p.dram("s5C", [DEPTH, 128, 2, 8, 128], F32, kind="ExternalInput")
        s5v = p.dram("s5v", [DEPTH, 128, 2, 2], F32, kind="ExternalInput")
        s5g = p.dram("s5g", [DEPTH, 256, 256], F32, kind="ExternalInput")
        cmat_d = p.dram("cmat", [128, 8, 64], F32, kind="ExternalInput")
        gdn_cw = p.dram("gdn_cw", [128, DEPTH, 9, 5], F32, kind="ExternalInput")
        gdn_sc = p.dram("gdn_sc", [128, DEPTH, 2, 12], F32, kind="ExternalInput")
        gdn_nw = p.dram("gdn_nw", [128, DEPTH], F32, kind="ExternalInput")
        pBA = p.dram("pBA", [T, 24], F32)
        rw_mu = p.dram("rw_mu", [128, DEPTH, 12], F32, kind="ExternalInput")
        rw_vec = p.dram("rw_vec", [128, DEPTH, 3, 9], F32, kind="ExternalInput")
        rw_up = p.dram("rw_up", [128, DEPTH, 3, 384], F32, kind="ExternalInput")
        cmask = p.dram("cmask", [128, 6], F32, kind="ExternalInput")
        yrw = p.dram("yrw", [384, T], F32)
        outT = p.dram("outT", [D, LL], F32, kind="ExternalOutput")
        xres = p.dram("xres", [D, T], F32)
        pT = p.dram("pT", [INC, T], F32)
        yT = p.dram("yT", [D, T], BF16)
        dbg = None
        if debug and "shape" in debug:
            dbg = p.dram("dbg", list(debug["shape"]), F32, kind="ExternalOutput")

        ones_bf = p.sbuf("ones_bf", [128, 128], BF16)
        p.memset("dve", ones_bf[:], 1.0)
        modv = p.sbuf("modv", [128, DEPTH, 48, 2], F32)
        modb_sb = p.sbuf("modb_sb", [128, DEPTH, 48], F32)
        nrm_sb = p.sbuf("nrm_sb", [128, 5, 8], F32)
        sc_sb = p.sbuf("sc_sb", [128, 8, 2], F32)
        coef = p.sbuf("coef", [128, DEPTH, 2, 6, 8], F32)
        ps = [p.psum("ps%d" % i, [128, 512]) for i in range(8)]

        p.dma(modb_sb[:], mod_b[:])
        p.dma(nrm_sb[:], nrm[:])
        p.dma(sc_sb[:], cT[:])
        sg = p.sbuf("sg", [128, 8, 2], F32)
        p.act(sg[:], sc_sb[:], AF.Sigmoid)
        p.tt("dve", sc_sb[:], sc_sb[:], sg[:], ALU.mult)

        with p.scope():
            mw = [p.sbuf("mw_a", [128, 8, 512], F32), p.sbuf("mw_b", [128, 8, 512], F32)]
            it = 0
            for l in range(DEPTH):
                src = mod_w.h.ap()[l].rearrange("(k p) n -> p k n", p=128)
                for g in range(12):
                    b = mw[it % 2]
                    it += 1
                    p.dma(b[:], View(mod_w, src[:, :, g * 512:(g + 1) * 512], None))
                    pst = ps[g % 2]
                    for jj in range(4):
                        for k in range(8):
                            p.mm(pst[:, jj * 2:jj * 2 + 2], b[:, k, jj * 128:(jj + 1) * 128], sc_sb[:, k, :],
                                 start=(k == 0), stop=(k == 7))
                    for w in range(2):
                        p.tt("dve", modv[:, l, g * 4:(g + 1) * 4, w],
                             pst[:, 0:8].rearrange("p (j w) -> p j w", w=2)[:, :, w],
                             modb_sb[:, l, g * 4:(g + 1) * 4], ALU.add)
        for l in range(DEPTH):
            for w in range(2):
                mv = lambda j0: modv[:, l, j0:j0 + 8, w]
                p.stt("dve", coef[:, l, w, 0, :], mv(8), 1.0, nrm_sb[:, l, :], ALU.add, ALU.mult)
                p.copy("dve", coef[:, l, w, 1, :], mv(0))
                p.copy("dve", coef[:, l, w, 2, :], mv(16))
                p.stt("dve", coef[:, l, w, 3, :], mv(32), 1.0, nrm_sb[:, 2 + l, :], ALU.add, ALU.mult)
                p.copy("dve", coef[:, l, w, 4, :], mv(24))
                p.copy("dve", coef[:, l, w, 5, :], mv(40))

        for k in range(8):
            p.dma(xres[k * 128:(k + 1) * 128, :], xT[k * 128:(k + 1) * 128, :])

        xres_v = lambda s, n: xres.h.ap().rearrange("(k p) t -> p k t", p=128)[:, :, s:s + n]
        yT_v = lambda s, n: yT.h.ap().rearrange("(k p) t -> p k t", p=128)[:, :, s:s + n]

        def load_cast(dst, src_ap_fn, ncols, src_buf, stg, engs=("dve", "pool")):
            i = 0
            for c0 in range(0, ncols, 512):
                c1 = min(ncols, c0 + 512)
                s = stg[i % 2]
                p.dma(s[:, :, 0:c1 - c0], View(src_buf, src_ap_fn(c0, c1), None))
                p.copy(engs[i % 2], dst[:, :, c0:c1], s[:, :, 0:c1 - c0])
                i += 1

        eps_sb = p.sbuf("eps_sb", [128, 1], F32)
        p.memset("dve", eps_sb[:], EPS)

        def rstd_from(dst, src, scale=1.0 / D, eps=None):
            p.act(dst, src, AF.Sqrt, bias=(eps if eps is not None else eps_sb)[:, 0:1], scale=scale)
            p.recip(dst, dst)

        def norm_block(xb, xn, n, cf_A, cf_sh, sq, rstd, pst, ntmp):
            for k in range(8):
                p.act(sq[:, k, 0:n], xb[:, k, 0:n], AF.Square)
            for k in range(8):
                p.mm(pst[:, 0:n], ones_bf[:], sq[:, k, 0:n], start=(k == 0), stop=(k == 7))
            rstd_from(rstd[:, 0:n], pst[:, 0:n])
            for k in range(8):
                eng = "dve" if k % 2 == 0 else "pool"
                tmp = ntmp[k % 2]
                p.tt(eng, tmp[:, 0:n], xb[:, k, 0:n], rstd[:, 0:n], ALU.mult)
                p.act(xn[:, k, 0:n], tmp[:, 0:n], AF.Identity, bias=cf_sh[:, k:k + 1], scale=cf_A[:, k:k + 1])

        PI = float(np.pi)
        cst = p.sbuf("cst", [128, 4], F32)
        p.memset("dve", cst[:, 0:1], -3.1415915)
        p.memset("dve", cst[:, 1:2], 1.0)

        def sincos(dst_c, dst_s, th, shp, tmpf):
            t_y, t_k, t_m, t_f, t_i = tmpf
            for dst, shift in ((dst_s, 0.5), (dst_c, 0.75)):
                p.ts("dve", t_y, th, 1.0 / (2 * PI), ALU.mult, shift, ALU.add)
                p.copy("dve", t_i, t_y)
                p.copy("dve", t_k, t_i)
                p.tt("dve", t_m, t_k, t_y, ALU.is_gt)
                p.tt("dve", t_k, t_k, t_m, ALU.subtract)
                p.tt("dve", t_f, t_y, t_k, ALU.subtract)
                p.ts("dve", t_f, t_f, 2 * PI, ALU.mult, -PI, ALU.add)
                p.tt("dve", t_m, t_f, t_f, ALU.mult)
                coefs = [(-1.0) ** k / float(np.prod(np.arange(1, 2 * k + 2, dtype=np.float64))) for k in range(10)]
                p.memset("dve", t_k, coefs[9])
                for k in range(8, -1, -1):
                    p.tt("dve", t_k, t_k, t_m, ALU.mult)
                    p.ts("dve", t_k, t_k, coefs[k], ALU.add)
                p.tt("dve", dst, t_k, t_f, ALU.mult)

        S5TC = 256

        def s5_mixer(l):
            TC = S5TC
            NCH = T // TC
            with p.scope():
                I32 = mybir.dt.int32
                c_sb = p.sbuf("s5c_sb", [128, 3, 16], F32)
                p.dma(c_sb[:], s5c[l])
                sm = {nm: p.sbuf("s5_" + nm, [128, 16], F32) for nm in
                      ("dt", "th", "r", "c1", "s1", "x", "y", "den", "cre", "cim", "t0", "t1", "ty", "tk", "tm", "tf", "cT", "sT")}
                ti32 = p.sbuf("s5_ti", [128, 16], I32)
                V = lambda nm: sm[nm][:, :]
                def exp_acc(dst, src):
                    tq, e_ = V("ty"), V("tk")
                    p.ts("dve", tq, src, 1.0 / 16.0, ALU.mult)
                    p.memset("dve", e_, 1.0)
                    for k in range(10, 0, -1):
                        p.tt("dve", e_, e_, tq, ALU.mult)
                        p.ts("dve", e_, e_, 1.0 / k, ALU.mult, 1.0, ALU.add)
                    for _ in range(4):
                        p.tt("dve", e_, e_, e_, ALU.mult)
                    p.copy("dve", dst, e_)
                exp_acc(V("dt"), c_sb[:, 2, :])
                p.tt("dve", V("th"), V("dt"), c_sb[:, 1, :], ALU.mult)
                p.tt("dve", V("t0"), V("dt"), c_sb[:, 0, :], ALU.mult)
                exp_acc(V("r"), V("t0"))
                sincos(V("c1"), V("s1"), V("th"), None, (V("ty"), V("tk"), V("tm"), V("tf"), ti32[:, :]))
                p.tt("dve", V("x"), V("r"), V("c1"), ALU.mult)
                p.ts("dve", V("x"), V("x"), -1.0, ALU.add)
                p.tt("dve", V("y"), V("r"), V("s1"), ALU.mult)
                p.tt("dve", V("den"), c_sb[:, 0, :], c_sb[:, 0, :], ALU.mult)
                p.tt("dve", V("t0"), c_sb[:, 1, :], c_sb[:, 1, :], ALU.mult)
                p.tt("dve", V("den"), V("den"), V("t0"), ALU.add)
                p.recip(V("den"), V("den"))
                p.tt("dve", V("t0"), V("x"), c_sb[:, 0, :], ALU.mult)
                p.tt("dve", V("t1"), V("y"), c_sb[:, 1, :], ALU.mult)
                p.tt("dve", V("t0"), V("t0"), V("t1"), ALU.add)
                p.tt("dve", V("cre"), V("t0"), V("den"), ALU.mult)
                p.tt("dve", V("t0"), V("y"), c_sb[:, 0, :], ALU.mult)
                p.tt("dve", V("t1"), V("x"), c_sb[:, 1, :], ALU.mult)
                p.tt("dve", V("t0"), V("t0"), V("t1"), ALU.subtract)
                p.tt("dve", V("cim"), V("t0"), V("den"), ALU.mult)

                B_bf = p.sbuf("s5B_bf", [128, 2, 8, 128], BF16)
                C_bf = p.sbuf("s5C_bf", [128, 2, 8, 128], BF16)
                G_bf = p.sbuf("s5G_bf", [128, 2, 256], BF16)
                v_sb = p.sbuf("s5v_sb", [128, 2, 2], F32)
                p.dma(v_sb[:], s5v[l])
                with p.scope():
                    stg = p.sbuf("s5stg", [128, 2, 8, 128], F32)
                    p.dma(stg[:], s5B[l])
                    p.copy("dve", B_bf[:], stg[:])
                    stg2 = p.sbuf("s5stg2", [128, 2, 8, 128], F32)
                    p.dma(stg2[:], s5C[l])
                    p.copy("pool", C_bf[:], stg2[:])
                    stg3 = p.sbuf("s5stg3", [128, 2, 256], F32)
                    p.dma(stg3[:], View(s5g, s5g.h.ap()[l].rearrange("(k p) n -> p k n", p=128), None))
                    p.copy("dve", G_bf[:], stg3[:])

                u_b = p.sbuf("s5u_b", [128, 2, T], BF16)
                yacc = p.sbuf("s5yacc", [128, 2, T], F32)
                ustg = [p.sbuf("s5ustg%d" % q, [128, 1088], F32) for q in range(2)]
                uq = 0
                for i in range(2):
                    for c in range(0, T, 1088):
                        st_ = ustg[uq % 2]
                        uq += 1
                        p.dma(st_[:, :], pT[i * 128:(i + 1) * 128, c:c + 1088])
                        p.copy("pool" if i else "act", u_b.k((i, c))[:, i, c:c + 1088], st_[:, :])

                tab = {nm: p.sbuf("s5tab_" + nm, [128, 8, TC], F32) for nm in ("er", "ei", "mr", "mi", "rt")}
                wk = {nm: [p.sbuf("s5w_%s%d" % (nm, q), [128, TC], F32) for q in range(2)] for nm in
                      ("br", "bi", "a", "b", "c", "d", "dr", "di", "gr", "gi")}
                hb = {nm: p.sbuf("s5h_" + nm, [128, 8, TC], BF16) for nm in ("re", "im")}
                tA_buf = p.sbuf("s5tA", [128, 8, TC], F32)
                carry = p.sbuf("s5carry", [128, 8, 2], F32)
                sc8 = {nm: p.sbuf("s5s8_" + nm, [128, 8], F32) for nm in ("mc", "ms", "t0", "t1", "t2")}
                ctmp = p.sbuf("s5ctmp", [128, 4], F32)

                for d in range(2):
                    ds = slice(d * 8, d * 8 + 8)
                    p.memset("dve", tab["er"][:, :, 0:1], 1.0)
                    p.memset("dve", tab["ei"][:, :, 0:1], 0.0)
                    p.copy("dve", sc8["mc"][:, :], sm["c1"][:, ds])
                    p.copy("dve", sc8["ms"][:, :], sm["s1"][:, ds])
                    m = 1
                    while m < TC:
                        bc = lambda v: v[:, :].rearrange("p (j o) -> p j o", o=1).bcast([128, 8, m])
                        er0, ei0 = tab["er"][:, :, 0:m], tab["ei"][:, :, 0:m]
                        er1, ei1 = tab["er"][:, :, m:2 * m], tab["ei"][:, :, m:2 * m]
                        t_a, t_b = tab["mr"][:, :, 0:m], tab["mi"][:, :, 0:m]
                        p.tt("dve", t_a, er0, bc(sc8["mc"]), ALU.mult)
                        p.tt("pool", t_b, ei0, bc(sc8["ms"]), ALU.mult)
                        p.tt("dve", er1, t_a, t_b, ALU.subtract)
                        p.tt("dve", t_a, er0, bc(sc8["ms"]), ALU.mult)
                        p.tt("pool", t_b, ei0, bc(sc8["mc"]), ALU.mult)
                        p.tt("dve", ei1, t_a, t_b, ALU.add)
                        p.tt("dve", sc8["t0"][:, :], sc8["mc"][:, :], sc8["mc"][:, :], ALU.mult)
                        p.tt("dve", sc8["t1"][:, :], sc8["ms"][:, :], sc8["ms"][:, :], ALU.mult)
                        p.tt("dve", sc8["t2"][:, :], sc8["mc"][:, :], sc8["ms"][:, :], ALU.mult)
                        p.tt("dve", sc8["mc"][:, :], sc8["t0"][:, :], sc8["t1"][:, :], ALU.subtract)
                        p.ts("dve", sc8["ms"][:, :], sc8["t2"][:, :], 2.0, ALU.mult)
                        m *= 2
                    bcT = lambda v: v.rearrange("p (j o) -> p j o", o=1).bcast([128, 8, TC])
                    cre_b, cim_b = bcT(sm["cre"][:, ds]), bcT(sm["cim"][:, ds])
                    t_a = tA_buf
                    p.tt("dve", tab["mr"][:], tab["er"][:], cre_b, ALU.mult)
                    p.tt("pool", t_a[:], tab["ei"][:], cim_b, ALU.mult)
                    p.tt("dve", tab["mr"][:], tab["mr"][:], t_a[:], ALU.add)
                    p.tt("dve", tab["mi"][:], tab["er"][:], cim_b, ALU.mult)
                    p.tt("pool", t_a[:], tab["ei"][:], cre_b, ALU.mult)
                    p.tt("dve", tab["mi"][:], tab["mi"][:], t_a[:], ALU.subtract)
                    p.memset("pool", tab["rt"][:], 1.0)
                    p.tt("pool", tab["rt"][:], tab["rt"][:], bcT(sm["r"][:, ds]), ALU.mult)
                    p.memset("dve", carry[:], 0.0)

                    def ord_(v):
                        return v if d == 0 else rev_view(v)

                    chunks = list(range(NCH)) if d == 0 else [0] + list(range(NCH - 1, 0, -1))
                    for ci, ch in enumerate(chunks):
                        t0_ = ch * TC
                        for j in range(8):
                            q = j % 2
                            pr, pi_ = ps[(2 * j) % 6], ps[(2 * j + 1) % 6]
                            p.mm(pr[:, 0:TC], B_bf[:, 0, j, :], u_b[:, j // 4, t0_:t0_ + TC])
                            p.mm(pi_[:, 0:TC], B_bf[:, 1, j, :], u_b[:, j // 4, t0_:t0_ + TC])
                            br, bi = wk["br"][q][:, :], wk["bi"][q][:, :]
                            p.copy("act", br, pr[:, 0:TC])
                            p.copy("act", bi, pi_[:, 0:TC])
                            mr, mi = ord_(tab["mr"][:, j, :]), ord_(tab["mi"][:, j, :])
                            er, ei = ord_(tab["er"][:, j, :]), ord_(tab["ei"][:, j, :])
                            a_, b_, c_, d_ = (wk[nm][q][:, :] for nm in ("a", "b", "c", "d"))
                            dr, di = wk["dr"][q][:, :], wk["di"][q][:, :]
                            gr, gi = wk["gr"][q][:, :], wk["gi"][q][:, :]
                            p.tt("dve", a_, br, mr, ALU.mult)
                            p.tt("pool", b_, bi, mi, ALU.mult)
                            p.tt("pool", c_, br, mi, ALU.mult)
                            p.tt("dve", d_, bi, mr, ALU.mult)
                            p.tt("dve", dr, a_, b_, ALU.subtract)
                            p.tt("pool", di, c_, d_, ALU.add)
                            p.scan(ord_(gr), tab["rt"][:, j, :], ord_(dr), carry[:, j, 0:1])
                            p.scan(ord_(gi), tab["rt"][:, j, :], ord_(di), carry[:, j, 1:2])
                            lr = gr[:, TC - 1:TC] if d == 0 else gr[:, 0:1]
                            li = gi[:, TC - 1:TC] if d == 0 else gi[:, 0:1]
                            mc, ms_ = sc8["mc"][:, j:j + 1], sc8["ms"][:, j:j + 1]
                            p.tt("dve", ctmp[:, 0:1], lr, mc, ALU.mult)
                            p.tt("dve", ctmp[:, 1:2], li, ms_, ALU.mult)
                            p.tt("dve", ctmp[:, 2:3], lr, ms_, ALU.mult)
                            p.tt("dve", ctmp[:, 3:4], li, mc, ALU.mult)
                            p.tt("dve", carry[:, j, 0:1], ctmp[:, 0:1], ctmp[:, 1:2], ALU.subtract)
                            p.tt("dve", carry[:, j, 1:2], ctmp[:, 2:3], ctmp[:, 3:4], ALU.add)
                            p.tt("dve", a_, gr, er, ALU.mult)
                            p.tt("pool", b_, gi, ei, ALU.mult)
                            p.tt("pool", c_, gr, ei, ALU.mult)
                            p.tt("dve", d_, gi, er, ALU.mult)
                            p.tt("dve", hb["re"].k(j)[:, j, :], a_, b_, ALU.subtract)
                            p.stt("dve", hb["im"].k(j)[:, j, :], c_, -1.0, d_, ALU.mult, ALU.subtract)
                        for i in range(2):
                            py = ps[6 + i]
                            for jj in range(4):
                                j = 4 * i + jj
                                p.mm(py[:, 0:TC], C_bf[:, 0, j, :], hb["re"].k(j)[:, j, :], start=(jj == 0), stop=False)
                                p.mm(py[:, 0:TC], C_bf[:, 1, j, :], hb["im"].k(j)[:, j, :], start=False, stop=(jj == 3))
                            ya = yacc.k((i, ch))[:, i, t0_:t0_ + TC]
                            if d == 0:
                                p.copy("act", ya, py[:, 0:TC])
                            else:
                                p.tt("dve", ya, ya, py[:, 0:TC], ALU.add)
                NB = 512
                zb = [p.sbuf("s5z%d" % q, [128, 2, NB], BF16) for q in range(2)]
                zf = [p.sbuf("s5zf%d" % q, [128, 2, NB], F32) for q in range(2)]
                t1 = p.sbuf("s5t1", [128, NB], F32)
                t2 = p.sbuf("s5t2", [128, NB], F32)
                ob = [p.sbuf("s5ob%d" % q, [128, 2, NB], BF16) for q in range(2)]
                ufs = [p.sbuf("s5uf%d" % q, [128, 2, NB], F32) for q in range(2)]
                for bi_, (s_, n, isctx) in enumerate(token_blocks(NB)):
                    q = bi_ % 2
                    u_f = ufs[q]
                    p.dma(u_f[:, :, 0:n], View(pT, pT.h.ap()[0:256, :].rearrange("(k p) t -> p k t", p=128)[:, :, s_:s_ + n], None))
                    for i in range(2):
                        yv = t1[:, 0:n]
                        p.stt("dve", yv, u_f[:, i, 0:n], v_sb[:, i, 0:1], yacc[:, i, s_:s_ + n], ALU.mult, ALU.add)
                        p.tt("pool", t2[:, 0:n], yv, yv, ALU.mult)
                        p.ts("dve", t2[:, 0:n], t2[:, 0:n], 0.044715, ALU.mult, 1.0, ALU.add)
                        p.tt("pool", t2[:, 0:n], t2[:, 0:n], yv, ALU.mult)
                        p.act(t2[:, 0:n], t2[:, 0:n], AF.Sigmoid, scale=1.5957691216)
                        p.tt("dve", zf[q][:, i, 0:n], yv, t2[:, 0:n], ALU.mult)
                        p.copy("pool", zb[q][:, i, 0:n], zf[q][:, i, 0:n])
                    for i in range(2):
                        pg = ps[i]
                        for k in range(2):
                            p.mm(pg[:, 0:n], G_bf[:, k, i * 128:(i + 1) * 128], zb[q][:, k, 0:n], start=(k == 0), stop=(k == 1))
                        p.act(t2[:, 0:n], pg[:, 0:n], AF.Sigmoid, bias=v_sb[:, i, 1:2])
                        p.tt("dve", ob[q][:, i, 0:n], zf[q][:, i, 0:n], t2[:, 0:n], ALU.mult)
                    p.dma(View(yT, yT.h.ap()[0:256, :].rearrange("(k p) t -> p k t", p=128)[:, :, s_:s_ + n], ("s5", s_)), ob[q][:, :, 0:n])

        cm = p.sbuf("cmat_sb", [128, 8, 64], F32)
        p.dma(cm[:], cmat_d[:])
        one_c = p.sbuf("one_c", [128, 1], F32)
        p.memset("dve", one_c[:], 1.0)
        CI, CL, CLS, CU, CUS, CONES, CSEL63, CSEL0 = range(8)
        NCK = T // 64
        BWD_ORDER = [3, 2, 1, 0] + list(range(NCK - 1, 3, -1))

        class PsumSlots:
            def __init__(self, banks, half):
                self.banks = banks
                self.half = half
                self.bi = -1
                self.j = 0

            def group(self):
                self.bi = (self.bi + 1) % len(self.banks)
                self.j = 0

            def get(self, part0=None):
                return self.getn(1)

            def getn(self, n):
                b = self.banks[self.bi]
                j = self.j
                self.j += n
                assert self.j <= 8
                h = self.half
                return ps[b].k(("h", h))[h * 64:(h + 1) * 64, j * 64:(j + n) * 64]

        def neumann(hs, M, Y0ps, W, pslots, part0):
            Ih = cm[hs, CI, :]
            XY = [W("XY0"), W("XY1")]
            TT = [W("T0"), W("T1")]
            p.copy("dve", XY[0][:, 64:128], Y0ps)
            p.tt("dve", TT[0], Ih, Y0ps, ALU.subtract)
            Xc, Yc, Tc = M, XY[0][:, 64:128], TT[0]
            for k in range(1, 6):
                pslots.group()
                nxy = 2 if k < 5 else 1
                pxy = pslots.getn(nxy)
                p.mm(pxy[:, 0:64], Yc, Xc)
                if k < 5:
                    p.mm(pxy[:, 64:128], Xc, Yc)
                XYn = XY[k % 2]
                p.copy("dve", XYn[:, 0:64 * nxy], pxy)
                Xn = XYn[:, 0:64]
                pslots.group()
                pz = pslots.get(part0)
                p.mm(pz, Xn, Tc)
                Tn = TT[k % 2]
                p.tt("dve", Tn, Tc, pz, ALU.add)
                Xc, Tc = Xn, Tn
                if k < 5:
                    Yc = XYn[:, 64:128]
            return Tc

        def gdn_mixer(l):
            with p.scope():
                ba = p.sbuf("g_ba", [128, NCK, 24], F32)
                for hf in range(2):
                    for n0 in range(0, NCK, 17):
                        p.dma(ba.k((hf, n0))[hf * 64:(hf + 1) * 64, n0:n0 + 17, :],
                              View(pBA, pBA.h.ap().rearrange("(n t) r -> t n r", t=64)[:, n0:n0 + 17, :], None))
                sc = p.sbuf("g_sc", [128, 2, 12], F32)
                p.dma(sc[:], gdn_sc[:, l])
                nA = p.sbuf("g_nA", [128, 12], F32)
                p.act(nA[:], sc[:, 0, :], AF.Exp)
                p.ts("dve", nA[:], nA[:], -1.0, ALU.mult)
                if debug.get("gdn_stop") == 0:
                    return
                names = ("beta", "g", "gc", "gl", "egc", "egl", "ed", "nbe")
                tk = {nm: p.sbuf("g_" + nm, [128, NCK, 12], F32) for nm in names}
                bcn = lambda v: v.rearrange("p (o r) -> p o r", o=1).bcast([128, NCK, 12])
                p.act(tk["beta"][:], ba[:, :, 0:12], AF.Sigmoid)
                p.tt("dve", tk["g"][:], ba[:, :, 12:24], bcn(sc[:, 1, :]), ALU.add)
                p.act(tk["g"][:], tk["g"][:], AF.Exp)
                p.act(tk["g"][:], tk["g"][:], AF.Ln, bias=one_c[:, 0:1])
                p.tt("dve", tk["g"][:], tk["g"][:], bcn(nA[:, :]), ALU.mult)
                if debug.get("gdn_stop") == 5:
                    return
                for d in range(2):
                    for hf in range(2):
                        hsl = slice(hf * 64, hf * 64 + 64)
                        tri = cm[hsl, CU if d == 0 else CL, :]
                        sel = cm[hsl, CSEL63 if d == 0 else CSEL0, :]
                        for n0 in range(0, NCK, 34):
                            r3 = lambda v: v.rearrange("p (n r) -> p n r", r=6)
                            pg = ps[0]
                            p.mm(r3(pg[hsl, 0:204]), tri, tk["g"][hsl, n0:n0 + 34, d * 6:(d + 1) * 6])
                            p.copy("dve", tk["gc"][hsl, n0:n0 + 34, d * 6:(d + 1) * 6], r3(pg[hsl, 0:204]))
                            pl = ps[1]
                            p.mm(r3(pl[hsl, 0:204]), sel, tk["gc"][hsl, n0:n0 + 34, d * 6:(d + 1) * 6])
                            p.copy("dve", tk["gl"][hsl, n0:n0 + 34, d * 6:(d + 1) * 6], r3(pl[hsl, 0:204]))
                if debug.get("gdn_stop") == 6:
                    return
                p.ts("dve", tk["egc"][:], tk["gc"][:], -100.0, ALU.max)
                p.act(tk["egc"][:], tk["egc"][:], AF.Exp)
                if debug.get("gdn_stop") == 7:
                    if debug.get("gdn_dump"):
                        p.dma(dbg[:, :], tk[debug["gdn_dump"]][:, :, :].rearrange("p n r -> p (n r)"))
                    return
                p.ts("dve", tk["egl"][:], tk["gl"][:], -100.0, ALU.max)
                p.act(tk["egl"][:], tk["egl"][:], AF.Exp)
                if debug.get("gdn_stop") == 8:
                    return
                p.tt("dve", tk["ed"][:], tk["gl"][:], tk["gc"][:], ALU.subtract)
                p.ts("dve", tk["ed"][:], tk["ed"][:], -100.0, ALU.max)
                p.act(tk["ed"][:], tk["ed"][:], AF.Exp)
                p.tt("dve", tk["nbe"][:], tk["beta"][:], tk["egc"][:], ALU.mult)
                p.ts("dve", tk["nbe"][:], tk["nbe"][:], -1.0, ALU.mult)

                if debug.get("gdn_stop") == 1:
                    return
                cw = p.sbuf("g_cw", [128, 9, 5], F32)
                p.dma(cw[:], gdn_cw[:, l])
                nw = p.sbuf("g_nw", [128, DEPTH], F32)
                p.dma(nw[:], gdn_nw[:])
                ones128 = p.sbuf("g_ones", [128, 128], F32)
                p.memset("dve", ones128[:], 0.0)
                p.memset("dve", ones128[0:64, 0:64], 1.0)
                p.memset("dve", ones128[64:128, 64:128], 1.0)
                eps_g = p.sbuf("g_eps", [128, 1], F32)
                p.memset("dve", eps_g[:], EPS)

                raw = p.sbuf("g_raw", [128, T], F32)
                qkv = [p.sbuf("g_%s" % nm, [128, T], F32) for nm in ("q", "k", "v")]
                oacc = p.sbuf("g_oacc", [128, T], F32)
                tmpn = [p.sbuf("g_tmpn%d" % i, [128, 512], F32) for i in range(2)]
                obs = [p.sbuf("g_ob%d" % i, [128, 512], BF16) for i in range(2)]
                NU = 4
                wk = {nm: [p.sbuf("g_w%s%d" % (nm, u), [128, 64], F32) for u in range(NU)] for nm in
                      ("dg", "Dx", "Di", "Ds", "M", "T0", "T1", "At", "AtT", "bV", "RHS", "vn", "Qg",
                       "Kd", "Kt", "Qt", "S", "dg2")}
                for nm in ("XY0", "XY1"):
                    wk[nm] = [p.sbuf("g_w%s%d" % (nm, u), [128, 128], F32) for u in range(NU)]

                for nm_ in wk:
                    for u in range(NU):
                        p.memset("pool", wk[nm_][u][:, :], 0.0)
                for hp in range(3):
                    for wi in range(3):
                        row0 = 256 + wi * 384 + hp * 128
                        tile_i = wi * 3 + hp
                        dst = qkv[wi]
                        for c in range(0, T, 1088):
                            p.dma(raw.k(c)[:, c:c + 1088], pT[row0:row0 + 128, c:c + 1088])
                        for (a_, b_) in ((0, LC), (LC, T)):
                            p.ts("dve", dst[:, a_:b_], raw[:, a_:b_], cw[:, tile_i, 2:3], ALU.mult)
                            for j in (0, 1, 3, 4):
                                sft = j - 2
                                lo, hi = max(a_, a_ - sft), min(b_, b_ - sft)
                                p.stt("dve", dst[:, lo:hi], raw[:, lo + sft:hi + sft], cw[:, tile_i, j:j + 1], dst[:, lo:hi], ALU.mult, ALU.add)
                        p.act(dst[:, :], dst[:, :], AF.Silu)
                        if wi < 2:
                            for bi_, (s_, n, isctx) in enumerate(token_blocks(512)):
                                tq_ = tmpn[bi_ % 2]
                                p.tt("pool", tq_[:, 0:n], dst[:, s_:s_ + n], dst[:, s_:s_ + n], ALU.mult)
                                pss = ps[2 + bi_ % 2]
                                p.mm(pss[:, 0:n], ones128[:, :], tq_[:, 0:n])
                                p.act(tq_[:, 0:n], pss[:, 0:n], AF.Sqrt, bias=eps_g[:, 0:1])
                                p.recip(tq_[:, 0:n], tq_[:, 0:n])
                                if wi == 0:
                                    p.stt("dve", dst[:, s_:s_ + n], dst[:, s_:s_ + n], 0.125, tq_[:, 0:n], ALU.mult, ALU.mult)
                                else:
                                    p.tt("dve", dst[:, s_:s_ + n], dst[:, s_:s_ + n], tq_[:, 0:n], ALU.mult)
                    if debug.get("gdn_stop") == 2:
                        return
                    qt, kt, vt = qkv
                    p.memset("pool", oacc[:], 0.0)
                    for u in range(NU):
                        p.memset("dve", wk["S"][u][:, :], 0.0)
                    pslots_u = {hh * 2 + d: PsumSlots([0, 1, 2, 3] if d == 0 else [4, 5, 6, 7], hh) for hh in range(2) for d in range(2)}
                    for step in range(debug.get("gdn_steps", NCK)):
                        units = []
                        for hh in range(2):
                            for d in range(2):
                                n = step if d == 0 else BWD_ORDER[step]
                                hs, cs = slice(hh * 64, hh * 64 + 64), slice(n * 64, n * 64 + 64)
                                ci = d * 6 + hp * 2 + hh
                                c = {"ps": pslots_u[hh * 2 + d], "u": hh * 2 + d, "hh": hh, "d": d, "n": n, "hs": hs, "cs": cs, "p0": hh * 64,
                                     "col": (lambda nm, hs=hs, n=n, ci=ci: tk[nm][hs, n, ci:ci + 1]),
                                     "W": (lambda nm, u=hh * 2 + d, hs=hs: wk[nm][u][hs, :])}
                                units.append(c)
                        for c in units:
                            hs, cs, W, col, p0 = c["hs"], c["cs"], c["W"], c["col"], c["p0"]
                            Ih = cm[hs, CI, :]
                            pslots = c["ps"]
                            pslots.group()
                            c["pK"], c["pQ"], c["pV"] = pslots.get(p0), pslots.get(p0), pslots.get(p0)
                            p.tr(c["pK"], kt[hs, cs], Ih)
                            p.tr(c["pQ"], qt[hs, cs], Ih)
                            p.tr(c["pV"], vt[hs, cs], Ih)
                            c["pKK"], c["pQK"] = pslots.get(p0), pslots.get(p0)
                            p.mm(c["pKK"], kt[hs, cs], kt[hs, cs])
                            p.mm(c["pQK"], qt[hs, cs], kt[hs, cs])
                            p.ts("dve", W("dg"), Ih, col("gc"), ALU.mult)
                            c["pG"] = pslots.get(p0)
                            p.mm(c["pG"], cm[hs, CONES, :], W("dg"))
                        if debug.get("gdn_stage", 9) <= 1:
                            continue
                        for c in units:
                            hs, cs, W, col, p0, d = c["hs"], c["cs"], c["W"], c["col"], c["p0"], c["d"]
                            mi, ms_ = (CL, CLS) if d == 0 else (CU, CUS)
                            s2 = debug.get("gdn_s2", 99)
                            e12 = "dve"
                            p.copy(e12, W("Kt"), c["pK"])
                            p.copy(e12, W("Qt"), c["pQ"])
                            if s2 >= 3:
                                p.ts("dve", W("bV"), c["pV"], col("beta"), ALU.mult)
                            if s2 >= 4:
                                p.ts("dve", W("Dx"), c["pG"], col("gc"), ALU.subtract, 0.0, ALU.max)
                            if s2 >= 5:
                                p.act(W("Dx"), W("Dx"), AF.Exp, scale=-1.0)
                            if s2 >= 6:
                                p.tt("pool", W("Di"), W("Dx"), cm[hs, mi, :], ALU.mult)
                                p.tt("pool", W("Ds"), W("Dx"), cm[hs, ms_, :], ALU.mult)
                            if s2 >= 8:
                                p.stt("dve", W("M"), c["pKK"], col("beta"), W("Ds"), ALU.mult, ALU.mult)
                            if s2 >= 9:
                                p.tt("dve", W("At"), c["pQK"], W("Di"), ALU.mult)
                            if s2 >= 10:
                                p.ts("dve", W("dg2"), cm[hs, CI, :], col("egc"), ALU.mult)
                            if s2 >= 11:
                                p.ts("pool", W("Kd"), W("Kt"), col("ed"), ALU.mult)
                        if debug.get("gdn_stage", 9) <= 2:
                            continue
                        for c in units:
                            hs, cs, W, col, p0 = c["hs"], c["cs"], c["W"], c["col"], c["p0"]
                            Ih = cm[hs, CI, :]
                            pslots = c["ps"]
                            pslots.group()
                            c["pY0"], c["pAT"], c["pKS"], c["pQg"] = pslots.get(p0), pslots.get(p0), pslots.get(p0), pslots.get(p0)
                            p.tr(c["pY0"], W("M"), Ih)
                            p.tr(c["pAT"], W("At"), Ih)
                            p.mm(c["pKS"], kt[hs, cs], W("S"))
                            p.mm(c["pQg"], W("Qt"), W("dg2"))
                        if debug.get("gdn_stage", 9) <= 3:
                            continue
                        for c in units:
                            W, col = c["W"], c["col"]
                            p.copy("dve", W("AtT"), c["pAT"])
                            p.stt("dve", W("RHS"), c["pKS"], col("nbe"), W("bV"), ALU.mult, ALU.add)
                            p.copy("dve", W("Qg"), c["pQg"])
                        if debug.get("gdn_stage", 9) <= 4.5:
                            continue
                        for c in units:
                            c["TT"] = neumann(c["hs"], c["W"]("M"), c["pY0"], c["W"], c["ps"], c["p0"])
                        if debug.get("gdn_stage", 9) <= 4:
                            continue
                        for c in units:
                            W = c["W"]
                            pslots = c["ps"]
                            pslots.group()
                            c["pvn"] = pslots.get(c["p0"])
                            p.mm(c["pvn"], c["TT"], W("RHS"))
                            p.copy("dve", W("vn"), c["pvn"])
                        for c in units:
                            hs, cs, W, col, p0, hh, n = c["hs"], c["cs"], c["W"], c["col"], c["p0"], c["hh"], c["n"]
                            pslots = c["ps"]
                            pslots.group()
                            po = pslots.get(p0)
                            p.mm(po, W("S"), W("Qg"), start=True, stop=False)
                            p.mm(po, W("vn"), W("AtT"), start=False, stop=True)
                            p.tt("dve", oacc.k((hh, n))[hs, cs], oacc.k((hh, n))[hs, cs], po, ALU.add)
                            pS = pslots.get(p0)
                            p.mm(pS, W("Kd"), W("vn"))
                            p.stt("dve", W("S"), W("S"), col("egl"), pS, ALU.mult, ALU.add)
                    zt = raw
                    rowz = 256 + 3 * 384 + hp * 128
                    for c_ in range(0, T, 1088):
                        p.dma(zt.k(c_)[:, c_:c_ + 1088], pT[rowz:rowz + 128, c_:c_ + 1088])
                    for bi_, (s_, n, isctx) in enumerate(token_blocks(512)):
                        tq_ = tmpn[bi_ % 2]
                        p.tt("pool", tq_[:, 0:n], oacc[:, s_:s_ + n], oacc[:, s_:s_ + n], ALU.mult)
                        pss = ps[2 + bi_ % 2]
                        p.mm(pss[:, 0:n], ones128[:, :], tq_[:, 0:n])
                        p.act(tq_[:, 0:n], pss[:, 0:n], AF.Sqrt, bias=eps_g[:, 0:1], scale=1.0 / 64.0)
                        p.recip(tq_[:, 0:n], tq_[:, 0:n])
                        p.stt("dve", tq_[:, 0:n], oacc[:, s_:s_ + n], nw[:, l:l + 1], tq_[:, 0:n], ALU.mult, ALU.mult)
                        p.act(zt[:, s_:s_ + n], zt[:, s_:s_ + n], AF.Silu)
                        p.tt("dve", obs[bi_ % 2][:, 0:n], tq_[:, 0:n], zt[:, s_:s_ + n], ALU.mult)
                        p.dma(yT.k(("g", hp, s_))[256 + hp * 128:256 + (hp + 1) * 128, s_:s_ + n], obs[bi_ % 2][:, 0:n])

        def rwkv_mixer(l):
            O_RW = 1816
            SB = 256
            NSB = T // SB
            CPS = SB // 64
            with p.scope():
                mu = p.sbuf("r_mu", [128, 12], F32)
                p.dma(mu[:], rw_mu[:, l])
                vec = p.sbuf("r_vec", [128, 3, 9], F32)
                p.dma(vec[:], rw_vec[:, l])
                ups = p.sbuf("r_ups", [128, 3, 384], F32)
                p.dma(ups[:], rw_up[:, l])
                cmk = p.sbuf("r_cmk", [128, 6], F32)
                p.dma(cmk[:], cmask[:])
                ones128 = p.sbuf("r_ones", [128, 128], F32)
                p.memset("dve", ones128[:], 0.0)
                p.memset("dve", ones128[0:64, 0:64], 1.0)
                p.memset("dve", ones128[64:128, 64:128], 1.0)
                epsr = p.sbuf("r_eps", [128, 2], F32)
                p.memset("dve", epsr[:, 0:1], EPS)
                p.memset("dve", epsr[:, 1:2], 64e-5)
                with p.scope():
                    raw = p.sbuf("r_raw", [128, T], F32)
                    sh = p.sbuf("r_sh", [128, T], F32)
                    g3 = lambda v: v.rearrange("p (r c) -> p r c", c=64)
                    for i in range(12):
                        rows = slice(O_RW + 128 * i, O_RW + 128 * (i + 1))
                        for c in range(0, T, 1088):
                            p.dma(raw.k(c)[:, c:c + 1088], pT[rows, c:c + 1088])
                        p.memset("pool", sh[:, :], 0.0)
                        X, S_ = g3(raw[:, LC:T]), g3(sh[:, LC:T])
                        p.stt("dve", S_[:, :, 1:64], X[:, :, 0:63], cmk[:, 0:1], S_[:, :, 1:64], ALU.mult, ALU.add)
                        p.stt("dve", S_[:, :, 0:63], X[:, :, 1:64], cmk[:, 1:2], S_[:, :, 0:63], ALU.mult, ALU.add)
                        p.stt("dve", S_[:, 1:64, :], X[:, 0:63, :], cmk[:, 2:3], S_[:, 1:64, :], ALU.mult, ALU.add)
                        p.stt("dve", S_[:, 0:63, :], X[:, 1:64, :], cmk[:, 3:4], S_[:, 0:63, :], ALU.mult, ALU.add)
                        p.stt("dve", sh[:, 1:LC], raw[:, 0:LC - 1], cmk[:, 4:5], sh[:, 1:LC], ALU.mult, ALU.add)
                        p.stt("dve", sh[:, 0:LC - 1], raw[:, 1:LC], cmk[:, 5:6], sh[:, 0:LC - 1], ALU.mult, ALU.add)
                        p.tt("pool", sh[:, :], sh[:, :], raw[:, :], ALU.subtract)
                        p.stt("dve", sh[:, :], sh[:, :], mu[:, i:i + 1], raw[:, :], ALU.mult, ALU.add)
                        for c in range(0, T, 1088):
                            p.dma(pT.k(("rw", i, c))[rows, c:c + 1088], sh[:, c:c + 1088])

                rmask = p.sbuf("r_rmask", [128, SB], F32)
                p.memset("dve", rmask[:, :], 1.0)
                p.memset("dve", rmask[:, :].rearrange("p (n t) -> p n t", t=64)[:, :, 0:1], 0.0)
                shared = {nm: p.sbuf("r_" + nm, [128, SB], F32) for nm in ("wdn", "adn", "th")}
                NU = 6
                wk = {nm: [p.sbuf("r_w%s%d" % (nm, u), [128, 64], F32) for u in range(NU)] for nm in
                      ("T0", "T1", "RHS", "Z", "nZ", "S")}
                for nm in ("XY0", "XY1"):
                    wk[nm] = [p.sbuf("r_w%s%d" % (nm, u), [128, 128], F32) for u in range(NU)]
                wk["G1"] = [p.sbuf("r_wG1_%d" % u, [128, 512], F32) for u in range(NU)]
                G1IDX = {"M": 0, "MT": 1, "LakT": 2, "LrkT": 3, "LrbT": 4, "Vt": 5, "Kt": 6, "Bt": 7}
                maskg = p.sbuf("r_maskg", [128, 512], F32)
                for nm_ in wk:
                    for u in range(NU):
                        p.memset("pool", wk[nm_][u][:, :], 0.0)
                pt = {(hp, nm): p.sbuf("r_%s%d" % (nm, hp), [128, SB], F32) for hp in range(3) for nm in
                      ("r", "k", "v", "kk", "lw", "lp", "E", "Ei", "ic", "A", "B", "K", "R", "ob")}
                tmp = p.sbuf("r_tmp", [128, SB], F32)
                colblocks = [(0, 256)]
                bank_sets = {0: [0, 1, 2], 1: [3, 4, 5], 2: [6, 7]}

                def lora_sig(dst, hp, d, which, src, bias_col):
                    dsl = slice(d * 64, d * 64 + 64)
                    for bi_, (c0, nb) in enumerate(colblocks):
                        pz_ = ps[bank_sets[hp][bi_ % len(bank_sets[hp])]]
                        p.mm(pz_[:, 0:nb], ups[dsl, which, hp * 128:(hp + 1) * 128], src[dsl, c0:c0 + nb])
                        p.act(dst[:, c0:c0 + nb], pz_[:, 0:nb], AF.Sigmoid, bias=bias_col)

                def blocksum(dst_fn, src, hp):
                    for bi_, (c0, nb) in enumerate(colblocks):
                        pz_ = ps[bank_sets[hp][bi_ % len(bank_sets[hp])]]
                        p.mm(pz_[:, 0:nb], ones128[:, :], src[:, c0:c0 + nb])
                        dst_fn(pz_[:, 0:nb], c0, nb)

                for d in range(2):
                    if d == 0:
                        groups = [(sb, list(range(sb * CPS, (sb + 1) * CPS))) for sb in range(NSB)]
                    else:
                        groups = [(0, [3, 2, 1, 0])] + [(sb, list(range((sb + 1) * CPS - 1, sb * CPS - 1, -1))) for sb in range(NSB - 1, 0, -1)] \
                                 + [(0, list(range(CPS - 1, 3, -1)))]
                    ordv = (lambda v: v) if d == 0 else rev_view
                    for u in range(NU):
                        p.memset("dve", wk["S"][u][:, :], 0.0)
                    pslots_u = {hp * 2 + hh: PsumSlots(bank_sets[hp], hh) for hp in range(3) for hh in range(2)}
                    ms_, msT, miT = (CLS, CUS, CU) if d == 0 else (CUS, CLS, CL)
                    for i_, mk_ in enumerate((ms_, msT, msT, miT, miT, CONES, CONES, CONES)):
                        p.copy("pool", maskg[:, i_ * 64:(i_ + 1) * 64], cm[:, mk_, :])
                    for (sb, chunks) in groups:
                        if not chunks:
                            continue
                        t0 = sb * SB
                        p.dma(shared["wdn"][:, :], pT[O_RW + 1152:O_RW + 1280, t0:t0 + SB])
                        p.dma(shared["adn"][:, :], pT[O_RW + 1280:O_RW + 1408, t0:t0 + SB])
                        p.act(shared["th"][:, :], shared["wdn"][:, :], AF.Tanh)
                        for hp in range(3):
                            G = lambda nm, hp=hp: pt[(hp, nm)]
                            for wi, nm in enumerate(("r", "k", "v")):
                                row0 = O_RW + wi * 384 + hp * 128
                                p.dma(G(nm)[:, :], pT[row0:row0 + 128, t0:t0 + SB])
                            if d == 1:
                                p.dma(G("ob")[:, :], yrw[hp * 128:(hp + 1) * 128, t0:t0 + SB])
                            lora_sig(G("lw"), hp, d, 0, shared["th"], vec[:, hp, 5 + d:6 + d])
                            p.ts("dve", G("lw")[:, :], G("lw")[:, :], -0.6065306597, ALU.mult)
                            lora_sig(G("ic"), hp, d, 1, shared["adn"], vec[:, hp, 7 + d:8 + d])
                            p.ts("dve", G("kk")[:, :], G("k")[:, :], vec[:, hp, 0:1], ALU.mult)
                            p.tt("pool", tmp[:, :], G("kk")[:, :], G("kk")[:, :], ALU.mult)
                            def kkfin(pv, c0, nb, G=G):
                                p.act(tmp[:, c0:c0 + nb], pv, AF.Sqrt, bias=epsr[:, 0:1])
                            blocksum(kkfin, tmp, hp)
                            p.recip(tmp[:, :], tmp[:, :])
                            p.tt("dve", G("kk")[:, :], G("kk")[:, :], tmp[:, :], ALU.mult)
                            p.scan(ordv(G("lp")[:, :]), ordv(rmask[:, :]) if d == 0 else rmask[:, :], ordv(G("lw")[:, :]), 0.0)
                            p.ts("dve", G("E")[:, :], G("lp")[:, :], 1.0, ALU.mult)
                            p.act(G("E")[:, :], G("E")[:, :], AF.Exp)
                            p.act(G("Ei")[:, :], G("lp")[:, :], AF.Exp, scale=-1.0)
                            p.tt("pool", tmp[:, :], G("lp")[:, :], G("lw")[:, :], ALU.subtract)
                            p.act(tmp[:, :], tmp[:, :], AF.Exp)
                            p.tt("dve", G("A")[:, :], G("kk")[:, :], tmp[:, :], ALU.mult)
                            p.tt("pool", G("R")[:, :], G("r")[:, :], G("E")[:, :], ALU.mult)
                            p.tt("pool", tmp[:, :], G("kk")[:, :], G("ic")[:, :], ALU.mult)
                            p.tt("dve", G("B")[:, :], tmp[:, :], G("Ei")[:, :], ALU.mult)
                            p.ts("dve", tmp[:, :], G("ic")[:, :], -1.0, ALU.add, vec[:, hp, 1:2], ALU.mult)
                            p.ts("dve", tmp[:, :], tmp[:, :], 1.0, ALU.add)
                            p.tt("pool", tmp[:, :], tmp[:, :], G("k")[:, :], ALU.mult)
                            p.tt("dve", G("K")[:, :], tmp[:, :], G("Ei")[:, :], ALU.mult)
                            if d == 0:
                                p.memset("pool", G("ob")[:, :], 0.0)
                        ms_, msT, miT = (CLS, CUS, CU) if d == 0 else (CUS, CLS, CL)
                        for n in chunks:
                            j = n - sb * CPS
                            cs = slice(j * 64, j * 64 + 64)
                            lastcol = j * 64 + (63 if d == 0 else 0)
                            units = []
                            for hp in range(3):
                                for hh in range(2):
                                    hs = slice(hh * 64, hh * 64 + 64)
                                    u = hp * 2 + hh
                                    units.append({"u": u, "hp": hp, "hs": hs, "p0": hh * 64, "ps": pslots_u[u],
                                                  "W": (lambda nm, u=u, hs=hs: (wk["G1"][u][hs, G1IDX[nm] * 64:(G1IDX[nm] + 1) * 64]
                                                                                if nm in G1IDX else wk[nm][u][hs, :])),
                                                  "F": (lambda nm, hp=hp, hs=hs, cs=cs: pt[(hp, nm)][hs, cs])})
                            for c in units:
                                W, F, hs, q = c["W"], c["F"], c["hs"], c["ps"]
                                Ih = cm[hs, CI, :]
                                q.group()
                                c["pg1"] = q.getn(8)
                                g1 = c["pg1"]
                                c["pM"], c["pMT"], c["pLak"], c["pLrk"], c["pLrb"] = (g1[:, i_ * 64:(i_ + 1) * 64] for i_ in range(5))
                                c["pV"], c["pK"], c["pB"] = (g1[:, i_ * 64:(i_ + 1) * 64] for i_ in range(5, 8))
                                p.mm(c["pM"], F("A"), F("B"))
                                p.mm(c["pMT"], F("B"), F("A"))
                                p.mm(c["pLak"], F("K"), F("A"))
                                p.mm(c["pLrk"], F("K"), F("R"))
                                p.mm(c["pLrb"], F("B"), F("R"))
                                p.tr(c["pV"], F("v"), Ih)
                                p.tr(c["pK"], F("K"), Ih)
                                p.tr(c["pB"], F("B"), Ih)
                            for c in units:
                                W, hs = c["W"], c["hs"]
                                p.tt("dve", wk["G1"][c["u"]][hs, :], c["pg1"], maskg[hs, :], ALU.mult)
                            for c in units:
                                W, F, q = c["W"], c["F"], c["ps"]
                                q.group()
                                pr_ = q.get()
                                p.mm(pr_, F("A"), W("S"), start=True, stop=False)
                                p.mm(pr_, W("LakT"), W("Vt"), start=False, stop=True)
                                p.copy("dve", W("RHS"), pr_)
                            for c in units:
                                c["TT"] = neumann(c["hs"], c["W"]("M"), c["W"]("MT"), c["W"], c["ps"], c["p0"])
                            for c in units:
                                W, q = c["W"], c["ps"]
                                q.group()
                                pz_ = q.get()
                                p.mm(pz_, c["TT"], W("RHS"))
                                p.copy("dve", W("Z"), pz_)
                                p.ts("dve", W("nZ"), W("Z"), -1.0, ALU.mult)
                            for c in units:
                                W, F, q, hs, hp = c["W"], c["F"], c["ps"], c["hs"], c["hp"]
                                q.group()
                                po, pS = q.get(), q.get()
                                p.mm(po, W("S"), F("R"), start=True, stop=False)
                                p.mm(po, W("Vt"), W("LrkT"), start=False, stop=False)
                                p.mm(po, W("nZ"), W("LrbT"), start=False, stop=True)
                                p.mm(pS, W("Kt"), W("Vt"), start=True, stop=False)
                                p.mm(pS, W("Bt"), W("nZ"), start=False, stop=True)
                                obv = pt[(hp, "ob")].k((c["u"], n))[hs, cs]
                                p.tt("dve", obv, obv, po, ALU.add)
                                p.tt("dve", W("S"), W("S"), pS, ALU.add)
                                p.ts("dve", W("S"), W("S"), pt[(hp, "E")][hs, lastcol:lastcol + 1], ALU.mult)
                        for hp in range(3):
                            lo, hi = min(chunks) * 64 - t0, (max(chunks) + 1) * 64 - t0
                            p.dma(yrw.k((hp, sb, lo))[hp * 128:(hp + 1) * 128, t0 + lo:t0 + hi], pt[(hp, "ob")][:, lo:hi])

                rwo16 = [p.sbuf("r_o16_%d" % i, [128, SB], BF16) for i in range(2)]
                gdn_t = shared["wdn"]
                sgd = shared["th"]
                for sb in range(NSB):
                    t0 = sb * SB
                    p.dma(gdn_t[:, :], pT[O_RW + 1408:O_RW + 1536, t0:t0 + SB])
                    p.dma(shared["adn"][:, :], pT[O_RW + 1280:O_RW + 1408, t0:t0 + SB])
                    p.act(sgd[:, :], gdn_t[:, :], AF.Sigmoid)
                    for hp in range(3):
                        G = lambda nm, hp=hp: pt[(hp, nm)]
                        for wi, nm in enumerate(("r", "k", "v")):
                            row0 = O_RW + wi * 384 + hp * 128
                            p.dma(G(nm)[:, :], pT[row0:row0 + 128, t0:t0 + SB])
                        y = G("ob")
                        p.dma(y[:, :], yrw[hp * 128:(hp + 1) * 128, t0:t0 + SB])
                        def mfin(pv, c0, nb, y=y, G=G):
                            p.stt("dve", G("A")[:, c0:c0 + nb], pv, -1.0 / 64.0, y[:, c0:c0 + nb], ALU.mult, ALU.add)
                        blocksum(mfin, y, hp)
                        p.tt("pool", tmp[:, :], G("A")[:, :], G("A")[:, :], ALU.mult)
                        def vfin(pv, c0, nb, G=G):
                            p.act(G("B")[:, c0:c0 + nb], pv, AF.Sqrt, bias=epsr[:, 1:2], scale=1.0 / 64.0)
                        blocksum(vfin, tmp, hp)
                        p.recip(G("B")[:, :], G("B")[:, :])
                        p.tt("dve", G("A")[:, :], G("A")[:, :], G("B")[:, :], ALU.mult)
                        p.ts("dve", G("A")[:, :], G("A")[:, :], vec[:, hp, 3:4], ALU.mult, vec[:, hp, 4:5], ALU.add)
                        lora_sig(G("ic"), hp, 0, 1, shared["adn"], vec[:, hp, 7:8])
                        lora_sig(G("E"), hp, 1, 1, shared["adn"], vec[:, hp, 8:9])
                        p.tt("dve", G("ic")[:, :], G("ic")[:, :], G("E")[:, :], ALU.add)
                        p.ts("dve", G("ic")[:, :], G("ic")[:, :], -2.0, ALU.add, vec[:, hp, 1:2], ALU.mult)
                        p.ts("dve", G("ic")[:, :], G("ic")[:, :], 2.0, ALU.add)
                        p.tt("pool", tmp[:, :], G("r")[:, :], G("k")[:, :], ALU.mult)
                        p.stt("dve", tmp[:, :], tmp[:, :], vec[:, hp, 2:3], G("ic")[:, :], ALU.mult, ALU.mult)
                        def bfin(pv, c0, nb, G=G):
                            p.tt("dve", G("K")[:, c0:c0 + nb], pv, G("v")[:, c0:c0 + nb], ALU.mult)
                        blocksum(bfin, tmp, hp)
                        p.tt("dve", G("A")[:, :], G("A")[:, :], G("K")[:, :], ALU.add)
                        ob16 = G("R")
                        for bi_, (c0, nb) in enumerate(colblocks):
                            pz_ = ps[bank_sets[hp][bi_ % len(bank_sets[hp])]]
                            p.mm(pz_[:, 0:nb], ups[:, 2, hp * 128:(hp + 1) * 128], sgd[:, c0:c0 + nb])
                            p.tt("dve", G("B")[:, c0:c0 + nb], G("A")[:, c0:c0 + nb], pz_[:, 0:nb], ALU.mult)
                        o16 = rwo16[hp % 2]
                        p.copy("pool", o16[:, :], G("B")[:, :])
                        p.dma(yT.k(("rw", hp, sb))[640 + hp * 128:640 + (hp + 1) * 128, t0:t0 + SB], o16[:, :])

        for l in range(debug.get("layers", DEPTH)):
            with p.scope():
                win_sb = p.sbuf("win_sb", [128, 8, INC], BF16)
                with p.scope():
                    stg = [p.sbuf("stg_a", [128, 8, 512], F32), p.sbuf("stg_b", [128, 8, 512], F32)]
                    wsrc = w_in.h.ap()[l].rearrange("(k p) n -> p k n", p=128)
                    load_cast(win_sb, lambda c0, c1: wsrc[:, :, c0:c1], INC, w_in, stg)
                xbs = [p.sbuf("xb%d" % i, [128, 8, 512], F32) for i in range(2)]
                xns = [p.sbuf("xn%d" % i, [128, 8, 512], BF16) for i in range(2)]
                sq = p.sbuf("sq", [128, 8, 512], BF16)
                rstd = p.sbuf("rstd", [128, 512], F32)
                evs = [p.sbuf("ev%d" % i, [128, 512], F32) for i in range(4)]
                ntmp = [p.sbuf("ntmp%d" % i, [128, 512], F32) for i in range(2)]
                ei = 0
                for bi, (s, n, isctx) in enumerate(token_blocks(512)):
                    xb, xn = xbs[bi % 2], xns[bi % 2]
                    w = 1 if isctx else 0
                    p.dma(xb[:, :, 0:n], View(xres, xres_v(s, n), None))
                    norm_block(xb, xn, n, coef[:, l, w, 0, :], coef[:, l, w, 1, :], sq, rstd, ps[7], ntmp)
                    for ti, (c0, m) in enumerate(IN_TILES):
                        pst = ps[ti % 6]
                        for k in range(8):
                            p.mm(pst[0:m, 0:n], win_sb[:, k, c0:c0 + m], xn[:, k, 0:n], start=(k == 0), stop=(k == 7))
                        ev = evs[ei % 4]
                        ei += 1
                        p.copy("act" if ti % 2 == 0 else "dve", ev[0:m, 0:n], pst[0:m, 0:n])
                        p.dma(pT.k(("A", bi, ti))[c0:c0 + m, s:s + n], ev[0:m, 0:n])
                    for tq in range(n // 128):
                        pst = ps[6]
                        for k in range(8):
                            p.mm(pst[:, 0:24], xn[:, k, tq * 128:(tq + 1) * 128], win_sb[:, k, 1792:1816], start=(k == 0), stop=(k == 7))
                        ev = evs[ei % 4]
                        ei += 1
                        p.copy("dve", ev[:, 0:24], pst[:, 0:24])
                        p.dma(pBA.k(("A", bi, tq))[s + tq * 128:s + (tq + 1) * 128, :], ev[:, 0:24])


            if debug.get("s5"):
                s5_mixer(l)
            if debug.get("gdn"):
                gdn_mixer(l)
            if debug.get("rwkv"):
                rwkv_mixer(l)
            if debug and debug.get("zero_mix"):
                with p.scope():
                    zb = p.sbuf("zb", [128, 8, 512], BF16)
                    p.memset("pool", zb[:], 0.0)
                    for (s, n, isctx) in token_blocks(512):
                        for (flag, ka, kb) in (("s5", 0, 2), ("gdn", 2, 5), ("rwkv", 5, 8)):
                            if not debug.get(flag):
                                p.dma(View(yT, yT_v(s, n)[:, ka:kb, :], ("z", s, ka)), zb[:, ka:kb, 0:n])

            with p.scope():
                wo_sb = p.sbuf("wo_sb", [128, 8, D], BF16)
                w1_sb = p.sbuf("w1_sb", [128, 8, 4 * D], BF16)
                w2_sb = p.sbuf("w2_sb", [128, 32, D], BF16)
                with p.scope():
                    stg = [p.sbuf("stg_a", [128, 8, 512], F32), p.sbuf("stg_b", [128, 8, 512], F32)]
                    s0 = w_out.h.ap()[l].rearrange("(k p) n -> p k n", p=128)
                    load_cast(wo_sb, lambda c0, c1: s0[:, :, c0:c1], D, w_out, stg)
                    s1 = w1.h.ap()[l].rearrange("(k p) n -> p k n", p=128)
                    load_cast(w1_sb, lambda c0, c1: s1[:, :, c0:c1], 4 * D, w1, stg)
                    s2 = w2.h.ap()[l].rearrange("(k p) n -> p k n", p=128)
                    for kq in range(4):
                        i = 0
                        for c0 in range(0, D, 512):
                            sgb = stg[i % 2]
                            i += 1
                            p.dma(sgb[:], View(w2, s2[:, kq * 8:(kq + 1) * 8, c0:c0 + 512], None))
                            p.copy("dve" if i % 2 else "pool", w2_sb[:, kq * 8:(kq + 1) * 8, c0:c0 + 512], sgb[:])
                NB = 256
                xbs = [p.sbuf("xb%d" % i, [128, 8, NB], F32) for i in range(2)]
                ybs = [p.sbuf("yb%d" % i, [128, 8, NB], BF16) for i in range(2)]
                hn = p.sbuf("hn", [128, 8, NB], BF16)
                sq = p.sbuf("sq", [128, 8, NB], BF16)
                rstd = p.sbuf("rstd", [128, NB], F32)
                hid = p.sbuf("hid", [128, 32, NB], BF16)
                rls = [p.sbuf("rl%d" % i, [128, NB], F32) for i in range(2)]
                ntmp = [p.sbuf("ntmp%d" % i, [128, NB], F32) for i in range(2)]
                last = (l == DEPTH - 1)
                for bi, (s, n, isctx) in enumerate(token_blocks(NB)):
                    if last and isctx:
                        continue
                    w = 1 if isctx else 0
                    xb, yb = xbs[bi % 2], ybs[bi % 2]
                    p.dma(xb[:, :, 0:n], View(xres, xres_v(s, n), None))
                    p.dma(yb[:, :, 0:n], View(yT, yT_v(s, n), None))
                    for dt_ in range(8):
                        pst = ps[dt_ % 4]
                        for k in range(8):
                            p.mm(pst[:, 0:n], wo_sb[:, k, dt_ * 128:(dt_ + 1) * 128], yb[:, k, 0:n], start=(k == 0), stop=(k == 7))
                        p.stt("dve", xb[:, dt_, 0:n], pst[:, 0:n], coef[:, l, w, 2, dt_:dt_ + 1], xb[:, dt_, 0:n], ALU.mult, ALU.add)
                    norm_block(xb, hn, n, coef[:, l, w, 3, :], coef[:, l, w, 4, :], sq, rstd, ps[7], ntmp)
                    for ft in range(32):
                        pst = ps[4 + ft % 3]
                        for k in range(8):
                            p.mm(pst[:, 0:n], w1_sb[:, k, ft * 128:(ft + 1) * 128], hn[:, k, 0:n], start=(k == 0), stop=(k == 7))
                        rl = rls[ft % 2]
                        p.act(rl[:, 0:n], pst[:, 0:n], AF.Relu)
                        p.tt("pool" if ft % 2 == 0 else "dve", hid[:, ft, 0:n], rl[:, 0:n], rl[:, 0:n], ALU.mult)
                    for dt_ in range(8):
                        pst = ps[dt_ % 4]
                        for ft in range(32):
                            p.mm(pst[:, 0:n], w2_sb[:, ft, dt_ * 128:(dt_ + 1) * 128], hid[:, ft, 0:n], start=(ft == 0), stop=(ft == 31))
                        p.stt("dve", xb[:, dt_, 0:n], pst[:, 0:n], coef[:, l, w, 5, dt_:dt_ + 1], xb[:, dt_, 0:n], ALU.mult, ALU.add)
                    if not last:
                        p.dma(View(xres, xres_v(s, n), None), xb[:, :, 0:n])
                    else:
                        for k in range(8):
                            p.act(sq[:, k, 0:n], xb[:, k, 0:n], AF.Square)
                        for k in range(8):
                            p.mm(ps[7][:, 0:n], ones_bf[:], sq[:, k, 0:n], start=(k == 0), stop=(k == 7))
                        rstd_from(rstd[:, 0:n], ps[7][:, 0:n])
                        for k in range(8):
                            p.stt("dve", xb[:, k, 0:n], xb[:, k, 0:n], nrm_sb[:, 4, k:k + 1], rstd[:, 0:n], ALU.mult, ALU.mult)
                        p.dma(View(outT, outT.h.ap().rearrange("(k p) t -> p k t", p=128)[:, :, s - LC:s - LC + n], ("o", s)), xb[:, :, 0:n])

        if debug and debug.get("tap"):
            debug["tap"](p, locals())
        p.wait_all("sp", [outT[:, :]] + ([dbg[:]] if dbg is not None else []))
        p.emit()
    return nc, p


def host_layout(inputs, b):
    f = np.float32
    x = np.asarray(inputs["x"], f)
    ctx = np.asarray(inputs["ctx"], f)
    c = np.asarray(inputs["c"], f)
    c_ctx = np.asarray(inputs["c_ctx"], f)
    fm = lambda v: np.ascontiguousarray(v.reshape(-1, 128).T)
    m = {}
    m["xT"] = np.ascontiguousarray(np.concatenate([ctx[b].T, x[b].T], axis=1))
    m["cT"] = np.ascontiguousarray(np.stack([fm(c[b]), fm(c_ctx)], axis=-1))
    m["mod_w"] = np.asarray(inputs["mod_w"], f)
    mb = np.asarray(inputs["mod_b"], f)
    m["mod_b"] = np.ascontiguousarray(np.stack([fm(mb[l]) for l in range(DEPTH)], axis=1))
    nm = [np.asarray(inputs["norm_mix"], f)[0], np.asarray(inputs["norm_mix"], f)[1],
          np.asarray(inputs["norm_mlp"], f)[0], np.asarray(inputs["norm_mlp"], f)[1], np.asarray(inputs["norm_final"], f)]
    m["nrm"] = np.ascontiguousarray(np.stack([fm(v) for v in nm], axis=1))
    G, P_, Cg = 16, 64, 16
    s5c = np.zeros((DEPTH, 128, 3, 16), f)
    s5B = np.zeros((DEPTH, 128, 2, 8, 128), f)
    s5C = np.zeros((DEPTH, 128, 2, 8, 128), f)
    are, aim, ldt = (np.asarray(inputs[k], f) for k in ("s5_a_re", "s5_a_im", "s5_log_dt"))
    bre, bim, cre, cim = (np.asarray(inputs[k], f) for k in ("s5_b_re", "s5_b_im", "s5_c_re", "s5_c_im"))
    for l in range(DEPTH):
        for d in range(2):
            for j in range(8):
                for gl in range(2):
                    g = 2 * j + gl
                    s5c[l, gl * 64:(gl + 1) * 64, 0, d * 8 + j] = are[l, d, g]
                    s5c[l, gl * 64:(gl + 1) * 64, 1, d * 8 + j] = aim[l, d, g]
                    s5c[l, gl * 64:(gl + 1) * 64, 2, d * 8 + j] = ldt[l, d, g]
        for j in range(8):
            for gl in range(2):
                g = 2 * j + gl
                r0 = 32 * (j % 4) + 16 * gl
                s5B[l, r0:r0 + 16, 0, j, gl * 64:(gl + 1) * 64] = bre[l, g].T
                s5B[l, r0:r0 + 16, 1, j, gl * 64:(gl + 1) * 64] = bim[l, g].T
                s5C[l, gl * 64:(gl + 1) * 64, 0, j, r0:r0 + 16] = cre[l, g].T
                s5C[l, gl * 64:(gl + 1) * 64, 1, j, r0:r0 + 16] = cim[l, g].T
    m["s5c"], m["s5B"], m["s5C"] = s5c, s5B, s5C
    sd, gb = np.asarray(inputs["s5_d"], f), np.asarray(inputs["s5_glu_b"], f)
    m["s5v"] = np.ascontiguousarray(np.stack([np.stack([fm(sd[l]), fm(gb[l])], axis=-1) for l in range(DEPTH)], axis=0))
    m["s5g"] = np.asarray(inputs["s5_glu_w"], f)
    cmat = np.zeros((128, 8, 64), f)
    ii = np.arange(64)
    mats = [np.eye(64), (ii[None, :] <= ii[:, None]), (ii[None, :] < ii[:, None]), (ii[None, :] >= ii[:, None]),
            (ii[None, :] > ii[:, None]), np.ones((64, 64)), np.repeat((ii == 63)[:, None], 64, 1), np.repeat((ii == 0)[:, None], 64, 1)]
    for i_, mt in enumerate(mats):
        cmat[0:64, i_, :] = mt.astype(f)
        cmat[64:128, i_, :] = mt.astype(f)
    m["cmat"] = cmat
    gcv = np.asarray(inputs["gdn_conv"], f)
    m["gdn_cw"] = np.ascontiguousarray(gcv.reshape(DEPTH, 5, 9, 128).transpose(3, 0, 2, 1))
    gal, gdb = np.asarray(inputs["gdn_a_log"], f), np.asarray(inputs["gdn_dt_bias"], f)
    gsc = np.stack([gal.reshape(DEPTH, 12), gdb.reshape(DEPTH, 12)], axis=1)
    m["gdn_sc"] = np.ascontiguousarray(np.broadcast_to(gsc[None], (128, DEPTH, 2, 12)))
    gnw = np.asarray(inputs["gdn_norm"], f)
    m["gdn_nw"] = np.ascontiguousarray(np.concatenate([gnw.T, gnw.T], axis=0))
    g_ = lambda k: np.asarray(inputs[k], f)
    m["rw_mu"] = np.ascontiguousarray(np.stack([fm(g_("rwkv_mu")[l]) for l in range(DEPTH)], axis=1))
    vecs = np.zeros((128, DEPTH, 3, 9), f)
    for l in range(DEPTH):
        srcs = [g_("rwkv_k_k")[l], g_("rwkv_k_a")[l], g_("rwkv_r_k")[l].reshape(-1), g_("rwkv_ln_w")[l], g_("rwkv_ln_b")[l],
                g_("rwkv_w0")[l, 0], g_("rwkv_w0")[l, 1], g_("rwkv_a0")[l, 0], g_("rwkv_a0")[l, 1]]
        for i_, v_ in enumerate(srcs):
            vecs[:, l, :, i_] = fm(v_)
    m["rw_vec"] = vecs
    ups = np.zeros((128, DEPTH, 3, 384), f)
    for l in range(DEPTH):
        ups[:, l, 0] = g_("rwkv_w_up")[l].reshape(128, 384)
        ups[:, l, 1] = g_("rwkv_a_up")[l].reshape(128, 384)
        ups[:, l, 2] = g_("rwkv_g_up")[l]
    m["rw_up"] = ups
    pp = np.arange(128)
    m["cmask"] = np.stack([(pp % 4 == 0), (pp % 4 == 1), (pp % 4 == 2), (pp % 4 == 3), (pp % 2 == 0), (pp % 2 == 1)], axis=1).astype(f)
    m["w_in"] = np.asarray(inputs["w_in"], f)
    m["w_out"] = np.asarray(inputs["w_out"], f)
    m["mlp_w1"] = np.asarray(inputs["mlp_w1"], f)
    m["mlp_w2"] = np.asarray(inputs["mlp_w2"], f)
    return m


def kernel(**inputs):
    nc, _ = build_program()
    in_maps = [host_layout(inputs, core // 2) for core in range(8)]
    res = run_bass_kernel_spmd(nc, in_maps, core_ids=list(range(8)))
    out = np.stack([np.asarray(res.results[2 * b]["outT"]).T for b in range(4)], axis=0)
    return np.ascontiguousarray(out.astype(np.float32))
```

```python
import numpy as np
import concourse.bass as bass
import concourse.mybir as mybir
from concourse.bass_utils import run_bass_kernel_spmd

F32 = mybir.dt.float32
BF16 = mybir.dt.bfloat16
ALU = mybir.AluOpType
AF = mybir.ActivationFunctionType

D = 1024
LC = 256
LL = 4096
T = LC + LL
DEPTH = 2
INC = 3352
EPS = 1e-6
NDMA = 16


class St:
    __slots__ = ("lw", "rd")

    def __init__(self):
        self.lw = None
        self.rd = {}


class Buf:
    def __init__(self, name, handle, dram=False):
        self.name = name
        self.h = handle
        self.dram = dram
        self.reg = {None: St()}

    def _ap(self, idx):
        base = self.h.ap() if self.dram else self.h
        return base[idx]

    def __getitem__(self, idx):
        return View(self, self._ap(idx), None)

    def k(self, key):
        return Keyed(self, key)

    def states(self, key):
        if key is None:
            return list(self.reg.values())
        if key not in self.reg:
            self.reg[key] = St()
        return [self.reg[key], self.reg[None]]


class Keyed:
    def __init__(self, buf, key):
        self.buf = buf
        self.key = key

    def __getitem__(self, idx):
        return View(self.buf, self.buf._ap(idx), self.key)


class View:
    def __init__(self, buf, ap, key):
        self.buf = buf
        self.ap = ap
        self.key = key

    def with_ap(self, ap):
        return View(self.buf, ap, self.key)

    def rearrange(self, pat, **kw):
        return View(self.buf, self.ap.rearrange(pat, **kw), self.key)

    def __getitem__(self, idx):
        return View(self.buf, self.ap[idx], self.key)

    def bcast(self, shape):
        return View(self.buf, self.ap.broadcast_to(shape), self.key)


class Scope:
    def __init__(self, p):
        self.p = p
        self.bufs = []

    def __enter__(self):
        from contextlib import ExitStack
        self.prev_stack = self.p.stack
        self.prev_scope = getattr(self.p, "cur_scope", None)
        self.es = ExitStack()
        self.es.__enter__()
        self.p.stack = self.es
        self.p.cur_scope = self
        return self

    def __exit__(self, *a):
        p = self.p
        freed = dict(getattr(p, "freed", {}))
        for b in self.bufs:
            for st in b.reg.values():
                toks = list(st.rd.items()) + ([st.lw] if st.lw is not None else [])
                for kk, vv in toks:
                    if freed.get(kk, 0) < vv:
                        freed[kk] = vv
        p.freed = freed
        p.stack = self.prev_stack
        p.cur_scope = self.prev_scope
        if self.prev_scope is not None:
            self.prev_scope.bufs.extend(self.bufs)
        self.es.__exit__(None, None, None)
        return False


class Prog:
    ENGS = ("pe", "dve", "act", "pool", "sp")

    def __init__(self, nc, stack):
        self.nc = nc
        self.stack = stack
        self.lists = {e: [] for e in self.ENGS}
        self.cnt = {e: 0 for e in self.ENGS}
        self.seen = {e: {} for e in self.ENGS}
        self.sems = {}
        for e in ("pe", "dve", "act", "pool"):
            self.sems[e] = stack.enter_context(nc.semaphore("s_" + e))
        self.dma_slots = {}
        for q in ("sp", "pool", "act"):
            self.dma_slots[q] = []
            for i in range(NDMA if q == "sp" else 4):
                key = "d_%s_%d" % (q, i)
                self.sems[key] = stack.enter_context(nc.semaphore(key))
                self.dma_slots[q].append([key, 0])
        self.dma_rr = {q: 0 for q in self.dma_slots}
        self.n_instr = 0

    def scope(self):
        return Scope(self)

    def uniq(self, name):
        self.n_names = getattr(self, "n_names", 0) + 1
        return "%s_%d" % (name, self.n_names)

    def sbuf(self, name, shape, dt):
        name = self.uniq(name)
        b = Buf(name, self.stack.enter_context(self.nc.sbuf_tensor(name, shape, dt)))
        b.reg[None].rd = dict(getattr(self, "freed", {}))
        if getattr(self, "cur_scope", None) is not None:
            self.cur_scope.bufs.append(b)
        return b

    def psum(self, name, shape, dt=F32):
        b = Buf(name, self.stack.enter_context(self.nc.psum_tensor(name, shape, dt)))
        b.is_psum = True
        return b

    def dram(self, name, shape, dt, kind="Internal"):
        return Buf(name, self.nc.dram_tensor(name, shape, dt, kind=kind), dram=True)

    def _need(self, eng, reads, writes):
        need = {}

        def add(tok):
            if tok is not None:
                if need.get(tok[0], 0) < tok[1]:
                    need[tok[0]] = tok[1]

        for v in reads:
            for st in v.buf.states(v.key):
                add(st.lw)
        for v in writes:
            for st in v.buf.states(v.key):
                add(st.lw)
                for kk, vv in st.rd.items():
                    add((kk, vv))
        out = []
        for kk, vv in need.items():
            if eng == "pe" and kk == "pe":
                continue
            if self.seen[eng].get(kk, 0) >= vv:
                continue
            self.seen[eng][kk] = vv
            out.append((kk, vv))
        return out

    def _mark(self, tok, reads, writes):
        for v in reads:
            st = v.buf.states(v.key)[0] if v.key is not None else v.buf.reg[None]
            st.rd[tok[0]] = tok[1]
        for v in writes:
            if v.key is None:
                for st in v.buf.reg.values():
                    st.lw = tok
                    st.rd = {}
            else:
                st = v.buf.states(v.key)[0]
                st.lw = tok
                st.rd = {}

    def op(self, eng, fn, reads, writes):
        writes = list(writes) + [v for v in reads if getattr(v.buf, "is_psum", False)]
        waits = self._need(eng, reads, writes)
        self.cnt[eng] += 1
        tok = (eng, self.cnt[eng])
        self.lists[eng].append((waits, fn, (eng, 1)))
        self._mark(tok, reads, writes)
        self.n_instr += 1

    def dma(self, out, in_, q="sp"):
        slots = self.dma_slots[q]
        i = self.dma_rr[q]
        self.dma_rr[q] = (i + 1) % len(slots)
        slot = slots[i]
        waits = self._need(q, [in_], [out])
        if slot[1] > 0 and self.seen[q].get(slot[0], 0) < slot[1]:
            self.seen[q][slot[0]] = slot[1]
            waits.append((slot[0], slot[1]))
        slot[1] += 16
        tok = (slot[0], slot[1])
        o_ap, i_ap = out.ap, in_.ap
        self.lists[q].append((waits, lambda e: e.dma_start(out=o_ap, in_=i_ap), (slot[0], 16)))
        self._mark(tok, [in_], [out])
        self.n_instr += 1

    def wait_all(self, eng, views):
        waits = self._need(eng, views, [])
        self.lists[eng].append((waits, None, None))

    def emit(self):
        nc = self.nc
        sems = self.sems
        with nc.Block() as block:
            def mk(lst):
                def body(e):
                    for waits, fn, inc in lst:
                        for kk, vv in waits:
                            e.wait_ge(sems[kk], vv)
                        if fn is not None:
                            fn(e).then_inc(sems[inc[0]], inc[1])
                return body

            block.tensor(mk(self.lists["pe"]))
            block.vector(mk(self.lists["dve"]))
            block.scalar(mk(self.lists["act"]))
            block.gpsimd(mk(self.lists["pool"]))
            block.sync(mk(self.lists["sp"]))

    def mm(self, out, lhsT, rhs, start=True, stop=True):
        self.op("pe", lambda e: e.matmul(out.ap, lhsT.ap, rhs.ap, start=start, stop=stop), [lhsT, rhs], [out])

    def tr(self, out, in_, ident):
        self.op("pe", lambda e: e.matmul(out.ap, in_.ap, ident.ap, start=True, stop=True), [in_, ident], [out])

    def tt(self, eng, out, a, b, op):
        self.op(eng, lambda e: e.tensor_tensor(out=out.ap, in0=a.ap, in1=b.ap, op=op), [a, b], [out])

    def ts(self, eng, out, a, s1, op0, s2=None, op1=None):
        reads = [a] + [s for s in (s1, s2) if isinstance(s, View)]
        s1a = s1.ap if isinstance(s1, View) else s1
        s2a = s2.ap if isinstance(s2, View) else s2
        if op1 is None:
            self.op(eng, lambda e: e.tensor_scalar(out=out.ap, in0=a.ap, scalar1=s1a, scalar2=None, op0=op0), reads, [out])
        else:
            self.op(eng, lambda e: e.tensor_scalar(out=out.ap, in0=a.ap, scalar1=s1a, scalar2=s2a, op0=op0, op1=op1), reads, [out])

    def stt(self, eng, out, a, s, b, op0, op1):
        reads = [a, b] + ([s] if isinstance(s, View) else [])
        sa = s.ap if isinstance(s, View) else s
        eng = "dve"
        self.op(eng, lambda e: e.scalar_tensor_tensor(out=out.ap, in0=a.ap, scalar=sa, in1=b.ap, op0=op0, op1=op1), reads, [out])

    def act(self, out, a, func, bias=None, scale=None):
        reads = [a] + [s for s in (bias, scale) if isinstance(s, View)]
        kw = {}
        if bias is not None:
            kw["bias"] = bias.ap if isinstance(bias, View) else bias
        if scale is not None:
            kw["scale"] = scale.ap if isinstance(scale, View) else scale
        self.op("act", lambda e: e.activation(out=out.ap, in_=a.ap, func=func, **kw), reads, [out])

    def copy(self, eng, out, a):
        if eng == "act":
            self.op("act", lambda e: e.activation(out=out.ap, in_=a.ap, func=AF.Copy), [a], [out])
        else:
            self.op(eng, lambda e: e.tensor_copy(out=out.ap, in_=a.ap), [a], [out])

    def recip(self, out, a):
        self.op("dve", lambda e: e.reciprocal(out=out.ap, in_=a.ap), [a], [out])

    def memset(self, eng, out, val):
        self.op(eng, lambda e: e.memset(out.ap, val), [], [out])

    def scan(self, out, d0, d1, init, op0=ALU.mult, op1=ALU.add):
        reads = [d0, d1] + ([init] if isinstance(init, View) else [])
        ia = init.ap if isinstance(init, View) else init
        self.op("dve", lambda e: e.tensor_tensor_scan(out=out.ap, data0=d0.ap, data1=d1.ap, initial=ia, op0=op0, op1=op1), reads, [out])


def rev_view(v):
    from concourse.ap import AP
    ap = v.ap
    l = [list(x) for x in ap.ap]
    assert l[-1][0] == 1, l
    off = ap.offset + (l[-1][1] - 1)
    l[-1][0] = -1
    return v.with_ap(AP(ap.tensor, off, l))


def token_blocks(n):
    nc_ = min(n, LC)
    blks = [(i, nc_, True) for i in range(0, LC, nc_)]
    for s in range(LC, T, n):
        blks.append((s, n, False))
    return blks


IN_TILES = [(0, 128), (128, 128)] + [(256 + 128 * i, 128) for i in range(12)] + [(1792, 24)] + \
           [(1816 + 128 * i, 128) for i in range(12)]


def build_program(debug=None):
    from contextlib import ExitStack
    if debug is None:
        debug = {"zero_mix": True, "s5": True, "gdn": True, "rwkv": True}
    nc = bass.Bass("TRN2", target_bir_lowering=False)
    with ExitStack() as stack:
        p = Prog(nc, stack)
        xT = p.dram("xT", [D, T], F32, kind="ExternalInput")
        cT = p.dram("cT", [128, 8, 2], F32, kind="ExternalInput")
        mod_w = p.dram("mod_w", [DEPTH, D, 6 * D], F32, kind="ExternalInput")
        mod_b = p.dram("mod_b", [128, DEPTH, 48], F32, kind="ExternalInput")
        nrm = p.dram("nrm", [128, 5, 8], F32, kind="ExternalInput")
        w_in = p.dram("w_in", [DEPTH, D, INC], F32, kind="ExternalInput")
        w_out = p.dram("w_out", [DEPTH, D, D], F32, kind="ExternalInput")
        w1 = p.dram("mlp_w1", [DEPTH, D, 4 * D], F32, kind="ExternalInput")
        w2 = p.dram("mlp_w2", [DEPTH, 4 * D, D], F32, kind="ExternalInput")
        s5c = p.dram("s5c", [DEPTH, 128, 3, 16], F32, kind="ExternalInput")
        s5B = p.dram("s5B", [DEPTH, 128, 2, 8, 128], F32, kind="ExternalInput")
        s5C = p.dram("s5C", [DEPTH, 128, 2, 8, 128], F32, kind="ExternalInput")
        s5v = p.dram("s5v", [DEPTH, 128, 2, 2], F32, kind="ExternalInput")
        s5g = p.dram("s5g", [DEPTH, 256, 256], F32, kind="ExternalInput")
        cmat_d = p.dram("cmat", [128, 8, 64], F32, kind="ExternalInput")
        gdn_cw = p.dram("gdn_cw", [128, DEPTH, 9, 5], F32, kind="ExternalInput")
        gdn_sc = p.dram("gdn_sc", [128, DEPTH, 2, 12], F32, kind="ExternalInput")
        gdn_nw = p.dram("gdn_nw", [128, DEPTH], F32, kind="ExternalInput")
        pBA = p.dram("pBA", [T, 24], F32)
        rw_mu = p.dram("rw_mu", [128, DEPTH, 12], F32, kind="ExternalInput")
        rw_vec = p.dram("rw_vec", [128, DEPTH, 3, 9], F32, kind="ExternalInput")
        rw_up = p.dram("rw_up", [128, DEPTH, 3, 384], F32, kind="ExternalInput")
        cmask = p.dram("cmask", [128, 6], F32, kind="ExternalInput")
        yrw = p.dram("yrw", [384, T], F32)
        outT = p.dram("outT", [D, LL], F32, kind="ExternalOutput")
        xres = p.dram("xres", [D, T], F32)
        pT = p.dram("pT", [INC, T], F32)
        yT = p.dram("yT", [D, T], BF16)
        dbg = None
        if debug and "shape" in debug:
            dbg = p.dram("dbg", list(debug["shape"]), F32, kind="ExternalOutput")

        ones_bf = p.sbuf("ones_bf", [128, 128], BF16)
        p.memset("dve", ones_bf[:], 1.0)
        modv = p.sbuf("modv", [128, DEPTH, 48, 2], F32)
        modb_sb = p.sbuf("modb_sb", [128, DEPTH, 48], F32)
        nrm_sb = p.sbuf("nrm_sb", [128, 5, 8], F32)
        sc_sb = p.sbuf("sc_sb", [128, 8, 2], F32)
        coef = p.sbuf("coef", [128, DEPTH, 2, 6, 8], F32)
        ps = [p.psum("ps%d" % i, [128, 512]) for i in range(8)]

        p.dma(modb_sb[:], mod_b[:])
        p.dma(nrm_sb[:], nrm[:])
        p.dma(sc_sb[:], cT[:])
        sg = p.sbuf("sg", [128, 8, 2], F32)
        p.act(sg[:], sc_sb[:], AF.Sigmoid)
        p.tt("dve", sc_sb[:], sc_sb[:], sg[:], ALU.mult)

        with p.scope():
            mw = [p.sbuf("mw_a", [128, 8, 512], F32), p.sbuf("mw_b", [128, 8, 512], F32)]
            it = 0
            for l in range(DEPTH):
                src = mod_w.h.ap()[l].rearrange("(k p) n -> p k n", p=128)
                for g in range(12):
                    b = mw[it % 2]
                    it += 1
                    p.dma(b[:], View(mod_w, src[:, :, g * 512:(g + 1) * 512], None))
                    pst = ps[g % 2]
                    for jj in range(4):
                        for k in range(8):
                            p.mm(pst[:, jj * 2:jj * 2 + 2], b[:, k, jj * 128:(jj + 1) * 128], sc_sb[:, k, :],
                                 start=(k == 0), stop=(k == 7))
                    for w in range(2):
                        p.tt("dve", modv[:, l, g * 4:(g + 1) * 4, w],
                             pst[:, 0:8].rearrange("p (j w) -> p j w", w=2)[:, :, w],
                             modb_sb[:, l, g * 4:(g + 1) * 4], ALU.add)
        for l in range(DEPTH):
            for w in range(2):
                mv = lambda j0: modv[:, l, j0:j0 + 8, w]
                p.stt("dve", coef[:, l, w, 0, :], mv(8), 1.0, nrm_sb[:, l, :], ALU.add, ALU.mult)
                p.copy("dve", coef[:, l, w, 1, :], mv(0))
                p.copy("dve", coef[:, l, w, 2, :], mv(16))
                p.stt("dve", coef[:, l, w, 3, :], mv(32), 1.0, nrm_sb[:, 2 + l, :], ALU.add, ALU.mult)
                p.copy("dve", coef[:, l, w, 4, :], mv(24))
                p.copy("dve", coef[:, l, w, 5, :], mv(40))

        for k in range(8):
            p.dma(xres[k * 128:(k + 1) * 128, :], xT[k * 128:(k + 1) * 128, :])

        xres_v = lambda s, n: xres.h.ap().rearrange("(k p) t -> p k t", p=128)[:, :, s:s + n]
        yT_v = lambda s, n: yT.h.ap().rearrange("(k p) t -> p k t", p=128)[:, :, s:s + n]

        def load_cast(dst, src_ap_fn, ncols, src_buf, stg, engs=("dve", "pool")):
            i = 0
            for c0 in range(0, ncols, 512):
                c1 = min(ncols, c0 + 512)
                s = stg[i % 2]
                p.dma(s[:, :, 0:c1 - c0], View(src_buf, src_ap_fn(c0, c1), None))
                p.copy(engs[i % 2], dst[:, :, c0:c1], s[:, :, 0:c1 - c0])
                i += 1

        eps_sb = p.sbuf("eps_sb", [128, 1], F32)
        p.memset("dve", eps_sb[:], EPS)

        def rstd_from(dst, src, scale=1.0 / D, eps=None):
            p.act(dst, src, AF.Sqrt, bias=(eps if eps is not None else eps_sb)[:, 0:1], scale=scale)
            p.recip(dst, dst)

        def norm_block(xb, xn, n, cf_A, cf_sh, sq, rstd, pst, ntmp):
            for k in range(8):
                p.act(sq[:, k, 0:n], xb[:, k, 0:n], AF.Square)
            for k in range(8):
                p.mm(pst[:, 0:n], ones_bf[:], sq[:, k, 0:n], start=(k == 0), stop=(k == 7))
            rstd_from(rstd[:, 0:n], pst[:, 0:n])
            for k in range(8):
                eng = "dve" if k % 2 == 0 else "pool"
                tmp = ntmp[k % 2]
                p.tt(eng, tmp[:, 0:n], xb[:, k, 0:n], rstd[:, 0:n], ALU.mult)
                p.act(xn[:, k, 0:n], tmp[:, 0:n], AF.Identity, bias=cf_sh[:, k:k + 1], scale=cf_A[:, k:k + 1])

        PI = float(np.pi)
        cst = p.sbuf("cst", [128, 4], F32)
        p.memset("dve", cst[:, 0:1], -3.1415915)
        p.memset("dve", cst[:, 1:2], 1.0)

        def sincos(dst_c, dst_s, th, shp, tmpf):
            t_y, t_k, t_m, t_f, t_i = tmpf
            for dst, shift in ((dst_s, 0.5), (dst_c, 0.75)):
                p.ts("dve", t_y, th, 1.0 / (2 * PI), ALU.mult, shift, ALU.add)
                p.copy("dve", t_i, t_y)
                p.copy("dve", t_k, t_i)
                p.tt("dve", t_m, t_k, t_y, ALU.is_gt)
                p.tt("dve", t_k, t_k, t_m, ALU.subtract)
                p.tt("dve", t_f, t_y, t_k, ALU.subtract)
                p.ts("dve", t_f, t_f, 2 * PI, ALU.mult, -PI, ALU.add)
                p.tt("dve", t_m, t_f, t_f, ALU.mult)
                coefs = [(-1.0) ** k / float(np.prod(np.arange(1, 2 * k + 2, dtype=np.float64))) for k in range(10)]
                p.memset("dve", t_k, coefs[9])
                for k in range(8, -1, -1):
                    p.tt("dve", t_k, t_k, t_m, ALU.mult)
                    p.ts("dve", t_k, t_k, coefs[k], ALU.add)
                p.tt("dve", dst, t_k, t_f, ALU.mult)

        S5TC = 256

        def s5_mixer(l):
            TC = S5TC
            NCH = T // TC
            with p.scope():
                I32 = mybir.dt.int32
                c_sb = p.sbuf("s5c_sb", [128, 3, 16], F32)
                p.dma(c_sb[:], s5c[l])
                sm = {nm: p.sbuf("s5_" + nm, [128, 16], F32) for nm in
                      ("dt", "th", "r", "c1", "s1", "x", "y", "den", "cre", "cim", "t0", "t1", "ty", "tk", "tm", "tf", "cT", "sT")}
                ti32 = p.sbuf("s5_ti", [128, 16], I32)
                V = lambda nm: sm[nm][:, :]
                def exp_acc(dst, src):
                    tq, e_ = V("ty"), V("tk")
                    p.ts("dve", tq, src, 1.0 / 16.0, ALU.mult)
                    p.memset("dve", e_, 1.0)
                    for k in range(10, 0, -1):
                        p.tt("dve", e_, e_, tq, ALU.mult)
                        p.ts("dve", e_, e_, 1.0 / k, ALU.mult, 1.0, ALU.add)
                    for _ in range(4):
                        p.tt("dve", e_, e_, e_, ALU.mult)
                    p.copy("dve", dst, e_)
                exp_acc(V("dt"), c_sb[:, 2, :])
                p.tt("dve", V("th"), V("dt"), c_sb[:, 1, :], ALU.mult)
                p.tt("dve", V("t0"), V("dt"), c_sb[:, 0, :], ALU.mult)
                exp_acc(V("r"), V("t0"))
                sincos(V("c1"), V("s1"), V("th"), None, (V("ty"), V("tk"), V("tm"), V("tf"), ti32[:, :]))
                p.tt("dve", V("x"), V("r"), V("c1"), ALU.mult)
                p.ts("dve", V("x"), V("x"), -1.0, ALU.add)
                p.tt("dve", V("y"), V("r"), V("s1"), ALU.mult)
                p.tt("dve", V("den"), c_sb[:, 0, :], c_sb[:, 0, :], ALU.mult)
                p.tt("dve", V("t0"), c_sb[:, 1, :], c_sb[:, 1, :], ALU.mult)
                p.tt("dve", V("den"), V("den"), V("t0"), ALU.add)
                p.recip(V("den"), V("den"))
                p.tt("dve", V("t0"), V("x"), c_sb[:, 0, :], ALU.mult)
                p.tt("dve", V("t1"), V("y"), c_sb[:, 1, :], ALU.mult)
                p.tt("dve", V("t0"), V("t0"), V("t1"), ALU.add)
                p.tt("dve", V("cre"), V("t0"), V("den"), ALU.mult)
                p.tt("dve", V("t0"), V("y"), c_sb[:, 0, :], ALU.mult)
                p.tt("dve", V("t1"), V("x"), c_sb[:, 1, :], ALU.mult)
                p.tt("dve", V("t0"), V("t0"), V("t1"), ALU.subtract)
                p.tt("dve", V("cim"), V("t0"), V("den"), ALU.mult)

                B_bf = p.sbuf("s5B_bf", [128, 2, 8, 128], BF16)
                C_bf = p.sbuf("s5C_bf", [128, 2, 8, 128], BF16)
                G_bf = p.sbuf("s5G_bf", [128, 2, 256], BF16)
                v_sb = p.sbuf("s5v_sb", [128, 2, 2], F32)
                p.dma(v_sb[:], s5v[l])
                with p.scope():
                    stg = p.sbuf("s5stg", [128, 2, 8, 128], F32)
                    p.dma(stg[:], s5B[l])
                    p.copy("dve", B_bf[:], stg[:])
                    stg2 = p.sbuf("s5stg2", [128, 2, 8, 128], F32)
                    p.dma(stg2[:], s5C[l])
                    p.copy("pool", C_bf[:], stg2[:])
                    stg3 = p.sbuf("s5stg3", [128, 2, 256], F32)
                    p.dma(stg3[:], View(s5g, s5g.h.ap()[l].rearrange("(k p) n -> p k n", p=128), None))
                    p.copy("dve", G_bf[:], stg3[:])

                u_b = p.sbuf("s5u_b", [128, 2, T], BF16)
                yacc = p.sbuf("s5yacc", [128, 2, T], F32)
                ustg = [p.sbuf("s5ustg%d" % q, [128, 1088], F32) for q in range(2)]
                uq = 0
                for i in range(2):
                    for c in range(0, T, 1088):
                        st_ = ustg[uq % 2]
                        uq += 1
                        p.dma(st_[:, :], pT[i * 128:(i + 1) * 128, c:c + 1088])
                        p.copy("pool" if i else "act", u_b.k((i, c))[:, i, c:c + 1088], st_[:, :])

                tab = {nm: p.sbuf("s5tab_" + nm, [128, 8, TC], F32) for nm in ("er", "ei", "mr", "mi", "rt")}
                wk = {nm: [p.sbuf("s5w_%s%d" % (nm, q), [128, TC], F32) for q in range(2)] for nm in
                      ("br", "bi", "a", "b", "c", "d", "dr", "di", "gr", "gi")}
                hb = {nm: p.sbuf("s5h_" + nm, [128, 8, TC], BF16) for nm in ("re", "im")}
                tA_buf = p.sbuf("s5tA", [128, 8, TC], F32)
                carry = p.sbuf("s5carry", [128, 8, 2], F32)
                sc8 = {nm: p.sbuf("s5s8_" + nm, [128, 8], F32) for nm in ("mc", "ms", "t0", "t1", "t2")}
                ctmp = p.sbuf("s5ctmp", [128, 4], F32)

                for d in range(2):
                    ds = slice(d * 8, d * 8 + 8)
                    p.memset("dve", tab["er"][:, :, 0:1], 1.0)
                    p.memset("dve", tab["ei"][:, :, 0:1], 0.0)
                    p.copy("dve", sc8["mc"][:, :], sm["c1"][:, ds])
                    p.copy("dve", sc8["ms"][:, :], sm["s1"][:, ds])
                    m = 1
                    while m < TC:
                        bc = lambda v: v[:, :].rearrange("p (j o) -> p j o", o=1).bcast([128, 8, m])
                        er0, ei0 = tab["er"][:, :, 0:m], tab["ei"][:, :, 0:m]
                        er1, ei1 = tab["er"][:, :, m:2 * m], tab["ei"][:, :, m:2 * m]
                        t_a, t_b = tab["mr"][:, :, 0:m], tab["mi"][:, :, 0:m]
                        p.tt("dve", t_a, er0, bc(sc8["mc"]), ALU.mult)
                        p.tt("pool", t_b, ei0, bc(sc8["ms"]), ALU.mult)
                        p.tt("dve", er1, t_a, t_b, ALU.subtract)
                        p.tt("dve", t_a, er0, bc(sc8["ms"]), ALU.mult)
                        p.tt("pool", t_b, ei0, bc(sc8["mc"]), ALU.mult)
                        p.tt("dve", ei1, t_a, t_b, ALU.add)
                        p.tt("dve", sc8["t0"][:, :], sc8["mc"][:, :], sc8["mc"][:, :], ALU.mult)
                        p.tt("dve", sc8["t1"][:, :], sc8["ms"][:, :], sc8["ms"][:, :], ALU.mult)
                        p.tt("dve", sc8["t2"][:, :], sc8["mc"][:, :], sc8["ms"][:, :], ALU.mult)
                        p.tt("dve", sc8["mc"][:, :], sc8["t0"][:, :], sc8["t1"][:, :], ALU.subtract)
                        p.ts("dve", sc8["ms"][:, :], sc8["t2"][:, :], 2.0, ALU.mult)
                        m *= 2
                    bcT = lambda v: v.rearrange("p (j o) -> p j o", o=1).bcast([128, 8, TC])
                    cre_b, cim_b = bcT(sm["cre"][:, ds]), bcT(sm["cim"][:, ds])
                    t_a = tA_buf
                    p.tt("dve", tab["mr"][:], tab["er"][:], cre_b, ALU.mult)
                    p.tt("pool", t_a[:], tab["ei"][:], cim_b, ALU.mult)
                    p.tt("dve", tab["mr"][:], tab["mr"][:], t_a[:], ALU.add)
                    p.tt("dve", tab["mi"][:], tab["er"][:], cim_b, ALU.mult)
                    p.tt("pool", t_a[:], tab["ei"][:], cre_b, ALU.mult)
                    p.tt("dve", tab["mi"][:], tab["mi"][:], t_a[:], ALU.subtract)
                    p.memset("pool", tab["rt"][:], 1.0)
                    p.tt("pool", tab["rt"][:], tab["rt"][:], bcT(sm["r"][:, ds]), ALU.mult)
                    p.memset("dve", carry[:], 0.0)

                    def ord_(v):
                        return v if d == 0 else rev_view(v)

                    chunks = list(range(NCH)) if d == 0 else [0] + list(range(NCH - 1, 0, -1))
                    for ci, ch in enumerate(chunks):
                        t0_ = ch * TC
                        for j in range(8):
                            q = j % 2
                            pr, pi_ = ps[(2 * j) % 6], ps[(2 * j + 1) % 6]
                            p.mm(pr[:, 0:TC], B_bf[:, 0, j, :], u_b[:, j // 4, t0_:t0_ + TC])
                            p.mm(pi_[:, 0:TC], B_bf[:, 1, j, :], u_b[:, j // 4, t0_:t0_ + TC])
                            br, bi = wk["br"][q][:, :], wk["bi"][q][:, :]
                            p.copy("act", br, pr[:, 0:TC])
                            p.copy("act", bi, pi_[:, 0:TC])
                            mr, mi = ord_(tab["mr"][:, j, :]), ord_(tab["mi"][:, j, :])
                            er, ei = ord_(tab["er"][:, j, :]), ord_(tab["ei"][:, j, :])
                            a_, b_, c_, d_ = (wk[nm][q][:, :] for nm in ("a", "b", "c", "d"))
                            dr, di = wk["dr"][q][:, :], wk["di"][q][:, :]
                            gr, gi = wk["gr"][q][:, :], wk["gi"][q][:, :]
                            p.tt("dve", a_, br, mr, ALU.mult)
                            p.tt("pool", b_, bi, mi, ALU.mult)
                            p.tt("pool", c_, br, mi, ALU.mult)
                            p.tt("dve", d_, bi, mr, ALU.mult)
                            p.tt("dve", dr, a_, b_, ALU.subtract)
                            p.tt("pool", di, c_, d_, ALU.add)
                            p.scan(ord_(gr), tab["rt"][:, j, :], ord_(dr), carry[:, j, 0:1])
                            p.scan(ord_(gi), tab["rt"][:, j, :], ord_(di), carry[:, j, 1:2])
                            lr = gr[:, TC - 1:TC] if d == 0 else gr[:, 0:1]
                            li = gi[:, TC - 1:TC] if d == 0 else gi[:, 0:1]
                            mc, ms_ = sc8["mc"][:, j:j + 1], sc8["ms"][:, j:j + 1]
                            p.tt("dve", ctmp[:, 0:1], lr, mc, ALU.mult)
                            p.tt("dve", ctmp[:, 1:2], li, ms_, ALU.mult)
                            p.tt("dve", ctmp[:, 2:3], lr, ms_, ALU.mult)
                            p.tt("dve", ctmp[:, 3:4], li, mc, ALU.mult)
                            p.tt("dve", carry[:, j, 0:1], ctmp[:, 0:1], ctmp[:, 1:2], ALU.subtract)
                            p.tt("dve", carry[:, j, 1:2], ctmp[:, 2:3], ctmp[:, 3:4], ALU.add)
                            p.tt("dve", a_, gr, er, ALU.mult)
                            p.tt("pool", b_, gi, ei, ALU.mult)
                            p.tt("pool", c_, gr, ei, ALU.mult)
                            p.tt("dve", d_, gi, er, ALU.mult)
                            p.tt("dve", hb["re"].k(j)[:, j, :], a_, b_, ALU.subtract)
                            p.stt("dve", hb["im"].k(j)[:, j, :], c_, -1.0, d_, ALU.mult, ALU.subtract)
                        for i in range(2):
                            py = ps[6 + i]
                            for jj in range(4):
                                j = 4 * i + jj
                                p.mm(py[:, 0:TC], C_bf[:, 0, j, :], hb["re"].k(j)[:, j, :], start=(jj == 0), stop=False)
                                p.mm(py[:, 0:TC], C_bf[:, 1, j, :], hb["im"].k(j)[:, j, :], start=False, stop=(jj == 3))
                            ya = yacc.k((i, ch))[:, i, t0_:t0_ + TC]
                            if d == 0:
                                p.copy("act", ya, py[:, 0:TC])
                            else:
                                p.tt("dve", ya, ya, py[:, 0:TC], ALU.add)
                NB = 512
                zb = [p.sbuf("s5z%d" % q, [128, 2, NB], BF16) for q in range(2)]
                zf = [p.sbuf("s5zf%d" % q, [128, 2, NB], F32) for q in range(2)]
                t1 = p.sbuf("s5t1", [128, NB], F32)
                t2 = p.sbuf("s5t2", [128, NB], F32)
                ob = [p.sbuf("s5ob%d" % q, [128, 2, NB], BF16) for q in range(2)]
                ufs = [p.sbuf("s5uf%d" % q, [128, 2, NB], F32) for q in range(2)]
                for bi_, (s_, n, isctx) in enumerate(token_blocks(NB)):
                    q = bi_ % 2
                    u_f = ufs[q]
                    p.dma(u_f[:, :, 0:n], View(pT, pT.h.ap()[0:256, :].rearrange("(k p) t -> p k t", p=128)[:, :, s_:s_ + n], None))
                    for i in range(2):
                        yv = t1[:, 0:n]
                        p.stt("dve", yv, u_f[:, i, 0:n], v_sb[:, i, 0:1], yacc[:, i, s_:s_ + n], ALU.mult, ALU.add)
                        p.tt("pool", t2[:, 0:n], yv, yv, ALU.mult)
                        p.ts("dve", t2[:, 0:n], t2[:, 0:n], 0.044715, ALU.mult, 1.0, ALU.add)
                        p.tt("pool", t2[:, 0:n], t2[:, 0:n], yv, ALU.mult)
                        p.act(t2[:, 0:n], t2[:, 0:n], AF.Sigmoid, scale=1.5957691216)
                        p.tt("dve", zf[q][:, i, 0:n], yv, t2[:, 0:n], ALU.mult)
                        p.copy("pool", zb[q][:, i, 0:n], zf[q][:, i, 0:n])
                    for i in range(2):
                        pg = ps[i]
                        for k in range(2):
                            p.mm(pg[:, 0:n], G_bf[:, k, i * 128:(i + 1) * 128], zb[q][:, k, 0:n], start=(k == 0), stop=(k == 1))
                        p.act(t2[:, 0:n], pg[:, 0:n], AF.Sigmoid, bias=v_sb[:, i, 1:2])
                        p.tt("dve", ob[q][:, i, 0:n], zf[q][:, i, 0:n], t2[:, 0:n], ALU.mult)
                    p.dma(View(yT, yT.h.ap()[0:256, :].rearrange("(k p) t -> p k t", p=128)[:, :, s_:s_ + n], ("s5", s_)), ob[q][:, :, 0:n])

        cm = p.sbuf("cmat_sb", [128, 8, 64], F32)
        p.dma(cm[:], cmat_d[:])
        one_c = p.sbuf("one_c", [128, 1], F32)
        p.memset("dve", one_c[:], 1.0)
        CI, CL, CLS, CU, CUS, CONES, CSEL63, CSEL0 = range(8)
        NCK = T // 64
        BWD_ORDER = [3, 2, 1, 0] + list(range(NCK - 1, 3, -1))

        class PsumSlots:
            def __init__(self, banks, half):
                self.banks = banks
                self.half = half
                self.bi = -1
                self.j = 0

            def group(self):
                self.bi = (self.bi + 1) % len(self.banks)
                self.j = 0

            def get(self, part0=None):
                return self.getn(1)

            def getn(self, n):
                b = self.banks[self.bi]
                j = self.j
                self.j += n
                assert self.j <= 8
                h = self.half
                return ps[b].k(("h", h))[h * 64:(h + 1) * 64, j * 64:(j + n) * 64]

        def neumann_gen(c, hs, M, Y0ps, W, pslots):
            Ih = cm[hs, CI, :]
            XY = [W("XY0"), W("XY1")]
            TT = [W("T0"), W("T1")]
            p.copy("dve", XY[0][:, 64:128], Y0ps)
            p.tt("dve", TT[0], Ih, Y0ps, ALU.subtract)
            Xc, Yc, Tc = M, XY[0][:, 64:128], TT[0]
            for k in range(1, 6):
                pslots.group()
                nxy = 2 if k < 5 else 1
                pxy = pslots.getn(nxy)
                p.mm(pxy[:, 0:64], Yc, Xc)
                if k < 5:
                    p.mm(pxy[:, 64:128], Xc, Yc)
                yield
                XYn = XY[k % 2]
                p.copy("dve", XYn[:, 0:64 * nxy], pxy)
                Xn = XYn[:, 0:64]
                pslots.group()
                pz = pslots.get()
                p.mm(pz, Xn, Tc)
                yield
                Tn = TT[k % 2]
                p.tt("dve", Tn, Tc, pz, ALU.add)
                Xc, Tc = Xn, Tn
                if k < 5:
                    Yc = XYn[:, 64:128]
            c["TT"] = Tc

        def run_interleaved(gens):
            gens = list(gens)
            while gens:
                for g in list(gens):
                    try:
                        next(g)
                    except StopIteration:
                        gens.remove(g)

        def gdn_mixer(l):
            with p.scope():
                ba = p.sbuf("g_ba", [128, NCK, 24], F32)
                for hf in range(2):
                    for n0 in range(0, NCK, 17):
                        p.dma(ba.k((hf, n0))[hf * 64:(hf + 1) * 64, n0:n0 + 17, :],
                              View(pBA, pBA.h.ap().rearrange("(n t) r -> t n r", t=64)[:, n0:n0 + 17, :], None))
                sc = p.sbuf("g_sc", [128, 2, 12], F32)
                p.dma(sc[:], gdn_sc[:, l])
                nA = p.sbuf("g_nA", [128, 12], F32)
                p.act(nA[:], sc[:, 0, :], AF.Exp)
                p.ts("dve", nA[:], nA[:], -1.0, ALU.mult)
                if debug.get("gdn_stop") == 0:
                    return
                names = ("beta", "g", "gc", "gl", "egc", "egl", "ed", "nbe")
                tk = {nm: p.sbuf("g_" + nm, [128, NCK, 12], F32) for nm in names}
                bcn = lambda v: v.rearrange("p (o r) -> p o r", o=1).bcast([128, NCK, 12])
                p.act(tk["beta"][:], ba[:, :, 0:12], AF.Sigmoid)
                p.tt("dve", tk["g"][:], ba[:, :, 12:24], bcn(sc[:, 1, :]), ALU.add)
                p.act(tk["g"][:], tk["g"][:], AF.Exp)
                p.act(tk["g"][:], tk["g"][:], AF.Ln, bias=one_c[:, 0:1])
                p.tt("dve", tk["g"][:], tk["g"][:], bcn(nA[:, :]), ALU.mult)
                if debug.get("gdn_stop") == 5:
                    return
                for d in range(2):
                    for hf in range(2):
                        hsl = slice(hf * 64, hf * 64 + 64)
                        tri = cm[hsl, CU if d == 0 else CL, :]
                        sel = cm[hsl, CSEL63 if d == 0 else CSEL0, :]
                        for n0 in range(0, NCK, 34):
                            r3 = lambda v: v.rearrange("p (n r) -> p n r", r=6)
                            pg = ps[0]
                            p.mm(r3(pg[hsl, 0:204]), tri, tk["g"][hsl, n0:n0 + 34, d * 6:(d + 1) * 6])
                            p.copy("dve", tk["gc"][hsl, n0:n0 + 34, d * 6:(d + 1) * 6], r3(pg[hsl, 0:204]))
                            pl = ps[1]
                            p.mm(r3(pl[hsl, 0:204]), sel, tk["gc"][hsl, n0:n0 + 34, d * 6:(d + 1) * 6])
                            p.copy("dve", tk["gl"][hsl, n0:n0 + 34, d * 6:(d + 1) * 6], r3(pl[hsl, 0:204]))
                if debug.get("gdn_stop") == 6:
                    return
                p.ts("dve", tk["egc"][:], tk["gc"][:], -100.0, ALU.max)
                p.act(tk["egc"][:], tk["egc"][:], AF.Exp)
                if debug.get("gdn_stop") == 7:
                    if debug.get("gdn_dump"):
                        p.dma(dbg[:, :], tk[debug["gdn_dump"]][:, :, :].rearrange("p n r -> p (n r)"))
                    return
                p.ts("dve", tk["egl"][:], tk["gl"][:], -100.0, ALU.max)
                p.act(tk["egl"][:], tk["egl"][:], AF.Exp)
                if debug.get("gdn_stop") == 8:
                    return
                p.tt("dve", tk["ed"][:], tk["gl"][:], tk["gc"][:], ALU.subtract)
                p.ts("dve", tk["ed"][:], tk["ed"][:], -100.0, ALU.max)
                p.act(tk["ed"][:], tk["ed"][:], AF.Exp)
                p.tt("dve", tk["nbe"][:], tk["beta"][:], tk["egc"][:], ALU.mult)
                p.ts("dve", tk["nbe"][:], tk["nbe"][:], -1.0, ALU.mult)

                if debug.get("gdn_stop") == 1:
                    return
                cw = p.sbuf("g_cw", [128, 9, 5], F32)
                p.dma(cw[:], gdn_cw[:, l])
                nw = p.sbuf("g_nw", [128, DEPTH], F32)
                p.dma(nw[:], gdn_nw[:])
                ones128 = p.sbuf("g_ones", [128, 128], F32)
                p.memset("dve", ones128[:], 0.0)
                p.memset("dve", ones128[0:64, 0:64], 1.0)
                p.memset("dve", ones128[64:128, 64:128], 1.0)
                eps_g = p.sbuf("g_eps", [128, 1], F32)
                p.memset("dve", eps_g[:], EPS)

                raw = p.sbuf("g_raw", [128, T], F32)
                qkv = [p.sbuf("g_%s" % nm, [128, T], F32) for nm in ("q", "k", "v")]
                oacc = p.sbuf("g_oacc", [128, T], F32)
                tmpn = [p.sbuf("g_tmpn%d" % i, [128, 512], F32) for i in range(2)]
                obs = [p.sbuf("g_ob%d" % i, [128, 512], BF16) for i in range(2)]
                NU = 4
                wk = {nm: [p.sbuf("g_w%s%d" % (nm, u), [128, 64], F32) for u in range(NU)] for nm in
                      ("dg", "Dx", "Di", "Ds", "M", "T0", "T1", "At", "AtT", "bV", "RHS", "vn", "Qg",
                       "Kd", "Kt", "Qt", "S", "dg2")}
                for nm in ("XY0", "XY1"):
                    wk[nm] = [p.sbuf("g_w%s%d" % (nm, u), [128, 128], F32) for u in range(NU)]

                for nm_ in wk:
                    for u in range(NU):
                        p.memset("pool", wk[nm_][u][:, :], 0.0)
                for hp in range(3):
                    for wi in range(3):
                        row0 = 256 + wi * 384 + hp * 128
                        tile_i = wi * 3 + hp
                        dst = qkv[wi]
                        for c in range(0, T, 1088):
                            p.dma(raw.k(c)[:, c:c + 1088], pT[row0:row0 + 128, c:c + 1088])
                        for (a_, b_) in ((0, LC), (LC, T)):
                            p.ts("dve", dst[:, a_:b_], raw[:, a_:b_], cw[:, tile_i, 2:3], ALU.mult)
                            for j in (0, 1, 3, 4):
                                sft = j - 2
                                lo, hi = max(a_, a_ - sft), min(b_, b_ - sft)
                                p.stt("dve", dst[:, lo:hi], raw[:, lo + sft:hi + sft], cw[:, tile_i, j:j + 1], dst[:, lo:hi], ALU.mult, ALU.add)
                        p.act(dst[:, :], dst[:, :], AF.Silu)
                        if wi < 2:
                            for bi_, (s_, n, isctx) in enumerate(token_blocks(512)):
                                tq_ = tmpn[bi_ % 2]
                                p.tt("pool", tq_[:, 0:n], dst[:, s_:s_ + n], dst[:, s_:s_ + n], ALU.mult)
                                pss = ps[2 + bi_ % 2]
                                p.mm(pss[:, 0:n], ones128[:, :], tq_[:, 0:n])
                                p.act(tq_[:, 0:n], pss[:, 0:n], AF.Sqrt, bias=eps_g[:, 0:1])
                                p.recip(tq_[:, 0:n], tq_[:, 0:n])
                                if wi == 0:
                                    p.stt("dve", dst[:, s_:s_ + n], dst[:, s_:s_ + n], 0.125, tq_[:, 0:n], ALU.mult, ALU.mult)
                                else:
                                    p.tt("dve", dst[:, s_:s_ + n], dst[:, s_:s_ + n], tq_[:, 0:n], ALU.mult)
                    if debug.get("gdn_stop") == 2:
                        return
                    qt, kt, vt = qkv
                    p.memset("pool", oacc[:], 0.0)
                    for u in range(NU):
                        p.memset("dve", wk["S"][u][:, :], 0.0)
                    pslots_u = {hh * 2 + d: PsumSlots([0, 1, 2, 3] if d == 0 else [4, 5, 6, 7], hh) for hh in range(2) for d in range(2)}
                    for step in range(debug.get("gdn_steps", NCK)):
                        units = []
                        for hh in range(2):
                            for d in range(2):
                                n = step if d == 0 else BWD_ORDER[step]
                                hs, cs = slice(hh * 64, hh * 64 + 64), slice(n * 64, n * 64 + 64)
                                ci = d * 6 + hp * 2 + hh
                                c = {"ps": pslots_u[hh * 2 + d], "u": hh * 2 + d, "hh": hh, "d": d, "n": n, "hs": hs, "cs": cs, "p0": hh * 64,
                                     "col": (lambda nm, hs=hs, n=n, ci=ci: tk[nm][hs, n, ci:ci + 1]),
                                     "W": (lambda nm, u=hh * 2 + d, hs=hs: wk[nm][u][hs, :])}
                                units.append(c)
                        for c in units:
                            hs, cs, W, col, p0 = c["hs"], c["cs"], c["W"], c["col"], c["p0"]
                            Ih = cm[hs, CI, :]
                            pslots = c["ps"]
                            pslots.group()
                            c["pK"], c["pQ"], c["pV"] = pslots.get(p0), pslots.get(p0), pslots.get(p0)
                            p.tr(c["pK"], kt[hs, cs], Ih)
                            p.tr(c["pQ"], qt[hs, cs], Ih)
                            p.tr(c["pV"], vt[hs, cs], Ih)
                            c["pKK"], c["pQK"] = pslots.get(p0), pslots.get(p0)
                            p.mm(c["pKK"], kt[hs, cs], kt[hs, cs])
                            p.mm(c["pQK"], qt[hs, cs], kt[hs, cs])
                            p.ts("dve", W("dg"), Ih, col("gc"), ALU.mult)
                            c["pG"] = pslots.get(p0)
                            p.mm(c["pG"], cm[hs, CONES, :], W("dg"))
                        if debug.get("gdn_stage", 9) <= 1:
                            continue
                        for c in units:
                            hs, cs, W, col, p0, d = c["hs"], c["cs"], c["W"], c["col"], c["p0"], c["d"]
                            mi, ms_ = (CL, CLS) if d == 0 else (CU, CUS)
                            s2 = debug.get("gdn_s2", 99)
                            e12 = "dve"
                            p.copy(e12, W("Kt"), c["pK"])
                            p.copy(e12, W("Qt"), c["pQ"])
                            if s2 >= 3:
                                p.ts("dve", W("bV"), c["pV"], col("beta"), ALU.mult)
                            if s2 >= 4:
                                p.ts("dve", W("Dx"), c["pG"], col("gc"), ALU.subtract, 0.0, ALU.max)
                            if s2 >= 5:
                                p.act(W("Dx"), W("Dx"), AF.Exp, scale=-1.0)
                            if s2 >= 6:
                                p.tt("pool", W("Di"), W("Dx"), cm[hs, mi, :], ALU.mult)
                                p.tt("pool", W("Ds"), W("Dx"), cm[hs, ms_, :], ALU.mult)
                            if s2 >= 8:
                                p.stt("dve", W("M"), c["pKK"], col("beta"), W("Ds"), ALU.mult, ALU.mult)
                            if s2 >= 9:
                                p.tt("dve", W("At"), c["pQK"], W("Di"), ALU.mult)
                            if s2 >= 10:
                                p.ts("dve", W("dg2"), cm[hs, CI, :], col("egc"), ALU.mult)
                            if s2 >= 11:
                                p.ts("pool", W("Kd"), W("Kt"), col("ed"), ALU.mult)
                        if debug.get("gdn_stage", 9) <= 2:
                            continue
                        for c in units:
                            hs, cs, W, col, p0 = c["hs"], c["cs"], c["W"], c["col"], c["p0"]
                            Ih = cm[hs, CI, :]
                            pslots = c["ps"]
                            pslots.group()
                            c["pY0"], c["pAT"], c["pKS"], c["pQg"] = pslots.get(p0), pslots.get(p0), pslots.get(p0), pslots.get(p0)
                            p.tr(c["pY0"], W("M"), Ih)
                            p.tr(c["pAT"], W("At"), Ih)
                            p.mm(c["pKS"], kt[hs, cs], W("S"))
                            p.mm(c["pQg"], W("Qt"), W("dg2"))
                        if debug.get("gdn_stage", 9) <= 3:
                            continue
                        for c in units:
                            W, col = c["W"], c["col"]
                            p.copy("dve", W("AtT"), c["pAT"])
                            p.stt("dve", W("RHS"), c["pKS"], col("nbe"), W("bV"), ALU.mult, ALU.add)
                            p.copy("dve", W("Qg"), c["pQg"])
                        if debug.get("gdn_stage", 9) <= 4.5:
                            continue
                        run_interleaved([neumann_gen(c, c["hs"], c["W"]("M"), c["pY0"], c["W"], c["ps"]) for c in units])
                        if debug.get("gdn_stage", 9) <= 4:
                            continue
                        for c in units:
                            W = c["W"]
                            pslots = c["ps"]
                            pslots.group()
                            c["pvn"] = pslots.get(c["p0"])
                            p.mm(c["pvn"], c["TT"], W("RHS"))
                            p.copy("dve", W("vn"), c["pvn"])
                        for c in units:
                            hs, cs, W, col, p0, hh, n = c["hs"], c["cs"], c["W"], c["col"], c["p0"], c["hh"], c["n"]
                            pslots = c["ps"]
                            pslots.group()
                            po = pslots.get(p0)
                            p.mm(po, W("S"), W("Qg"), start=True, stop=False)
                            p.mm(po, W("vn"), W("AtT"), start=False, stop=True)
                            p.tt("dve", oacc.k((hh, n))[hs, cs], oacc.k((hh, n))[hs, cs], po, ALU.add)
                            pS = pslots.get(p0)
                            p.mm(pS, W("Kd"), W("vn"))
                            p.stt("dve", W("S"), W("S"), col("egl"), pS, ALU.mult, ALU.add)
                    zt = raw
                    rowz = 256 + 3 * 384 + hp * 128
                    for c_ in range(0, T, 1088):
                        p.dma(zt.k(c_)[:, c_:c_ + 1088], pT[rowz:rowz + 128, c_:c_ + 1088])
                    for bi_, (s_, n, isctx) in enumerate(token_blocks(512)):
                        tq_ = tmpn[bi_ % 2]
                        p.tt("pool", tq_[:, 0:n], oacc[:, s_:s_ + n], oacc[:, s_:s_ + n], ALU.mult)
                        pss = ps[2 + bi_ % 2]
                        p.mm(pss[:, 0:n], ones128[:, :], tq_[:, 0:n])
                        p.act(tq_[:, 0:n], pss[:, 0:n], AF.Sqrt, bias=eps_g[:, 0:1], scale=1.0 / 64.0)
                        p.recip(tq_[:, 0:n], tq_[:, 0:n])
                        p.stt("dve", tq_[:, 0:n], oacc[:, s_:s_ + n], nw[:, l:l + 1], tq_[:, 0:n], ALU.mult, ALU.mult)
                        p.act(zt[:, s_:s_ + n], zt[:, s_:s_ + n], AF.Silu)
                        p.tt("dve", obs[bi_ % 2][:, 0:n], tq_[:, 0:n], zt[:, s_:s_ + n], ALU.mult)
                        p.dma(yT.k(("g", hp, s_))[256 + hp * 128:256 + (hp + 1) * 128, s_:s_ + n], obs[bi_ % 2][:, 0:n])

        def rwkv_mixer(l):
            O_RW = 1816
            SB = 256
            NSB = T // SB
            CPS = SB // 64
            with p.scope():
                mu = p.sbuf("r_mu", [128, 12], F32)
                p.dma(mu[:], rw_mu[:, l])
                vec = p.sbuf("r_vec", [128, 3, 9], F32)
                p.dma(vec[:], rw_vec[:, l])
                ups = p.sbuf("r_ups", [128, 3, 384], F32)
                p.dma(ups[:], rw_up[:, l])
                cmk = p.sbuf("r_cmk", [128, 6], F32)
                p.dma(cmk[:], cmask[:])
                ones128 = p.sbuf("r_ones", [128, 128], F32)
                p.memset("dve", ones128[:], 0.0)
                p.memset("dve", ones128[0:64, 0:64], 1.0)
                p.memset("dve", ones128[64:128, 64:128], 1.0)
                epsr = p.sbuf("r_eps", [128, 2], F32)
                p.memset("dve", epsr[:, 0:1], EPS)
                p.memset("dve", epsr[:, 1:2], 64e-5)
                with p.scope():
                    raw = p.sbuf("r_raw", [128, T], F32)
                    sh = p.sbuf("r_sh", [128, T], F32)
                    g3 = lambda v: v.rearrange("p (r c) -> p r c", c=64)
                    for i in range(12):
                        rows = slice(O_RW + 128 * i, O_RW + 128 * (i + 1))
                        for c in range(0, T, 1088):
                            p.dma(raw.k(c)[:, c:c + 1088], pT[rows, c:c + 1088])
                        p.memset("pool", sh[:, :], 0.0)
                        X, S_ = g3(raw[:, LC:T]), g3(sh[:, LC:T])
                        p.stt("dve", S_[:, :, 1:64], X[:, :, 0:63], cmk[:, 0:1], S_[:, :, 1:64], ALU.mult, ALU.add)
                        p.stt("dve", S_[:, :, 0:63], X[:, :, 1:64], cmk[:, 1:2], S_[:, :, 0:63], ALU.mult, ALU.add)
                        p.stt("dve", S_[:, 1:64, :], X[:, 0:63, :], cmk[:, 2:3], S_[:, 1:64, :], ALU.mult, ALU.add)
                        p.stt("dve", S_[:, 0:63, :], X[:, 1:64, :], cmk[:, 3:4], S_[:, 0:63, :], ALU.mult, ALU.add)
                        p.stt("dve", sh[:, 1:LC], raw[:, 0:LC - 1], cmk[:, 4:5], sh[:, 1:LC], ALU.mult, ALU.add)
                        p.stt("dve", sh[:, 0:LC - 1], raw[:, 1:LC], cmk[:, 5:6], sh[:, 0:LC - 1], ALU.mult, ALU.add)
                        p.tt("pool", sh[:, :], sh[:, :], raw[:, :], ALU.subtract)
                        p.stt("dve", sh[:, :], sh[:, :], mu[:, i:i + 1], raw[:, :], ALU.mult, ALU.add)
                        for c in range(0, T, 1088):
                            p.dma(pT.k(("rw", i, c))[rows, c:c + 1088], sh[:, c:c + 1088])

                rmask = p.sbuf("r_rmask", [128, SB], F32)
                p.memset("dve", rmask[:, :], 1.0)
                p.memset("dve", rmask[:, :].rearrange("p (n t) -> p n t", t=64)[:, :, 0:1], 0.0)
                shared = {nm: p.sbuf("r_" + nm, [128, SB], F32) for nm in ("wdn", "adn", "th")}
                NU = 6
                wk = {nm: [p.sbuf("r_w%s%d" % (nm, u), [128, 64], F32) for u in range(NU)] for nm in
                      ("T0", "T1", "RHS", "Z", "nZ", "S")}
                for nm in ("XY0", "XY1"):
                    wk[nm] = [p.sbuf("r_w%s%d" % (nm, u), [128, 128], F32) for u in range(NU)]
                wk["G1"] = [p.sbuf("r_wG1_%d" % u, [128, 512], F32) for u in range(NU)]
                G1IDX = {"M": 0, "MT": 1, "LakT": 2, "LrkT": 3, "LrbT": 4, "Vt": 5, "Kt": 6, "Bt": 7}
                maskg = p.sbuf("r_maskg", [128, 512], F32)
                for nm_ in wk:
                    for u in range(NU):
                        p.memset("pool", wk[nm_][u][:, :], 0.0)
                pt = {(hp, nm): p.sbuf("r_%s%d" % (nm, hp), [128, SB], F32) for hp in range(3) for nm in
                      ("r", "k", "v", "kk", "lw", "lp", "E", "Ei", "ic", "A", "B", "K", "R", "ob")}
                tmp = p.sbuf("r_tmp", [128, SB], F32)
                colblocks = [(0, 256)]
                bank_sets = {0: [0, 1, 2], 1: [3, 4, 5], 2: [6, 7]}

                def lora_sig(dst, hp, d, which, src, bias_col):
                    dsl = slice(d * 64, d * 64 + 64)
                    for bi_, (c0, nb) in enumerate(colblocks):
                        pz_ = ps[bank_sets[hp][bi_ % len(bank_sets[hp])]]
                        p.mm(pz_[:, 0:nb], ups[dsl, which, hp * 128:(hp + 1) * 128], src[dsl, c0:c0 + nb])
                        p.act(dst[:, c0:c0 + nb], pz_[:, 0:nb], AF.Sigmoid, bias=bias_col)

                def blocksum(dst_fn, src, hp):
                    for bi_, (c0, nb) in enumerate(colblocks):
                        pz_ = ps[bank_sets[hp][bi_ % len(bank_sets[hp])]]
                        p.mm(pz_[:, 0:nb], ones128[:, :], src[:, c0:c0 + nb])
                        dst_fn(pz_[:, 0:nb], c0, nb)

                for d in range(2):
                    if d == 0:
                        groups = [(sb, list(range(sb * CPS, (sb + 1) * CPS))) for sb in range(NSB)]
                    else:
                        groups = [(0, [3, 2, 1, 0])] + [(sb, list(range((sb + 1) * CPS - 1, sb * CPS - 1, -1))) for sb in range(NSB - 1, 0, -1)] \
                                 + [(0, list(range(CPS - 1, 3, -1)))]
                    ordv = (lambda v: v) if d == 0 else rev_view
                    for u in range(NU):
                        p.memset("dve", wk["S"][u][:, :], 0.0)
                    pslots_u = {hp * 2 + hh: PsumSlots(bank_sets[hp], hh) for hp in range(3) for hh in range(2)}
                    ms_, msT, miT = (CLS, CUS, CU) if d == 0 else (CUS, CLS, CL)
                    for i_, mk_ in enumerate((ms_, msT, msT, miT, miT, CONES, CONES, CONES)):
                        p.copy("pool", maskg[:, i_ * 64:(i_ + 1) * 64], cm[:, mk_, :])
                    for (sb, chunks) in groups:
                        if not chunks:
                            continue
                        t0 = sb * SB
                        p.dma(shared["wdn"][:, :], pT[O_RW + 1152:O_RW + 1280, t0:t0 + SB])
                        p.dma(shared["adn"][:, :], pT[O_RW + 1280:O_RW + 1408, t0:t0 + SB])
                        p.act(shared["th"][:, :], shared["wdn"][:, :], AF.Tanh)
                        for hp in range(3):
                            G = lambda nm, hp=hp: pt[(hp, nm)]
                            for wi, nm in enumerate(("r", "k", "v")):
                                row0 = O_RW + wi * 384 + hp * 128
                                p.dma(G(nm)[:, :], pT[row0:row0 + 128, t0:t0 + SB])
                            if d == 1:
                                p.dma(G("ob")[:, :], yrw[hp * 128:(hp + 1) * 128, t0:t0 + SB])
                            lora_sig(G("lw"), hp, d, 0, shared["th"], vec[:, hp, 5 + d:6 + d])
                            p.ts("dve", G("lw")[:, :], G("lw")[:, :], -0.6065306597, ALU.mult)
                            lora_sig(G("ic"), hp, d, 1, shared["adn"], vec[:, hp, 7 + d:8 + d])
                            p.ts("dve", G("kk")[:, :], G("k")[:, :], vec[:, hp, 0:1], ALU.mult)
                            p.tt("pool", tmp[:, :], G("kk")[:, :], G("kk")[:, :], ALU.mult)
                            def kkfin(pv, c0, nb, G=G):
                                p.act(tmp[:, c0:c0 + nb], pv, AF.Sqrt, bias=epsr[:, 0:1])
                            blocksum(kkfin, tmp, hp)
                            p.recip(tmp[:, :], tmp[:, :])
                            p.tt("dve", G("kk")[:, :], G("kk")[:, :], tmp[:, :], ALU.mult)
                            p.scan(ordv(G("lp")[:, :]), ordv(rmask[:, :]) if d == 0 else rmask[:, :], ordv(G("lw")[:, :]), 0.0)
                            p.ts("dve", G("E")[:, :], G("lp")[:, :], 1.0, ALU.mult)
                            p.act(G("E")[:, :], G("E")[:, :], AF.Exp)
                            p.act(G("Ei")[:, :], G("lp")[:, :], AF.Exp, scale=-1.0)
                            p.tt("pool", tmp[:, :], G("lp")[:, :], G("lw")[:, :], ALU.subtract)
                            p.act(tmp[:, :], tmp[:, :], AF.Exp)
                            p.tt("dve", G("A")[:, :], G("kk")[:, :], tmp[:, :], ALU.mult)
                            p.tt("pool", G("R")[:, :], G("r")[:, :], G("E")[:, :], ALU.mult)
                            p.tt("pool", tmp[:, :], G("kk")[:, :], G("ic")[:, :], ALU.mult)
                            p.tt("dve", G("B")[:, :], tmp[:, :], G("Ei")[:, :], ALU.mult)
                            p.ts("dve", tmp[:, :], G("ic")[:, :], -1.0, ALU.add, vec[:, hp, 1:2], ALU.mult)
                            p.ts("dve", tmp[:, :], tmp[:, :], 1.0, ALU.add)
                            p.tt("pool", tmp[:, :], tmp[:, :], G("k")[:, :], ALU.mult)
                            p.tt("dve", G("K")[:, :], tmp[:, :], G("Ei")[:, :], ALU.mult)
                            if d == 0:
                                p.memset("pool", G("ob")[:, :], 0.0)
                        ms_, msT, miT = (CLS, CUS, CU) if d == 0 else (CUS, CLS, CL)
                        for n in chunks:
                            j = n - sb * CPS
                            cs = slice(j * 64, j * 64 + 64)
                            lastcol = j * 64 + (63 if d == 0 else 0)
                            units = []
                            for hp in range(3):
                                for hh in range(2):
                                    hs = slice(hh * 64, hh * 64 + 64)
                                    u = hp * 2 + hh
                                    units.append({"u": u, "hp": hp, "hs": hs, "p0": hh * 64, "ps": pslots_u[u],
                                                  "W": (lambda nm, u=u, hs=hs: (wk["G1"][u][hs, G1IDX[nm] * 64:(G1IDX[nm] + 1) * 64]
                                                                                if nm in G1IDX else wk[nm][u][hs, :])),
                                                  "F": (lambda nm, hp=hp, hs=hs, cs=cs: pt[(hp, nm)][hs, cs])})
                            for c in units:
                                W, F, hs, q = c["W"], c["F"], c["hs"], c["ps"]
                                Ih = cm[hs, CI, :]
                                q.group()
                                c["pg1"] = q.getn(8)
                                g1 = c["pg1"]
                                c["pM"], c["pMT"], c["pLak"], c["pLrk"], c["pLrb"] = (g1[:, i_ * 64:(i_ + 1) * 64] for i_ in range(5))
                                c["pV"], c["pK"], c["pB"] = (g1[:, i_ * 64:(i_ + 1) * 64] for i_ in range(5, 8))
                                p.mm(c["pM"], F("A"), F("B"))
                                p.mm(c["pMT"], F("B"), F("A"))
                                p.mm(c["pLak"], F("K"), F("A"))
                                p.mm(c["pLrk"], F("K"), F("R"))
                                p.mm(c["pLrb"], F("B"), F("R"))
                                p.tr(c["pV"], F("v"), Ih)
                                p.tr(c["pK"], F("K"), Ih)
                                p.tr(c["pB"], F("B"), Ih)
                            for c in units:
                                W, hs = c["W"], c["hs"]
                                p.tt("dve", wk["G1"][c["u"]][hs, :], c["pg1"], maskg[hs, :], ALU.mult)
                            for c in units:
                                W, F, q = c["W"], c["F"], c["ps"]
                                q.group()
                                pr_ = q.get()
                                p.mm(pr_, F("A"), W("S"), start=True, stop=False)
                                p.mm(pr_, W("LakT"), W("Vt"), start=False, stop=True)
                                p.copy("dve", W("RHS"), pr_)
                            run_interleaved([neumann_gen(c, c["hs"], c["W"]("M"), c["W"]("MT"), c["W"], c["ps"]) for c in units])
                            for c in units:
                                W, q = c["W"], c["ps"]
                                q.group()
                                pz_ = q.get()
                                p.mm(pz_, c["TT"], W("RHS"))
                                p.copy("dve", W("Z"), pz_)
                                p.ts("dve", W("nZ"), W("Z"), -1.0, ALU.mult)
                            for c in units:
                                W, F, q, hs, hp = c["W"], c["F"], c["ps"], c["hs"], c["hp"]
                                q.group()
                                po, pS = q.get(), q.get()
                                p.mm(po, W("S"), F("R"), start=True, stop=False)
                                p.mm(po, W("Vt"), W("LrkT"), start=False, stop=False)
                                p.mm(po, W("nZ"), W("LrbT"), start=False, stop=True)
                                p.mm(pS, W("Kt"), W("Vt"), start=True, stop=False)
                                p.mm(pS, W("Bt"), W("nZ"), start=False, stop=True)
                                obv = pt[(hp, "ob")].k((c["u"], n))[hs, cs]
                                p.tt("dve", obv, obv, po, ALU.add)
                                p.tt("dve", W("S"), W("S"), pS, ALU.add)
                                p.ts("dve", W("S"), W("S"), pt[(hp, "E")][hs, lastcol:lastcol + 1], ALU.mult)
                        for hp in range(3):
                            lo, hi = min(chunks) * 64 - t0, (max(chunks) + 1) * 64 - t0
                            p.dma(yrw.k((hp, sb, lo))[hp * 128:(hp + 1) * 128, t0 + lo:t0 + hi], pt[(hp, "ob")][:, lo:hi])

                rwo16 = [p.sbuf("r_o16_%d" % i, [128, SB], BF16) for i in range(2)]
                gdn_t = shared["wdn"]
                sgd = shared["th"]
                for sb in range(NSB):
                    t0 = sb * SB
                    p.dma(gdn_t[:, :], pT[O_RW + 1408:O_RW + 1536, t0:t0 + SB])
                    p.dma(shared["adn"][:, :], pT[O_RW + 1280:O_RW + 1408, t0:t0 + SB])
                    p.act(sgd[:, :], gdn_t[:, :], AF.Sigmoid)
                    for hp in range(3):
                        G = lambda nm, hp=hp: pt[(hp, nm)]
                        for wi, nm in enumerate(("r", "k", "v")):
                            row0 = O_RW + wi * 384 + hp * 128
                            p.dma(G(nm)[:, :], pT[row0:row0 + 128, t0:t0 + SB])
                        y = G("ob")
                        p.dma(y[:, :], yrw[hp * 128:(hp + 1) * 128, t0:t0 + SB])
                        def mfin(pv, c0, nb, y=y, G=G):
                            p.stt("dve", G("A")[:, c0:c0 + nb], pv, -1.0 / 64.0, y[:, c0:c0 + nb], ALU.mult, ALU.add)
                        blocksum(mfin, y, hp)
                        p.tt("pool", tmp[:, :], G("A")[:, :], G("A")[:, :], ALU.mult)
                        def vfin(pv, c0, nb, G=G):
                            p.act(G("B")[:, c0:c0 + nb], pv, AF.Sqrt, bias=epsr[:, 1:2], scale=1.0 / 64.0)
                        blocksum(vfin, tmp, hp)
                        p.recip(G("B")[:, :], G("B")[:, :])
                        p.tt("dve", G("A")[:, :], G("A")[:, :], G("B")[:, :], ALU.mult)
                        p.ts("dve", G("A")[:, :], G("A")[:, :], vec[:, hp, 3:4], ALU.mult, vec[:, hp, 4:5], ALU.add)
                        lora_sig(G("ic"), hp, 0, 1, shared["adn"], vec[:, hp, 7:8])
                        lora_sig(G("E"), hp, 1, 1, shared["adn"], vec[:, hp, 8:9])
                        p.tt("dve", G("ic")[:, :], G("ic")[:, :], G("E")[:, :], ALU.add)
                        p.ts("dve", G("ic")[:, :], G("ic")[:, :], -2.0, ALU.add, vec[:, hp, 1:2], ALU.mult)
                        p.ts("dve", G("ic")[:, :], G("ic")[:, :], 2.0, ALU.add)
                        p.tt("pool", tmp[:, :], G("r")[:, :], G("k")[:, :], ALU.mult)
                        p.stt("dve", tmp[:, :], tmp[:, :], vec[:, hp, 2:3], G("ic")[:, :], ALU.mult, ALU.mult)
                        def bfin(pv, c0, nb, G=G):
                            p.tt("dve", G("K")[:, c0:c0 + nb], pv, G("v")[:, c0:c0 + nb], ALU.mult)
                        blocksum(bfin, tmp, hp)
                        p.tt("dve", G("A")[:, :], G("A")[:, :], G("K")[:, :], ALU.add)
                        ob16 = G("R")
                        for bi_, (c0, nb) in enumerate(colblocks):
                            pz_ = ps[bank_sets[hp][bi_ % len(bank_sets[hp])]]
                            p.mm(pz_[:, 0:nb], ups[:, 2, hp * 128:(hp + 1) * 128], sgd[:, c0:c0 + nb])
                            p.tt("dve", G("B")[:, c0:c0 + nb], G("A")[:, c0:c0 + nb], pz_[:, 0:nb], ALU.mult)
                        o16 = rwo16[hp % 2]
                        p.copy("pool", o16[:, :], G("B")[:, :])
                        p.dma(yT.k(("rw", hp, sb))[640 + hp * 128:640 + (hp + 1) * 128, t0:t0 + SB], o16[:, :])

        for l in range(debug.get("layers", DEPTH)):
            with p.scope():
                win_sb = p.sbuf("win_sb", [128, 8, INC], BF16)
                with p.scope():
                    stg = [p.sbuf("stg_a", [128, 8, 512], F32), p.sbuf("stg_b", [128, 8, 512], F32)]
                    wsrc = w_in.h.ap()[l].rearrange("(k p) n -> p k n", p=128)
                    load_cast(win_sb, lambda c0, c1: wsrc[:, :, c0:c1], INC, w_in, stg)
                xbs = [p.sbuf("xb%d" % i, [128, 8, 512], F32) for i in range(2)]
                xns = [p.sbuf("xn%d" % i, [128, 8, 512], BF16) for i in range(2)]
                sq = p.sbuf("sq", [128, 8, 512], BF16)
                rstd = p.sbuf("rstd", [128, 512], F32)
                evs = [p.sbuf("ev%d" % i, [128, 512], F32) for i in range(4)]
                ntmp = [p.sbuf("ntmp%d" % i, [128, 512], F32) for i in range(2)]
                ei = 0
                for bi, (s, n, isctx) in enumerate(token_blocks(512)):
                    xb, xn = xbs[bi % 2], xns[bi % 2]
                    w = 1 if isctx else 0
                    p.dma(xb[:, :, 0:n], View(xres, xres_v(s, n), None))
                    norm_block(xb, xn, n, coef[:, l, w, 0, :], coef[:, l, w, 1, :], sq, rstd, ps[7], ntmp)
                    for ti, (c0, m) in enumerate(IN_TILES):
                        pst = ps[ti % 6]
                        for k in range(8):
                            p.mm(pst[0:m, 0:n], win_sb[:, k, c0:c0 + m], xn[:, k, 0:n], start=(k == 0), stop=(k == 7))
                        ev = evs[ei % 4]
                        ei += 1
                        p.copy("act" if ti % 2 == 0 else "dve", ev[0:m, 0:n], pst[0:m, 0:n])
                        p.dma(pT.k(("A", bi, ti))[c0:c0 + m, s:s + n], ev[0:m, 0:n])
                    for tq in range(n // 128):
                        pst = ps[6]
                        for k in range(8):
                            p.mm(pst[:, 0:24], xn[:, k, tq * 128:(tq + 1) * 128], win_sb[:, k, 1792:1816], start=(k == 0), stop=(k == 7))
                        ev = evs[ei % 4]
                        ei += 1
                        p.copy("dve", ev[:, 0:24], pst[:, 0:24])
                        p.dma(pBA.k(("A", bi, tq))[s + tq * 128:s + (tq + 1) * 128, :], ev[:, 0:24])


            if debug.get("s5"):
                s5_mixer(l)
            if debug.get("gdn"):
                gdn_mixer(l)
            if debug.get("rwkv"):
                rwkv_mixer(l)
            if debug and debug.get("zero_mix"):
                with p.scope():
                    zb = p.sbuf("zb", [128, 8, 512], BF16)
                    p.memset("pool", zb[:], 0.0)
                    for (s, n, isctx) in token_blocks(512):
                        for (flag, ka, kb) in (("s5", 0, 2), ("gdn", 2, 5), ("rwkv", 5, 8)):
                            if not debug.get(flag):
                                p.dma(View(yT, yT_v(s, n)[:, ka:kb, :], ("z", s, ka)), zb[:, ka:kb, 0:n])

            with p.scope():
                wo_sb = p.sbuf("wo_sb", [128, 8, D], BF16)
                w1_sb = p.sbuf("w1_sb", [128, 8, 4 * D], BF16)
                w2_sb = p.sbuf("w2_sb", [128, 32, D], BF16)
                with p.scope():
                    stg = [p.sbuf("stg_a", [128, 8, 512], F32), p.sbuf("stg_b", [128, 8, 512], F32)]
                    s0 = w_out.h.ap()[l].rearrange("(k p) n -> p k n", p=128)
                    load_cast(wo_sb, lambda c0, c1: s0[:, :, c0:c1], D, w_out, stg)
                    s1 = w1.h.ap()[l].rearrange("(k p) n -> p k n", p=128)
                    load_cast(w1_sb, lambda c0, c1: s1[:, :, c0:c1], 4 * D, w1, stg)
                    s2 = w2.h.ap()[l].rearrange("(k p) n -> p k n", p=128)
                    for kq in range(4):
                        i = 0
                        for c0 in range(0, D, 512):
                            sgb = stg[i % 2]
                            i += 1
                            p.dma(sgb[:], View(w2, s2[:, kq * 8:(kq + 1) * 8, c0:c0 + 512], None))
                            p.copy("dve" if i % 2 else "pool", w2_sb[:, kq * 8:(kq + 1) * 8, c0:c0 + 512], sgb[:])
                NB = 256
                xbs = [p.sbuf("xb%d" % i, [128, 8, NB], F32) for i in range(2)]
                ybs = [p.sbuf("yb%d" % i, [128, 8, NB], BF16) for i in range(2)]
                hn = p.sbuf("hn", [128, 8, NB], BF16)
                sq = p.sbuf("sq", [128, 8, NB], BF16)
                rstd = p.sbuf("rstd", [128, NB], F32)
                hid = p.sbuf("hid", [128, 32, NB], BF16)
                rls = [p.sbuf("rl%d" % i, [128, NB], F32) for i in range(2)]
                ntmp = [p.sbuf("ntmp%d" % i, [128, NB], F32) for i in range(2)]
                last = (l == DEPTH - 1)
                for bi, (s, n, isctx) in enumerate(token_blocks(NB)):
                    if last and isctx:
                        continue
                    w = 1 if isctx else 0
                    xb, yb = xbs[bi % 2], ybs[bi % 2]
                    p.dma(xb[:, :, 0:n], View(xres, xres_v(s, n), None))
                    p.dma(yb[:, :, 0:n], View(yT, yT_v(s, n), None))
                    for dt_ in range(8):
                        pst = ps[dt_ % 4]
                        for k in range(8):
                            p.mm(pst[:, 0:n], wo_sb[:, k, dt_ * 128:(dt_ + 1) * 128], yb[:, k, 0:n], start=(k == 0), stop=(k == 7))
                        p.stt("dve", xb[:, dt_, 0:n], pst[:, 0:n], coef[:, l, w, 2, dt_:dt_ + 1], xb[:, dt_, 0:n], ALU.mult, ALU.add)
                    norm_block(xb, hn, n, coef[:, l, w, 3, :], coef[:, l, w, 4, :], sq, rstd, ps[7], ntmp)
                    for ft in range(32):
                        pst = ps[4 + ft % 3]
                        for k in range(8):
                            p.mm(pst[:, 0:n], w1_sb[:, k, ft * 128:(ft + 1) * 128], hn[:, k, 0:n], start=(k == 0), stop=(k == 7))
                        rl = rls[ft % 2]
                        p.act(rl[:, 0:n], pst[:, 0:n], AF.Relu)
                        p.tt("pool" if ft % 2 == 0 else "dve", hid[:, ft, 0:n], rl[:, 0:n], rl[:, 0:n], ALU.mult)
                    for dt_ in range(8):
                        pst = ps[dt_ % 4]
                        for ft in range(32):
                            p.mm(pst[:, 0:n], w2_sb[:, ft, dt_ * 128:(dt_ + 1) * 128], hid[:, ft, 0:n], start=(ft == 0), stop=(ft == 31))
                        p.stt("dve", xb[:, dt_, 0:n], pst[:, 0:n], coef[:, l, w, 5, dt_:dt_ + 1], xb[:, dt_, 0:n], ALU.mult, ALU.add)
                    if not last:
                        p.dma(View(xres, xres_v(s, n), None), xb[:, :, 0:n])
                    else:
                        for k in range(8):
                            p.act(sq[:, k, 0:n], xb[:, k, 0:n], AF.Square)
                        for k in range(8):
                            p.mm(ps[7][:, 0:n], ones_bf[:], sq[:, k, 0:n], start=(k == 0), stop=(k == 7))
                        rstd_from(rstd[:, 0:n], ps[7][:, 0:n])
                        for k in range(8):
                            p.stt("dve", xb[:, k, 0:n], xb[:, k, 0:n], nrm_sb[:, 4, k:k + 1], rstd[:, 0:n], ALU.mult, ALU.mult)
                        p.dma(View(outT, outT.h.ap().rearrange("(k p) t -> p k t", p=128)[:, :, s - LC:s - LC + n], ("o", s)), xb[:, :, 0:n])

        if debug and debug.get("tap"):
            debug["tap"](p, locals())
        p.wait_all("sp", [outT[:, :]] + ([dbg[:]] if dbg is not None else []))
        p.emit()
    return nc, p


def host_layout(inputs, b):
    f = np.float32
    x = np.asarray(inputs["x"], f)
    ctx = np.asarray(inputs["ctx"], f)
    c = np.asarray(inputs["c"], f)
    c_ctx = np.asarray(inputs["c_ctx"], f)
    fm = lambda v: np.ascontiguousarray(v.reshape(-1, 128).T)
    m = {}
    m["xT"] = np.ascontiguousarray(np.concatenate([ctx[b].T, x[b].T], axis=1))
    m["cT"] = np.ascontiguousarray(np.stack([fm(c[b]), fm(c_ctx)], axis=-1))
    m["mod_w"] = np.asarray(inputs["mod_w"], f)
    mb = np.asarray(inputs["mod_b"], f)
    m["mod_b"] = np.ascontiguousarray(np.stack([fm(mb[l]) for l in range(DEPTH)], axis=1))
    nm = [np.asarray(inputs["norm_mix"], f)[0], np.asarray(inputs["norm_mix"], f)[1],
          np.asarray(inputs["norm_mlp"], f)[0], np.asarray(inputs["norm_mlp"], f)[1], np.asarray(inputs["norm_final"], f)]
    m["nrm"] = np.ascontiguousarray(np.stack([fm(v) for v in nm], axis=1))
    G, P_, Cg = 16, 64, 16
    s5c = np.zeros((DEPTH, 128, 3, 16), f)
    s5B = np.zeros((DEPTH, 128, 2, 8, 128), f)
    s5C = np.zeros((DEPTH, 128, 2, 8, 128), f)
    are, aim, ldt = (np.asarray(inputs[k], f) for k in ("s5_a_re", "s5_a_im", "s5_log_dt"))
    bre, bim, cre, cim = (np.asarray(inputs[k], f) for k in ("s5_b_re", "s5_b_im", "s5_c_re", "s5_c_im"))
    for l in range(DEPTH):
        for d in range(2):
            for j in range(8):
                for gl in range(2):
                    g = 2 * j + gl
                    s5c[l, gl * 64:(gl + 1) * 64, 0, d * 8 + j] = are[l, d, g]
                    s5c[l, gl * 64:(gl + 1) * 64, 1, d * 8 + j] = aim[l, d, g]
                    s5c[l, gl * 64:(gl + 1) * 64, 2, d * 8 + j] = ldt[l, d, g]
        for j in range(8):
            for gl in range(2):
                g = 2 * j + gl
                r0 = 32 * (j % 4) + 16 * gl
                s5B[l, r0:r0 + 16, 0, j, gl * 64:(gl + 1) * 64] = bre[l, g].T
                s5B[l, r0:r0 + 16, 1, j, gl * 64:(gl + 1) * 64] = bim[l, g].T
                s5C[l, gl * 64:(gl + 1) * 64, 0, j, r0:r0 + 16] = cre[l, g].T
                s5C[l, gl * 64:(gl + 1) * 64, 1, j, r0:r0 + 16] = cim[l, g].T
    m["s5c"], m["s5B"], m["s5C"] = s5c, s5B, s5C
    sd, gb = np.asarray(inputs["s5_d"], f), np.asarray(inputs["s5_glu_b"], f)
    m["s5v"] = np.ascontiguousarray(np.stack([np.stack([fm(sd[l]), fm(gb[l])], axis=-1) for l in range(DEPTH)], axis=0))
    m["s5g"] = np.asarray(inputs["s5_glu_w"], f)
    cmat = np.zeros((128, 8, 64), f)
    ii = np.arange(64)
    mats = [np.eye(64), (ii[None, :] <= ii[:, None]), (ii[None, :] < ii[:, None]), (ii[None, :] >= ii[:, None]),
            (ii[None, :] > ii[:, None]), np.ones((64, 64)), np.repeat((ii == 63)[:, None], 64, 1), np.repeat((ii == 0)[:, None], 64, 1)]
    for i_, mt in enumerate(mats):
        cmat[0:64, i_, :] = mt.astype(f)
        cmat[64:128, i_, :] = mt.astype(f)
    m["cmat"] = cmat
    gcv = np.asarray(inputs["gdn_conv"], f)
    m["gdn_cw"] = np.ascontiguousarray(gcv.reshape(DEPTH, 5, 9, 128).transpose(3, 0, 2, 1))
    gal, gdb = np.asarray(inputs["gdn_a_log"], f), np.asarray(inputs["gdn_dt_bias"], f)
    gsc = np.stack([gal.reshape(DEPTH, 12), gdb.reshape(DEPTH, 12)], axis=1)
    m["gdn_sc"] = np.ascontiguousarray(np.broadcast_to(gsc[None], (128, DEPTH, 2, 12)))
    gnw = np.asarray(inputs["gdn_norm"], f)
    m["gdn_nw"] = np.ascontiguousarray(np.concatenate([gnw.T, gnw.T], axis=0))
    g_ = lambda k: np.asarray(inputs[k], f)
    m["rw_mu"] = np.ascontiguousarray(np.stack([fm(g_("rwkv_mu")[l]) for l in range(DEPTH)], axis=1))
    vecs = np.zeros((128, DEPTH, 3, 9), f)
    for l in range(DEPTH):
        srcs = [g_("rwkv_k_k")[l], g_("rwkv_k_a")[l], g_("rwkv_r_k")[l].reshape(-1), g_("rwkv_ln_w")[l], g_("rwkv_ln_b")[l],
                g_("rwkv_w0")[l, 0], g_("rwkv_w0")[l, 1], g_("rwkv_a0")[l, 0], g_("rwkv_a0")[l, 1]]
        for i_, v_ in enumerate(srcs):
            vecs[:, l, :, i_] = fm(v_)
    m["rw_vec"] = vecs
    ups = np.zeros((128, DEPTH, 3, 384), f)
    for l in range(DEPTH):
        ups[:, l, 0] = g_("rwkv_w_up")[l].reshape(128, 384)
        ups[:, l, 1] = g_("rwkv_a_up")[l].reshape(128, 384)
        ups[:, l, 2] = g_("rwkv_g_up")[l]
    m["rw_up"] = ups
    pp = np.arange(128)
    m["cmask"] = np.stack([(pp % 4 == 0), (pp % 4 == 1), (pp % 4 == 2), (pp % 4 == 3), (pp % 2 == 0), (pp % 2 == 1)], axis=1).astype(f)
    m["w_in"] = np.asarray(inputs["w_in"], f)
    m["w_out"] = np.asarray(inputs["w_out"], f)
    m["mlp_w1"] = np.asarray(inputs["mlp_w1"], f)
    m["mlp_w2"] = np.asarray(inputs["mlp_w2"], f)
    return m


def kernel(**inputs):
    nc, _ = build_program()
    in_maps = [host_layout(inputs, core // 2) for core in range(8)]
    res = run_bass_kernel_spmd(nc, in_maps, core_ids=list(range(8)))
    out = np.stack([np.asarray(res.results[2 * b]["outT"]).T for b in range(4)], axis=0)
    return np.ascontiguousarray(out.astype(np.float32))
```

```python
import numpy as np
import concourse.bass as bass
import concourse.mybir as mybir
from concourse.bass_utils import run_bass_kernel_spmd

F32 = mybir.dt.float32
BF16 = mybir.dt.bfloat16
ALU = mybir.AluOpType
AF = mybir.ActivationFunctionType

D = 1024
LC = 256
LL = 4096
T = LC + LL
DEPTH = 2
INC = 3352
EPS = 1e-6
NDMA = 16


class St:
    __slots__ = ("lw", "rd")

    def __init__(self):
        self.lw = None
        self.rd = {}


class Buf:
    def __init__(self, name, handle, dram=False):
        self.name = name
        self.h = handle
        self.dram = dram
        self.reg = {None: St()}

    def _ap(self, idx):
        base = self.h.ap() if self.dram else self.h
        return base[idx]

    def __getitem__(self, idx):
        return View(self, self._ap(idx), None)

    def k(self, key):
        return Keyed(self, key)

    def states(self, key):
        if key is None:
            return list(self.reg.values())
        if key not in self.reg:
            self.reg[key] = St()
        return [self.reg[key], self.reg[None]]


class Keyed:
    def __init__(self, buf, key):
        self.buf = buf
        self.key = key

    def __getitem__(self, idx):
        return View(self.buf, self.buf._ap(idx), self.key)


class View:
    def __init__(self, buf, ap, key):
        self.buf = buf
        self.ap = ap
        self.key = key

    def with_ap(self, ap):
        return View(self.buf, ap, self.key)

    def rearrange(self, pat, **kw):
        return View(self.buf, self.ap.rearrange(pat, **kw), self.key)

    def __getitem__(self, idx):
        return View(self.buf, self.ap[idx], self.key)

    def bcast(self, shape):
        return View(self.buf, self.ap.broadcast_to(shape), self.key)


class Scope:
    def __init__(self, p):
        self.p = p
        self.bufs = []

    def __enter__(self):
        from contextlib import ExitStack
        self.prev_stack = self.p.stack
        self.prev_scope = getattr(self.p, "cur_scope", None)
        self.es = ExitStack()
        self.es.__enter__()
        self.p.stack = self.es
        self.p.cur_scope = self
        return self

    def __exit__(self, *a):
        p = self.p
        freed = dict(getattr(p, "freed", {}))
        for b in self.bufs:
            for st in b.reg.values():
                toks = list(st.rd.items()) + ([st.lw] if st.lw is not None else [])
                for kk, vv in toks:
                    if freed.get(kk, 0) < vv:
                        freed[kk] = vv
        p.freed = freed
        p.stack = self.prev_stack
        p.cur_scope = self.prev_scope
        if self.prev_scope is not None:
            self.prev_scope.bufs.extend(self.bufs)
        self.es.__exit__(None, None, None)
        return False


class Prog:
    ENGS = ("pe", "dve", "act", "pool", "sp")

    def __init__(self, nc, stack):
        self.nc = nc
        self.stack = stack
        self.lists = {e: [] for e in self.ENGS}
        self.cnt = {e: 0 for e in self.ENGS}
        self.seen = {e: {} for e in self.ENGS}
        self.sems = {}
        for e in ("pe", "dve", "act", "pool"):
            self.sems[e] = stack.enter_context(nc.semaphore("s_" + e))
        self.dma_slots = {}
        for q in ("sp", "pool", "act"):
            self.dma_slots[q] = []
            for i in range(NDMA if q == "sp" else 4):
                key = "d_%s_%d" % (q, i)
                self.sems[key] = stack.enter_context(nc.semaphore(key))
                self.dma_slots[q].append([key, 0])
        self.dma_rr = {q: 0 for q in self.dma_slots}
        self.n_instr = 0

    def scope(self):
        return Scope(self)

    def uniq(self, name):
        self.n_names = getattr(self, "n_names", 0) + 1
        return "%s_%d" % (name, self.n_names)

    def sbuf(self, name, shape, dt):
        name = self.uniq(name)
        b = Buf(name, self.stack.enter_context(self.nc.sbuf_tensor(name, shape, dt)))
        b.reg[None].rd = dict(getattr(self, "freed", {}))
        if getattr(self, "cur_scope", None) is not None:
            self.cur_scope.bufs.append(b)
        return b

    def psum(self, name, shape, dt=F32):
        b = Buf(name, self.stack.enter_context(self.nc.psum_tensor(name, shape, dt)))
        b.is_psum = True
        return b

    def dram(self, name, shape, dt, kind="Internal"):
        return Buf(name, self.nc.dram_tensor(name, shape, dt, kind=kind), dram=True)

    def _need(self, eng, reads, writes):
        need = {}

        def add(tok):
            if tok is not None:
                if need.get(tok[0], 0) < tok[1]:
                    need[tok[0]] = tok[1]

        for v in reads:
            for st in v.buf.states(v.key):
                add(st.lw)
        for v in writes:
            for st in v.buf.states(v.key):
                add(st.lw)
                for kk, vv in st.rd.items():
                    add((kk, vv))
        out = []
        for kk, vv in need.items():
            if eng == "pe" and kk == "pe":
                continue
            if self.seen[eng].get(kk, 0) >= vv:
                continue
            self.seen[eng][kk] = vv
            out.append((kk, vv))
        return out

    def _mark(self, tok, reads, writes):
        for v in reads:
            st = v.buf.states(v.key)[0] if v.key is not None else v.buf.reg[None]
            st.rd[tok[0]] = tok[1]
        for v in writes:
            if v.key is None:
                for st in v.buf.reg.values():
                    st.lw = tok
                    st.rd = {}
            else:
                st = v.buf.states(v.key)[0]
                st.lw = tok
                st.rd = {}

    def op(self, eng, fn, reads, writes):
        writes = list(writes) + [v for v in reads if getattr(v.buf, "is_psum", False)]
        waits = self._need(eng, reads, writes)
        self.cnt[eng] += 1
        tok = (eng, self.cnt[eng])
        self.lists[eng].append((waits, fn, (eng, 1)))
        self._mark(tok, reads, writes)
        self.n_instr += 1

    def dma(self, out, in_, q="sp"):
        slots = self.dma_slots[q]
        i = self.dma_rr[q]
        self.dma_rr[q] = (i + 1) % len(slots)
        slot = slots[i]
        waits = self._need(q, [in_], [out])
        if slot[1] > 0 and self.seen[q].get(slot[0], 0) < slot[1]:
            self.seen[q][slot[0]] = slot[1]
            waits.append((slot[0], slot[1]))
        slot[1] += 16
        tok = (slot[0], slot[1])
        o_ap, i_ap = out.ap, in_.ap
        self.lists[q].append((waits, lambda e: e.dma_start(out=o_ap, in_=i_ap), (slot[0], 16)))
        self._mark(tok, [in_], [out])
        self.n_instr += 1

    def wait_all(self, eng, views):
        waits = self._need(eng, views, [])
        self.lists[eng].append((waits, None, None))

    def emit(self):
        nc = self.nc
        sems = self.sems
        with nc.Block() as block:
            def mk(lst):
                def body(e):
                    for waits, fn, inc in lst:
                        for kk, vv in waits:
                            e.wait_ge(sems[kk], vv)
                        if fn is not None:
                            fn(e).then_inc(sems[inc[0]], inc[1])
                return body

            block.tensor(mk(self.lists["pe"]))
            block.vector(mk(self.lists["dve"]))
            block.scalar(mk(self.lists["act"]))
            block.gpsimd(mk(self.lists["pool"]))
            block.sync(mk(self.lists["sp"]))

    def mm(self, out, lhsT, rhs, start=True, stop=True):
        self.op("pe", lambda e: e.matmul(out.ap, lhsT.ap, rhs.ap, start=start, stop=stop), [lhsT, rhs], [out])

    def tr(self, out, in_, ident):
        self.op("pe", lambda e: e.matmul(out.ap, in_.ap, ident.ap, start=True, stop=True), [in_, ident], [out])

    def tt(self, eng, out, a, b, op):
        self.op(eng, lambda e: e.tensor_tensor(out=out.ap, in0=a.ap, in1=b.ap, op=op), [a, b], [out])

    def ts(self, eng, out, a, s1, op0, s2=None, op1=None):
        reads = [a] + [s for s in (s1, s2) if isinstance(s, View)]
        s1a = s1.ap if isinstance(s1, View) else s1
        s2a = s2.ap if isinstance(s2, View) else s2
        if op1 is None:
            self.op(eng, lambda e: e.tensor_scalar(out=out.ap, in0=a.ap, scalar1=s1a, scalar2=None, op0=op0), reads, [out])
        else:
            self.op(eng, lambda e: e.tensor_scalar(out=out.ap, in0=a.ap, scalar1=s1a, scalar2=s2a, op0=op0, op1=op1), reads, [out])

    def stt(self, eng, out, a, s, b, op0, op1):
        reads = [a, b] + ([s] if isinstance(s, View) else [])
        sa = s.ap if isinstance(s, View) else s
        eng = "dve"
        self.op(eng, lambda e: e.scalar_tensor_tensor(out=out.ap, in0=a.ap, scalar=sa, in1=b.ap, op0=op0, op1=op1), reads, [out])

    def act(self, out, a, func, bias=None, scale=None):
        reads = [a] + [s for s in (bias, scale) if isinstance(s, View)]
        kw = {}
        if bias is not None:
            kw["bias"] = bias.ap if isinstance(bias, View) else bias
        if scale is not None:
            kw["scale"] = scale.ap if isinstance(scale, View) else scale
        self.op("act", lambda e: e.activation(out=out.ap, in_=a.ap, func=func, **kw), reads, [out])

    def copy(self, eng, out, a):
        if eng == "act":
            self.op("act", lambda e: e.activation(out=out.ap, in_=a.ap, func=AF.Copy), [a], [out])
        else:
            self.op(eng, lambda e: e.tensor_copy(out=out.ap, in_=a.ap), [a], [out])

    def recip(self, out, a):
        self.op("dve", lambda e: e.reciprocal(out=out.ap, in_=a.ap), [a], [out])

    def memset(self, eng, out, val):
        self.op(eng, lambda e: e.memset(out.ap, val), [], [out])

    def scan(self, out, d0, d1, init, op0=ALU.mult, op1=ALU.add):
        reads = [d0, d1] + ([init] if isinstance(init, View) else [])
        ia = init.ap if isinstance(init, View) else init
        self.op("dve", lambda e: e.tensor_tensor_scan(out=out.ap, data0=d0.ap, data1=d1.ap, initial=ia, op0=op0, op1=op1), reads, [out])


def rev_view(v):
    from concourse.ap import AP
    ap = v.ap
    l = [list(x) for x in ap.ap]
    assert l[-1][0] == 1, l
    off = ap.offset + (l[-1][1] - 1)
    l[-1][0] = -1
    return v.with_ap(AP(ap.tensor, off, l))


def token_blocks(n):
    nc_ = min(n, LC)
    blks = [(i, nc_, True) for i in range(0, LC, nc_)]
    for s in range(LC, T, n):
        blks.append((s, n, False))
    return blks


IN_TILES = [(0, 128), (128, 128)] + [(256 + 128 * i, 128) for i in range(12)] + [(1792, 24)] + \
           [(1816 + 128 * i, 128) for i in range(12)]


def build_program(debug=None):
    from contextlib import ExitStack
    if debug is None:
        debug = {"zero_mix": True, "s5": True, "gdn": True, "rwkv": True}
    nc = bass.Bass("TRN2", target_bir_lowering=False)
    with ExitStack() as stack:
        p = Prog(nc, stack)
        xT = p.dram("xT", [D, T], F32, kind="ExternalInput")
        cT = p.dram("cT", [128, 8, 2], F32, kind="ExternalInput")
        mod_w = p.dram("mod_w", [DEPTH, D, 6 * D], F32, kind="ExternalInput")
        mod_b = p.dram("mod_b", [128, DEPTH, 48], F32, kind="ExternalInput")
        nrm = p.dram("nrm", [128, 5, 8], F32, kind="ExternalInput")
        w_in = p.dram("w_in", [DEPTH, D, INC], F32, kind="ExternalInput")
        w_out = p.dram("w_out", [DEPTH, D, D], F32, kind="ExternalInput")
        w1 = p.dram("mlp_w1", [DEPTH, D, 4 * D], F32, kind="ExternalInput")
        w2 = p.dram("mlp_w2", [DEPTH, 4 * D, D], F32, kind="ExternalInput")
        s5c = p.dram("s5c", [DEPTH, 128, 3, 16], F32, kind="ExternalInput")
        s5B = p.dram("s5B", [DEPTH, 128, 2, 8, 128], F32, kind="ExternalInput")
        s5C = p.dram("s5C", [DEPTH, 128, 2, 8, 128], F32, kind="ExternalInput")
        s5v = p.dram("s5v", [DEPTH, 128, 2, 2], F32, kind="ExternalInput")
        s5g = p.dram("s5g", [DEPTH, 256, 256], F32, kind="ExternalInput")
        cmat_d = p.dram("cmat", [128, 8, 64], F32, kind="ExternalInput")
        gdn_cw = p.dram("gdn_cw", [128, DEPTH, 9, 5], F32, kind="ExternalInput")
        gdn_sc = p.dram("gdn_sc", [128, DEPTH, 2, 12], F32, kind="ExternalInput")
        gdn_nw = p.dram("gdn_nw", [128, DEPTH], F32, kind="ExternalInput")
        pBA = p.dram("pBA", [T, 24], F32)
        rw_mu = p.dram("rw_mu", [128, DEPTH, 12], F32, kind="ExternalInput")
        rw_vec = p.dram("rw_vec", [128, DEPTH, 3, 9], F32, kind="ExternalInput")
        rw_up = p.dram("rw_up", [128, DEPTH, 3, 384], F32, kind="ExternalInput")
        cmask = p.dram("cmask", [128, 6], F32, kind="ExternalInput")
        yrw = p.dram("yrw", [384, T], F32)
        outT = p.dram("outT", [D, LL], F32, kind="ExternalOutput")
        xres = p.dram("xres", [D, T], F32)
        pT = p.dram("pT", [INC, T], F32)
        yT = p.dram("yT", [D, T], BF16)
        dbg = None
        if debug and "shape" in debug:
            dbg = p.dram("dbg", list(debug["shape"]), F32, kind="ExternalOutput")

        ones_bf = p.sbuf("ones_bf", [128, 128], BF16)
        p.memset("dve", ones_bf[:], 1.0)
        modv = p.sbuf("modv", [128, DEPTH, 48, 2], F32)
        modb_sb = p.sbuf("modb_sb", [128, DEPTH, 48], F32)
        nrm_sb = p.sbuf("nrm_sb", [128, 5, 8], F32)
        sc_sb = p.sbuf("sc_sb", [128, 8, 2], F32)
        coef = p.sbuf("coef", [128, DEPTH, 2, 6, 8], F32)
        ps = [p.psum("ps%d" % i, [128, 512]) for i in range(8)]

        p.dma(modb_sb[:], mod_b[:])
        p.dma(nrm_sb[:], nrm[:])
        p.dma(sc_sb[:], cT[:])
        sg = p.sbuf("sg", [128, 8, 2], F32)
        p.act(sg[:], sc_sb[:], AF.Sigmoid)
        p.tt("dve", sc_sb[:], sc_sb[:], sg[:], ALU.mult)

        with p.scope():
            mw = [p.sbuf("mw_a", [128, 8, 512], F32), p.sbuf("mw_b", [128, 8, 512], F32)]
            it = 0
            for l in range(DEPTH):
                src = mod_w.h.ap()[l].rearrange("(k p) n -> p k n", p=128)
                for g in range(12):
                    b = mw[it % 2]
                    it += 1
                    p.dma(b[:], View(mod_w, src[:, :, g * 512:(g + 1) * 512], None))
                    pst = ps[g % 2]
                    for jj in range(4):
                        for k in range(8):
                            p.mm(pst[:, jj * 2:jj * 2 + 2], b[:, k, jj * 128:(jj + 1) * 128], sc_sb[:, k, :],
                                 start=(k == 0), stop=(k == 7))
                    for w in range(2):
                        p.tt("dve", modv[:, l, g * 4:(g + 1) * 4, w],
                             pst[:, 0:8].rearrange("p (j w) -> p j w", w=2)[:, :, w],
                             modb_sb[:, l, g * 4:(g + 1) * 4], ALU.add)
        for l in range(DEPTH):
            for w in range(2):
                mv = lambda j0: modv[:, l, j0:j0 + 8, w]
                p.stt("dve", coef[:, l, w, 0, :], mv(8), 1.0, nrm_sb[:, l, :], ALU.add, ALU.mult)
                p.copy("dve", coef[:, l, w, 1, :], mv(0))
                p.copy("dve", coef[:, l, w, 2, :], mv(16))
                p.stt("dve", coef[:, l, w, 3, :], mv(32), 1.0, nrm_sb[:, 2 + l, :], ALU.add, ALU.mult)
                p.copy("dve", coef[:, l, w, 4, :], mv(24))
                p.copy("dve", coef[:, l, w, 5, :], mv(40))

        for k in range(8):
            p.dma(xres[k * 128:(k + 1) * 128, :], xT[k * 128:(k + 1) * 128, :])

        xres_v = lambda s, n: xres.h.ap().rearrange("(k p) t -> p k t", p=128)[:, :, s:s + n]
        yT_v = lambda s, n: yT.h.ap().rearrange("(k p) t -> p k t", p=128)[:, :, s:s + n]

        def load_cast(dst, src_ap_fn, ncols, src_buf, stg, engs=("dve", "pool")):
            i = 0
            for c0 in range(0, ncols, 512):
                c1 = min(ncols, c0 + 512)
                s = stg[i % 2]
                p.dma(s[:, :, 0:c1 - c0], View(src_buf, src_ap_fn(c0, c1), None))
                p.copy(engs[i % 2], dst[:, :, c0:c1], s[:, :, 0:c1 - c0])
                i += 1

        eps_sb = p.sbuf("eps_sb", [128, 1], F32)
        p.memset("dve", eps_sb[:], EPS)

        def rstd_from(dst, src, scale=1.0 / D, eps=None):
            p.act(dst, src, AF.Sqrt, bias=(eps if eps is not None else eps_sb)[:, 0:1], scale=scale)
            p.recip(dst, dst)

        def norm_block(xb, xn, n, cf_A, cf_sh, sq, rstd, pst, ntmp):
            for k in range(8):
                p.act(sq[:, k, 0:n], xb[:, k, 0:n], AF.Square)
            for k in range(8):
                p.mm(pst[:, 0:n], ones_bf[:], sq[:, k, 0:n], start=(k == 0), stop=(k == 7))
            rstd_from(rstd[:, 0:n], pst[:, 0:n])
            for k in range(8):
                eng = "dve" if k % 2 == 0 else "pool"
                tmp = ntmp[k % 2]
                p.tt(eng, tmp[:, 0:n], xb[:, k, 0:n], rstd[:, 0:n], ALU.mult)
                p.act(xn[:, k, 0:n], tmp[:, 0:n], AF.Identity, bias=cf_sh[:, k:k + 1], scale=cf_A[:, k:k + 1])

        PI = float(np.pi)
        cst = p.sbuf("cst", [128, 4], F32)
        p.memset("dve", cst[:, 0:1], -3.1415915)
        p.memset("dve", cst[:, 1:2], 1.0)

        def sincos(dst_c, dst_s, th, shp, tmpf):
            t_y, t_k, t_m, t_f, t_i = tmpf
            for dst, shift in ((dst_s, 0.5), (dst_c, 0.75)):
                p.ts("dve", t_y, th, 1.0 / (2 * PI), ALU.mult, shift, ALU.add)
                p.copy("dve", t_i, t_y)
                p.copy("dve", t_k, t_i)
                p.tt("dve", t_m, t_k, t_y, ALU.is_gt)
                p.tt("dve", t_k, t_k, t_m, ALU.subtract)
                p.tt("dve", t_f, t_y, t_k, ALU.subtract)
                p.ts("dve", t_f, t_f, 2 * PI, ALU.mult, -PI, ALU.add)
                p.tt("dve", t_m, t_f, t_f, ALU.mult)
                coefs = [(-1.0) ** k / float(np.prod(np.arange(1, 2 * k + 2, dtype=np.float64))) for k in range(10)]
                p.memset("dve", t_k, coefs[9])
                for k in range(8, -1, -1):
                    p.tt("dve", t_k, t_k, t_m, ALU.mult)
                    p.ts("dve", t_k, t_k, coefs[k], ALU.add)
                p.tt("dve", dst, t_k, t_f, ALU.mult)

        S5TC = 256

        def s5_mixer(l):
            TC = S5TC
            NCH = T // TC
            with p.scope():
                I32 = mybir.dt.int32
                c_sb = p.sbuf("s5c_sb", [128, 3, 16], F32)
                p.dma(c_sb[:], s5c[l])
                sm = {nm: p.sbuf("s5_" + nm, [128, 16], F32) for nm in
                      ("dt", "th", "r", "c1", "s1", "x", "y", "den", "cre", "cim", "t0", "t1", "ty", "tk", "tm", "tf", "cT", "sT")}
                ti32 = p.sbuf("s5_ti", [128, 16], I32)
                V = lambda nm: sm[nm][:, :]
                def exp_acc(dst, src):
                    tq, e_ = V("ty"), V("tk")
                    p.ts("dve", tq, src, 1.0 / 16.0, ALU.mult)
                    p.memset("dve", e_, 1.0)
                    for k in range(10, 0, -1):
                        p.tt("dve", e_, e_, tq, ALU.mult)
                        p.ts("dve", e_, e_, 1.0 / k, ALU.mult, 1.0, ALU.add)
                    for _ in range(4):
                        p.tt("dve", e_, e_, e_, ALU.mult)
                    p.copy("dve", dst, e_)
                exp_acc(V("dt"), c_sb[:, 2, :])
                p.tt("dve", V("th"), V("dt"), c_sb[:, 1, :], ALU.mult)
                p.tt("dve", V("t0"), V("dt"), c_sb[:, 0, :], ALU.mult)
                exp_acc(V("r"), V("t0"))
                sincos(V("c1"), V("s1"), V("th"), None, (V("ty"), V("tk"), V("tm"), V("tf"), ti32[:, :]))
                p.tt("dve", V("x"), V("r"), V("c1"), ALU.mult)
                p.ts("dve", V("x"), V("x"), -1.0, ALU.add)
                p.tt("dve", V("y"), V("r"), V("s1"), ALU.mult)
                p.tt("dve", V("den"), c_sb[:, 0, :], c_sb[:, 0, :], ALU.mult)
                p.tt("dve", V("t0"), c_sb[:, 1, :], c_sb[:, 1, :], ALU.mult)
                p.tt("dve", V("den"), V("den"), V("t0"), ALU.add)
                p.recip(V("den"), V("den"))
                p.tt("dve", V("t0"), V("x"), c_sb[:, 0, :], ALU.mult)
                p.tt("dve", V("t1"), V("y"), c_sb[:, 1, :], ALU.mult)
                p.tt("dve", V("t0"), V("t0"), V("t1"), ALU.add)
                p.tt("dve", V("cre"), V("t0"), V("den"), ALU.mult)
                p.tt("dve", V("t0"), V("y"), c_sb[:, 0, :], ALU.mult)
                p.tt("dve", V("t1"), V("x"), c_sb[:, 1, :], ALU.mult)
                p.tt("dve", V("t0"), V("t0"), V("t1"), ALU.subtract)
                p.tt("dve", V("cim"), V("t0"), V("den"), ALU.mult)

                B_bf = p.sbuf("s5B_bf", [128, 2, 8, 128], BF16)
                C_bf = p.sbuf("s5C_bf", [128, 2, 8, 128], BF16)
                G_bf = p.sbuf("s5G_bf", [128, 2, 256], BF16)
                v_sb = p.sbuf("s5v_sb", [128, 2, 2], F32)
                p.dma(v_sb[:], s5v[l])
                with p.scope():
                    stg = p.sbuf("s5stg", [128, 2, 8, 128], F32)
                    p.dma(stg[:], s5B[l])
                    p.copy("dve", B_bf[:], stg[:])
                    stg2 = p.sbuf("s5stg2", [128, 2, 8, 128], F32)
                    p.dma(stg2[:], s5C[l])
                    p.copy("pool", C_bf[:], stg2[:])
                    stg3 = p.sbuf("s5stg3", [128, 2, 256], F32)
                    p.dma(stg3[:], View(s5g, s5g.h.ap()[l].rearrange("(k p) n -> p k n", p=128), None))
                    p.copy("dve", G_bf[:], stg3[:])

                u_b = p.sbuf("s5u_b", [128, 2, T], BF16)
                yacc = p.sbuf("s5yacc", [128, 2, T], F32)
                ustg = [p.sbuf("s5ustg%d" % q, [128, 1088], F32) for q in range(2)]
                uq = 0
                for i in range(2):
                    for c in range(0, T, 1088):
                        st_ = ustg[uq % 2]
                        uq += 1
                        p.dma(st_[:, :], pT[i * 128:(i + 1) * 128, c:c + 1088])
                        p.copy("pool" if i else "act", u_b.k((i, c))[:, i, c:c + 1088], st_[:, :])

                tab = {nm: p.sbuf("s5tab_" + nm, [128, 8, TC], F32) for nm in ("er", "ei", "mr", "mi", "rt")}
                wk = {nm: [p.sbuf("s5w_%s%d" % (nm, q), [128, TC], F32) for q in range(2)] for nm in
                      ("br", "bi", "a", "b", "c", "d", "dr", "di", "gr", "gi")}
                hb = {nm: p.sbuf("s5h_" + nm, [128, 8, TC], BF16) for nm in ("re", "im")}
                tA_buf = p.sbuf("s5tA", [128, 8, TC], F32)
                carry = p.sbuf("s5carry", [128, 8, 2], F32)
                sc8 = {nm: p.sbuf("s5s8_" + nm, [128, 8], F32) for nm in ("mc", "ms", "t0", "t1", "t2")}
                ctmp = p.sbuf("s5ctmp", [128, 4], F32)

                for d in range(2):
                    ds = slice(d * 8, d * 8 + 8)
                    p.memset("dve", tab["er"][:, :, 0:1], 1.0)
                    p.memset("dve", tab["ei"][:, :, 0:1], 0.0)
                    p.copy("dve", sc8["mc"][:, :], sm["c1"][:, ds])
                    p.copy("dve", sc8["ms"][:, :], sm["s1"][:, ds])
                    m = 1
                    while m < TC:
                        bc = lambda v: v[:, :].rearrange("p (j o) -> p j o", o=1).bcast([128, 8, m])
                        er0, ei0 = tab["er"][:, :, 0:m], tab["ei"][:, :, 0:m]
                        er1, ei1 = tab["er"][:, :, m:2 * m], tab["ei"][:, :, m:2 * m]
                        t_a, t_b = tab["mr"][:, :, 0:m], tab["mi"][:, :, 0:m]
                        p.tt("dve", t_a, er0, bc(sc8["mc"]), ALU.mult)
                        p.tt("pool", t_b, ei0, bc(sc8["ms"]), ALU.mult)
                        p.tt("dve", er1, t_a, t_b, ALU.subtract)
                        p.tt("dve", t_a, er0, bc(sc8["ms"]), ALU.mult)
                        p.tt("pool", t_b, ei0, bc(sc8["mc"]), ALU.mult)
                        p.tt("dve", ei1, t_a, t_b, ALU.add)
                        p.tt("dve", sc8["t0"][:, :], sc8["mc"][:, :], sc8["mc"][:, :], ALU.mult)
                        p.tt("dve", sc8["t1"][:, :], sc8["ms"][:, :], sc8["ms"][:, :], ALU.mult)
                        p.tt("dve", sc8["t2"][:, :], sc8["mc"][:, :], sc8["ms"][:, :], ALU.mult)
                        p.tt("dve", sc8["mc"][:, :], sc8["t0"][:, :], sc8["t1"][:, :], ALU.subtract)
                        p.ts("dve", sc8["ms"][:, :], sc8["t2"][:, :], 2.0, ALU.mult)
                        m *= 2
                    bcT = lambda v: v.rearrange("p (j o) -> p j o", o=1).bcast([128, 8, TC])
                    cre_b, cim_b = bcT(sm["cre"][:, ds]), bcT(sm["cim"][:, ds])
                    t_a = tA_buf
                    p.tt("dve", tab["mr"][:], tab["er"][:], cre_b, ALU.mult)
                    p.tt("pool", t_a[:], tab["ei"][:], cim_b, ALU.mult)
                    p.tt("dve", tab["mr"][:], tab["mr"][:], t_a[:], ALU.add)
                    p.tt("dve", tab["mi"][:], tab["er"][:], cim_b, ALU.mult)
                    p.tt("pool", t_a[:], tab["ei"][:], cre_b, ALU.mult)
                    p.tt("dve", tab["mi"][:], tab["mi"][:], t_a[:], ALU.subtract)
                    p.memset("pool", tab["rt"][:], 1.0)
                    p.tt("pool", tab["rt"][:], tab["rt"][:], bcT(sm["r"][:, ds]), ALU.mult)
                    p.memset("dve", carry[:], 0.0)

                    def ord_(v):
                        return v if d == 0 else rev_view(v)

                    chunks = list(range(NCH)) if d == 0 else [0] + list(range(NCH - 1, 0, -1))
                    for ci, ch in enumerate(chunks):
                        t0_ = ch * TC
                        for j in range(8):
                            q = j % 2
                            pr, pi_ = ps[(2 * j) % 6], ps[(2 * j + 1) % 6]
                            p.mm(pr[:, 0:TC], B_bf[:, 0, j, :], u_b[:, j // 4, t0_:t0_ + TC])
                            p.mm(pi_[:, 0:TC], B_bf[:, 1, j, :], u_b[:, j // 4, t0_:t0_ + TC])
                            br, bi = wk["br"][q][:, :], wk["bi"][q][:, :]
                            p.copy("act", br, pr[:, 0:TC])
                            p.copy("act", bi, pi_[:, 0:TC])
                            mr, mi = ord_(tab["mr"][:, j, :]), ord_(tab["mi"][:, j, :])
                            er, ei = ord_(tab["er"][:, j, :]), ord_(tab["ei"][:, j, :])
                            a_, b_, c_, d_ = (wk[nm][q][:, :] for nm in ("a", "b", "c", "d"))
                            dr, di = wk["dr"][q][:, :], wk["di"][q][:, :]
                            gr, gi = wk["gr"][q][:, :], wk["gi"][q][:, :]
                            p.tt("dve", a_, br, mr, ALU.mult)
                            p.tt("pool", b_, bi, mi, ALU.mult)
                            p.tt("pool", c_, br, mi, ALU.mult)
                            p.tt("dve", d_, bi, mr, ALU.mult)
                            p.tt("dve", dr, a_, b_, ALU.subtract)
                            p.tt("pool", di, c_, d_, ALU.add)
                            p.scan(ord_(gr), tab["rt"][:, j, :], ord_(dr), carry[:, j, 0:1])
                            p.scan(ord_(gi), tab["rt"][:, j, :], ord_(di), carry[:, j, 1:2])
                            lr = gr[:, TC - 1:TC] if d == 0 else gr[:, 0:1]
                            li = gi[:, TC - 1:TC] if d == 0 else gi[:, 0:1]
                            mc, ms_ = sc8["mc"][:, j:j + 1], sc8["ms"][:, j:j + 1]
                            p.tt("dve", ctmp[:, 0:1], lr, mc, ALU.mult)
                            p.tt("dve", ctmp[:, 1:2], li, ms_, ALU.mult)
                            p.tt("dve", ctmp[:, 2:3], lr, ms_, ALU.mult)
                            p.tt("dve", ctmp[:, 3:4], li, mc, ALU.mult)
                            p.tt("dve", carry[:, j, 0:1], ctmp[:, 0:1], ctmp[:, 1:2], ALU.subtract)
                            p.tt("dve", carry[:, j, 1:2], ctmp[:, 2:3], ctmp[:, 3:4], ALU.add)
                            p.tt("dve", a_, gr, er, ALU.mult)
                            p.tt("pool", b_, gi, ei, ALU.mult)
                            p.tt("pool", c_, gr, ei, ALU.mult)
                            p.tt("dve", d_, gi, er, ALU.mult)
                            p.tt("dve", hb["re"].k(j)[:, j, :], a_, b_, ALU.subtract)
                            p.stt("dve", hb["im"].k(j)[:, j, :], c_, -1.0, d_, ALU.mult, ALU.subtract)
                        for i in range(2):
                            py = ps[6 + i]
                            for jj in range(4):
                                j = 4 * i + jj
                                p.mm(py[:, 0:TC], C_bf[:, 0, j, :], hb["re"].k(j)[:, j, :], start=(jj == 0), stop=False)
                                p.mm(py[:, 0:TC], C_bf[:, 1, j, :], hb["im"].k(j)[:, j, :], start=False, stop=(jj == 3))
                            ya = yacc.k((i, ch))[:, i, t0_:t0_ + TC]
                            if d == 0:
                                p.copy("act", ya, py[:, 0:TC])
                            else:
                                p.tt("dve", ya, ya, py[:, 0:TC], ALU.add)
                NB = 512
                zb = [p.sbuf("s5z%d" % q, [128, 2, NB], BF16) for q in range(2)]
                zf = [p.sbuf("s5zf%d" % q, [128, 2, NB], F32) for q in range(2)]
                t1 = p.sbuf("s5t1", [128, NB], F32)
                t2 = p.sbuf("s5t2", [128, NB], F32)
                ob = [p.sbuf("s5ob%d" % q, [128, 2, NB], BF16) for q in range(2)]
                ufs = [p.sbuf("s5uf%d" % q, [128, 2, NB], F32) for q in range(2)]
                for bi_, (s_, n, isctx) in enumerate(token_blocks(NB)):
                    q = bi_ % 2
                    u_f = ufs[q]
                    p.dma(u_f[:, :, 0:n], View(pT, pT.h.ap()[0:256, :].rearrange("(k p) t -> p k t", p=128)[:, :, s_:s_ + n], None))
                    for i in range(2):
                        yv = t1[:, 0:n]
                        p.stt("dve", yv, u_f[:, i, 0:n], v_sb[:, i, 0:1], yacc[:, i, s_:s_ + n], ALU.mult, ALU.add)
                        p.tt("pool", t2[:, 0:n], yv, yv, ALU.mult)
                        p.ts("dve", t2[:, 0:n], t2[:, 0:n], 0.044715, ALU.mult, 1.0, ALU.add)
                        p.tt("pool", t2[:, 0:n], t2[:, 0:n], yv, ALU.mult)
                        p.act(t2[:, 0:n], t2[:, 0:n], AF.Sigmoid, scale=1.5957691216)
                        p.tt("dve", zf[q][:, i, 0:n], yv, t2[:, 0:n], ALU.mult)
                        p.copy("pool", zb[q][:, i, 0:n], zf[q][:, i, 0:n])
                    for i in range(2):
                        pg = ps[i]
                        for k in range(2):
                            p.mm(pg[:, 0:n], G_bf[:, k, i * 128:(i + 1) * 128], zb[q][:, k, 0:n], start=(k == 0), stop=(k == 1))
                        p.act(t2[:, 0:n], pg[:, 0:n], AF.Sigmoid, bias=v_sb[:, i, 1:2])
                        p.tt("dve", ob[q][:, i, 0:n], zf[q][:, i, 0:n], t2[:, 0:n], ALU.mult)
                    p.dma(View(yT, yT.h.ap()[0:256, :].rearrange("(k p) t -> p k t", p=128)[:, :, s_:s_ + n], ("s5", s_)), ob[q][:, :, 0:n])

        cm = p.sbuf("cmat_sb", [128, 8, 64], F32)
        p.dma(cm[:], cmat_d[:])
        one_c = p.sbuf("one_c", [128, 1], F32)
        p.memset("dve", one_c[:], 1.0)
        CI, CL, CLS, CU, CUS, CONES, CSEL63, CSEL0 = range(8)
        NCK = T // 64
        BWD_ORDER = [3, 2, 1, 0] + list(range(NCK - 1, 3, -1))

        class PsumSlots:
            def __init__(self, banks, half):
                self.banks = banks
                self.half = half
                self.bi = -1
                self.j = 0

            def group(self):
                self.bi = (self.bi + 1) % len(self.banks)
                self.j = 0

            def get(self, part0=None):
                return self.getn(1)

            def getn(self, n):
                b = self.banks[self.bi]
                j = self.j
                self.j += n
                assert self.j <= 8
                h = self.half
                return ps[b].k(("h", h))[h * 64:(h + 1) * 64, j * 64:(j + n) * 64]

        def neumann_gen(c, hs, M, Y0ps, W, pslots):
            Ih = cm[hs, CI, :]
            XY = [W("XY0"), W("XY1")]
            TT = [W("T0"), W("T1")]
            p.copy("dve", XY[0][:, 64:128], Y0ps)
            p.tt("dve", TT[0], Ih, Y0ps, ALU.subtract)
            Xc, Yc, Tc = M, XY[0][:, 64:128], TT[0]
            for k in range(1, 6):
                pslots.group()
                nxy = 2 if k < 5 else 1
                pxy = pslots.getn(nxy)
                p.mm(pxy[:, 0:64], Yc, Xc)
                if k < 5:
                    p.mm(pxy[:, 64:128], Xc, Yc)
                yield
                XYn = XY[k % 2]
                p.copy("dve", XYn[:, 0:64 * nxy], pxy)
                Xn = XYn[:, 0:64]
                pslots.group()
                pz = pslots.get()
                p.mm(pz, Xn, Tc)
                yield
                Tn = TT[k % 2]
                p.tt("dve", Tn, Tc, pz, ALU.add)
                Xc, Tc = Xn, Tn
                if k < 5:
                    Yc = XYn[:, 64:128]
            c["TT"] = Tc

        def run_interleaved(gens):
            gens = list(gens)
            while gens:
                for g in list(gens):
                    try:
                        next(g)
                    except StopIteration:
                        gens.remove(g)

        def gdn_mixer(l):
            with p.scope():
                ba = p.sbuf("g_ba", [128, NCK, 24], F32)
                for hf in range(2):
                    for n0 in range(0, NCK, 17):
                        p.dma(ba.k((hf, n0))[hf * 64:(hf + 1) * 64, n0:n0 + 17, :],
                              View(pBA, pBA.h.ap().rearrange("(n t) r -> t n r", t=64)[:, n0:n0 + 17, :], None))
                sc = p.sbuf("g_sc", [128, 2, 12], F32)
                p.dma(sc[:], gdn_sc[:, l])
                nA = p.sbuf("g_nA", [128, 12], F32)
                p.act(nA[:], sc[:, 0, :], AF.Exp)
                p.ts("dve", nA[:], nA[:], -1.0, ALU.mult)
                if debug.get("gdn_stop") == 0:
                    return
                names = ("beta", "g", "gc", "gl", "egc", "egl", "ed", "nbe")
                tk = {nm: p.sbuf("g_" + nm, [128, NCK, 12], F32) for nm in names}
                bcn = lambda v: v.rearrange("p (o r) -> p o r", o=1).bcast([128, NCK, 12])
                p.act(tk["beta"][:], ba[:, :, 0:12], AF.Sigmoid)
                p.tt("dve", tk["g"][:], ba[:, :, 12:24], bcn(sc[:, 1, :]), ALU.add)
                p.act(tk["g"][:], tk["g"][:], AF.Exp)
                p.act(tk["g"][:], tk["g"][:], AF.Ln, bias=one_c[:, 0:1])
                p.tt("dve", tk["g"][:], tk["g"][:], bcn(nA[:, :]), ALU.mult)
                if debug.get("gdn_stop") == 5:
                    return
                for d in range(2):
                    for hf in range(2):
                        hsl = slice(hf * 64, hf * 64 + 64)
                        tri = cm[hsl, CU if d == 0 else CL, :]
                        sel = cm[hsl, CSEL63 if d == 0 else CSEL0, :]
                        for n0 in range(0, NCK, 34):
                            r3 = lambda v: v.rearrange("p (n r) -> p n r", r=6)
                            pg = ps[0]
                            p.mm(r3(pg[hsl, 0:204]), tri, tk["g"][hsl, n0:n0 + 34, d * 6:(d + 1) * 6])
                            p.copy("dve", tk["gc"][hsl, n0:n0 + 34, d * 6:(d + 1) * 6], r3(pg[hsl, 0:204]))
                            pl = ps[1]
                            p.mm(r3(pl[hsl, 0:204]), sel, tk["gc"][hsl, n0:n0 + 34, d * 6:(d + 1) * 6])
                            p.copy("dve", tk["gl"][hsl, n0:n0 + 34, d * 6:(d + 1) * 6], r3(pl[hsl, 0:204]))
                if debug.get("gdn_stop") == 6:
                    return
                p.ts("dve", tk["egc"][:], tk["gc"][:], -100.0, ALU.max)
                p.act(tk["egc"][:], tk["egc"][:], AF.Exp)
                if debug.get("gdn_stop") == 7:
                    if debug.get("gdn_dump"):
                        p.dma(dbg[:, :], tk[debug["gdn_dump"]][:, :, :].rearrange("p n r -> p (n r)"))
                    return
                p.ts("dve", tk["egl"][:], tk["gl"][:], -100.0, ALU.max)
                p.act(tk["egl"][:], tk["egl"][:], AF.Exp)
                if debug.get("gdn_stop") == 8:
                    return
                p.tt("dve", tk["ed"][:], tk["gl"][:], tk["gc"][:], ALU.subtract)
                p.ts("dve", tk["ed"][:], tk["ed"][:], -100.0, ALU.max)
                p.act(tk["ed"][:], tk["ed"][:], AF.Exp)
                p.tt("dve", tk["nbe"][:], tk["beta"][:], tk["egc"][:], ALU.mult)
                p.ts("dve", tk["nbe"][:], tk["nbe"][:], -1.0, ALU.mult)

                if debug.get("gdn_stop") == 1:
                    return
                cw = p.sbuf("g_cw", [128, 9, 5], F32)
                p.dma(cw[:], gdn_cw[:, l])
                nw = p.sbuf("g_nw", [128, DEPTH], F32)
                p.dma(nw[:], gdn_nw[:])
                ones128 = p.sbuf("g_ones", [128, 128], F32)
                p.memset("dve", ones128[:], 0.0)
                p.memset("dve", ones128[0:64, 0:64], 1.0)
                p.memset("dve", ones128[64:128, 64:128], 1.0)
                eps_g = p.sbuf("g_eps", [128, 1], F32)
                p.memset("dve", eps_g[:], EPS)

                raw = p.sbuf("g_raw", [128, T], F32)
                qkv = [p.sbuf("g_%s" % nm, [128, T], F32) for nm in ("q", "k", "v")]
                oacc = p.sbuf("g_oacc", [128, T], F32)
                tmpn = [p.sbuf("g_tmpn%d" % i, [128, 512], F32) for i in range(2)]
                obs = [p.sbuf("g_ob%d" % i, [128, 512], BF16) for i in range(2)]
                NU = 4
                wk = {nm: [p.sbuf("g_w%s%d" % (nm, u), [128, 64], F32) for u in range(NU)] for nm in
                      ("dg", "Dx", "Di", "Ds", "M", "T0", "T1", "At", "AtT", "bV", "RHS", "vn", "Qg",
                       "Kd", "Kt", "Qt", "S", "dg2")}
                for nm in ("XY0", "XY1"):
                    wk[nm] = [p.sbuf("g_w%s%d" % (nm, u), [128, 128], F32) for u in range(NU)]

                for nm_ in wk:
                    for u in range(NU):
                        p.memset("pool", wk[nm_][u][:, :], 0.0)
                for hp in range(3):
                    for wi in range(3):
                        row0 = 256 + wi * 384 + hp * 128
                        tile_i = wi * 3 + hp
                        dst = qkv[wi]
                        for c in range(0, T, 1088):
                            p.dma(raw.k(c)[:, c:c + 1088], pT[row0:row0 + 128, c:c + 1088])
                        for (a_, b_) in ((0, LC), (LC, T)):
                            p.ts("dve", dst[:, a_:b_], raw[:, a_:b_], cw[:, tile_i, 2:3], ALU.mult)
                            for j in (0, 1, 3, 4):
                                sft = j - 2
                                lo, hi = max(a_, a_ - sft), min(b_, b_ - sft)
                                p.stt("dve", dst[:, lo:hi], raw[:, lo + sft:hi + sft], cw[:, tile_i, j:j + 1], dst[:, lo:hi], ALU.mult, ALU.add)
                        p.act(dst[:, :], dst[:, :], AF.Silu)
                        if wi < 2:
                            for bi_, (s_, n, isctx) in enumerate(token_blocks(512)):
                                tq_ = tmpn[bi_ % 2]
                                p.tt("pool", tq_[:, 0:n], dst[:, s_:s_ + n], dst[:, s_:s_ + n], ALU.mult)
                                pss = ps[2 + bi_ % 2]
                                p.mm(pss[:, 0:n], ones128[:, :], tq_[:, 0:n])
                                p.act(tq_[:, 0:n], pss[:, 0:n], AF.Sqrt, bias=eps_g[:, 0:1])
                                p.recip(tq_[:, 0:n], tq_[:, 0:n])
                                if wi == 0:
                                    p.stt("dve", dst[:, s_:s_ + n], dst[:, s_:s_ + n], 0.125, tq_[:, 0:n], ALU.mult, ALU.mult)
                                else:
                                    p.tt("dve", dst[:, s_:s_ + n], dst[:, s_:s_ + n], tq_[:, 0:n], ALU.mult)
                    if debug.get("gdn_stop") == 2:
                        return
                    qt, kt, vt = qkv
                    p.memset("pool", oacc[:], 0.0)
                    for u in range(NU):
                        p.memset("dve", wk["S"][u][:, :], 0.0)
                    pslots_u = {hh * 2 + d: PsumSlots([0, 1, 2, 3] if d == 0 else [4, 5, 6, 7], hh) for hh in range(2) for d in range(2)}
                    for step in range(debug.get("gdn_steps", NCK)):
                        units = []
                        for hh in range(2):
                            for d in range(2):
                                n = step if d == 0 else BWD_ORDER[step]
                                hs, cs = slice(hh * 64, hh * 64 + 64), slice(n * 64, n * 64 + 64)
                                ci = d * 6 + hp * 2 + hh
                                c = {"ps": pslots_u[hh * 2 + d], "u": hh * 2 + d, "hh": hh, "d": d, "n": n, "hs": hs, "cs": cs, "p0": hh * 64,
                                     "col": (lambda nm, hs=hs, n=n, ci=ci: tk[nm][hs, n, ci:ci + 1]),
                                     "W": (lambda nm, u=hh * 2 + d, hs=hs: wk[nm][u][hs, :])}
                                units.append(c)
                        for c in units:
                            hs, cs, W, col, p0 = c["hs"], c["cs"], c["W"], c["col"], c["p0"]
                            Ih = cm[hs, CI, :]
                            pslots = c["ps"]
                            pslots.group()
                            c["pK"], c["pQ"], c["pV"] = pslots.get(p0), pslots.get(p0), pslots.get(p0)
                            p.tr(c["pK"], kt[hs, cs], Ih)
                            p.tr(c["pQ"], qt[hs, cs], Ih)
                            p.tr(c["pV"], vt[hs, cs], Ih)
                            c["pKK"], c["pQK"] = pslots.get(p0), pslots.get(p0)
                            p.mm(c["pKK"], kt[hs, cs], kt[hs, cs])
                            p.mm(c["pQK"], qt[hs, cs], kt[hs, cs])
                            p.ts("dve", W("dg"), Ih, col("gc"), ALU.mult)
                            c["pG"] = pslots.get(p0)
                            p.mm(c["pG"], cm[hs, CONES, :], W("dg"))
                        if debug.get("gdn_stage", 9) <= 1:
                            continue
                        for c in units:
                            hs, cs, W, col, p0, d = c["hs"], c["cs"], c["W"], c["col"], c["p0"], c["d"]
                            mi, ms_ = (CL, CLS) if d == 0 else (CU, CUS)
                            s2 = debug.get("gdn_s2", 99)
                            e12 = "dve"
                            p.copy(e12, W("Kt"), c["pK"])
                            p.copy(e12, W("Qt"), c["pQ"])
                            if s2 >= 3:
                                p.ts("dve", W("bV"), c["pV"], col("beta"), ALU.mult)
                            if s2 >= 4:
                                p.ts("dve", W("Dx"), c["pG"], col("gc"), ALU.subtract, 0.0, ALU.max)
                            if s2 >= 5:
                                p.act(W("Dx"), W("Dx"), AF.Exp, scale=-1.0)
                            if s2 >= 6:
                                p.tt("pool", W("Di"), W("Dx"), cm[hs, mi, :], ALU.mult)
                                p.tt("pool", W("Ds"), W("Dx"), cm[hs, ms_, :], ALU.mult)
                            if s2 >= 8:
                                p.stt("dve", W("M"), c["pKK"], col("beta"), W("Ds"), ALU.mult, ALU.mult)
                            if s2 >= 9:
                                p.tt("dve", W("At"), c["pQK"], W("Di"), ALU.mult)
                            if s2 >= 10:
                                p.ts("dve", W("dg2"), cm[hs, CI, :], col("egc"), ALU.mult)
                            if s2 >= 11:
                                p.ts("pool", W("Kd"), W("Kt"), col("ed"), ALU.mult)
                        if debug.get("gdn_stage", 9) <= 2:
                            continue
                        for c in units:
                            hs, cs, W, col, p0 = c["hs"], c["cs"], c["W"], c["col"], c["p0"]
                            Ih = cm[hs, CI, :]
                            pslots = c["ps"]
                            pslots.group()
                            c["pY0"], c["pAT"], c["pKS"], c["pQg"] = pslots.get(p0), pslots.get(p0), pslots.get(p0), pslots.get(p0)
                            p.tr(c["pY0"], W("M"), Ih)
                            p.tr(c["pAT"], W("At"), Ih)
                            p.mm(c["pKS"], kt[hs, cs], W("S"))
                            p.mm(c["pQg"], W("Qt"), W("dg2"))
                        if debug.get("gdn_stage", 9) <= 3:
                            continue
                        for c in units:
                            W, col = c["W"], c["col"]
                            p.copy("dve", W("AtT"), c["pAT"])
                            p.stt("dve", W("RHS"), c["pKS"], col("nbe"), W("bV"), ALU.mult, ALU.add)
                            p.copy("dve", W("Qg"), c["pQg"])
                        if debug.get("gdn_stage", 9) <= 4.5:
                            continue
                        run_interleaved([neumann_gen(c, c["hs"], c["W"]("M"), c["pY0"], c["W"], c["ps"]) for c in units])
                        if debug.get("gdn_stage", 9) <= 4:
                            continue
                        for c in units:
                            W = c["W"]
                            pslots = c["ps"]
                            pslots.group()
                            c["pvn"] = pslots.get(c["p0"])
                            p.mm(c["pvn"], c["TT"], W("RHS"))
                            p.copy("dve", W("vn"), c["pvn"])
                        for c in units:
                            hs, cs, W, col, p0, hh, n = c["hs"], c["cs"], c["W"], c["col"], c["p0"], c["hh"], c["n"]
                            pslots = c["ps"]
                            pslots.group()
                            po = pslots.get(p0)
                            p.mm(po, W("S"), W("Qg"), start=True, stop=False)
                            p.mm(po, W("vn"), W("AtT"), start=False, stop=True)
                            p.tt("dve", oacc.k((hh, n))[hs, cs], oacc.k((hh, n))[hs, cs], po, ALU.add)
                            pS = pslots.get(p0)
                            p.mm(pS, W("Kd"), W("vn"))
                            p.stt("dve", W("S"), W("S"), col("egl"), pS, ALU.mult, ALU.add)
                    zt = raw
                    rowz = 256 + 3 * 384 + hp * 128
                    for c_ in range(0, T, 1088):
                        p.dma(zt.k(c_)[:, c_:c_ + 1088], pT[rowz:rowz + 128, c_:c_ + 1088])
                    for bi_, (s_, n, isctx) in enumerate(token_blocks(512)):
                        tq_ = tmpn[bi_ % 2]
                        p.tt("pool", tq_[:, 0:n], oacc[:, s_:s_ + n], oacc[:, s_:s_ + n], ALU.mult)
                        pss = ps[2 + bi_ % 2]
                        p.mm(pss[:, 0:n], ones128[:, :], tq_[:, 0:n])
                        p.act(tq_[:, 0:n], pss[:, 0:n], AF.Sqrt, bias=eps_g[:, 0:1], scale=1.0 / 64.0)
                        p.recip(tq_[:, 0:n], tq_[:, 0:n])
                        p.stt("dve", tq_[:, 0:n], oacc[:, s_:s_ + n], nw[:, l:l + 1], tq_[:, 0:n], ALU.mult, ALU.mult)
                        p.act(zt[:, s_:s_ + n], zt[:, s_:s_ + n], AF.Silu)
                        p.tt("dve", obs[bi_ % 2][:, 0:n], tq_[:, 0:n], zt[:, s_:s_ + n], ALU.mult)
                        p.dma(yT.k(("g", hp, s_))[256 + hp * 128:256 + (hp + 1) * 128, s_:s_ + n], obs[bi_ % 2][:, 0:n])

        def rwkv_mixer(l):
            O_RW = 1816
            SB = 256
            NSB = T // SB
            CPS = SB // 64
            with p.scope():
                mu = p.sbuf("r_mu", [128, 12], F32)
                p.dma(mu[:], rw_mu[:, l])
                vec = p.sbuf("r_vec", [128, 3, 9], F32)
                p.dma(vec[:], rw_vec[:, l])
                ups = p.sbuf("r_ups", [128, 3, 384], F32)
                p.dma(ups[:], rw_up[:, l])
                cmk = p.sbuf("r_cmk", [128, 6], F32)
                p.dma(cmk[:], cmask[:])
                ones128 = p.sbuf("r_ones", [128, 128], F32)
                p.memset("dve", ones128[:], 0.0)
                p.memset("dve", ones128[0:64, 0:64], 1.0)
                p.memset("dve", ones128[64:128, 64:128], 1.0)
                epsr = p.sbuf("r_eps", [128, 2], F32)
                p.memset("dve", epsr[:, 0:1], EPS)
                p.memset("dve", epsr[:, 1:2], 64e-5)
                with p.scope():
                    raw = p.sbuf("r_raw", [128, T], F32)
                    sh = p.sbuf("r_sh", [128, T], F32)
                    g3 = lambda v: v.rearrange("p (r c) -> p r c", c=64)
                    for i in range(12):
                        rows = slice(O_RW + 128 * i, O_RW + 128 * (i + 1))
                        for c in range(0, T, 1088):
                            p.dma(raw.k(c)[:, c:c + 1088], pT[rows, c:c + 1088])
                        p.memset("pool", sh[:, :], 0.0)
                        X, S_ = g3(raw[:, LC:T]), g3(sh[:, LC:T])
                        p.stt("dve", S_[:, :, 1:64], X[:, :, 0:63], cmk[:, 0:1], S_[:, :, 1:64], ALU.mult, ALU.add)
                        p.stt("dve", S_[:, :, 0:63], X[:, :, 1:64], cmk[:, 1:2], S_[:, :, 0:63], ALU.mult, ALU.add)
                        p.stt("dve", S_[:, 1:64, :], X[:, 0:63, :], cmk[:, 2:3], S_[:, 1:64, :], ALU.mult, ALU.add)
                        p.stt("dve", S_[:, 0:63, :], X[:, 1:64, :], cmk[:, 3:4], S_[:, 0:63, :], ALU.mult, ALU.add)
                        p.stt("dve", sh[:, 1:LC], raw[:, 0:LC - 1], cmk[:, 4:5], sh[:, 1:LC], ALU.mult, ALU.add)
                        p.stt("dve", sh[:, 0:LC - 1], raw[:, 1:LC], cmk[:, 5:6], sh[:, 0:LC - 1], ALU.mult, ALU.add)
                        p.tt("pool", sh[:, :], sh[:, :], raw[:, :], ALU.subtract)
                        p.stt("dve", sh[:, :], sh[:, :], mu[:, i:i + 1], raw[:, :], ALU.mult, ALU.add)
                        for c in range(0, T, 1088):
                            p.dma(pT.k(("rw", i, c))[rows, c:c + 1088], sh[:, c:c + 1088])

                rmask = p.sbuf("r_rmask", [128, SB], F32)
                p.memset("dve", rmask[:, :], 1.0)
                p.memset("dve", rmask[:, :].rearrange("p (n t) -> p n t", t=64)[:, :, 0:1], 0.0)
                shared = {nm: p.sbuf("r_" + nm, [128, SB], F32) for nm in ("wdn", "adn", "th")}
                NU = 6
                wk = {nm: [p.sbuf("r_w%s%d" % (nm, u), [128, 64], F32) for u in range(NU)] for nm in
                      ("T0", "T1", "RHS", "Z", "nZ", "S")}
                for nm in ("XY0", "XY1"):
                    wk[nm] = [p.sbuf("r_w%s%d" % (nm, u), [128, 128], F32) for u in range(NU)]
                wk["G1"] = [p.sbuf("r_wG1_%d" % u, [128, 512], F32) for u in range(NU)]
                G1IDX = {"M": 0, "MT": 1, "LakT": 2, "LrkT": 3, "LrbT": 4, "Vt": 5, "Kt": 6, "Bt": 7}
                maskg = p.sbuf("r_maskg", [128, 512], F32)
                for nm_ in wk:
                    for u in range(NU):
                        p.memset("pool", wk[nm_][u][:, :], 0.0)
                pt = {(hp, nm): p.sbuf("r_%s%d" % (nm, hp), [128, SB], F32) for hp in range(3) for nm in
                      ("r", "k", "v", "kk", "lw", "lp", "E", "Ei", "ic", "A", "B", "K", "R", "ob")}
                tmp = p.sbuf("r_tmp", [128, SB], F32)
                colblocks = [(0, 256)]
                bank_sets = {0: [0, 1, 2], 1: [3, 4, 5], 2: [6, 7]}

                def lora_sig(dst, hp, d, which, src, bias_col):
                    dsl = slice(d * 64, d * 64 + 64)
                    for bi_, (c0, nb) in enumerate(colblocks):
                        pz_ = ps[bank_sets[hp][bi_ % len(bank_sets[hp])]]
                        p.mm(pz_[:, 0:nb], ups[dsl, which, hp * 128:(hp + 1) * 128], src[dsl, c0:c0 + nb])
                        p.act(dst[:, c0:c0 + nb], pz_[:, 0:nb], AF.Sigmoid, bias=bias_col)

                def blocksum(dst_fn, src, hp):
                    for bi_, (c0, nb) in enumerate(colblocks):
                        pz_ = ps[bank_sets[hp][bi_ % len(bank_sets[hp])]]
                        p.mm(pz_[:, 0:nb], ones128[:, :], src[:, c0:c0 + nb])
                        dst_fn(pz_[:, 0:nb], c0, nb)

                for d in range(2):
                    if d == 0:
                        groups = [(sb, list(range(sb * CPS, (sb + 1) * CPS))) for sb in range(NSB)]
                    else:
                        groups = [(0, [3, 2, 1, 0])] + [(sb, list(range((sb + 1) * CPS - 1, sb * CPS - 1, -1))) for sb in range(NSB - 1, 0, -1)] \
                                 + [(0, list(range(CPS - 1, 3, -1)))]
                    ordv = (lambda v: v) if d == 0 else rev_view
                    for u in range(NU):
                        p.memset("dve", wk["S"][u][:, :], 0.0)
                    pslots_u = {hp * 2 + hh: PsumSlots(bank_sets[hp], hh) for hp in range(3) for hh in range(2)}
                    ms_, msT, miT = (CLS, CUS, CU) if d == 0 else (CUS, CLS, CL)
                    for i_, mk_ in enumerate((ms_, msT, msT, miT, miT, CONES, CONES, CONES)):
                        p.copy("pool", maskg[:, i_ * 64:(i_ + 1) * 64], cm[:, mk_, :])
                    for (sb, chunks) in groups:
                        if not chunks:
                            continue
                        t0 = sb * SB
                        p.dma(shared["wdn"][:, :], pT[O_RW + 1152:O_RW + 1280, t0:t0 + SB])
                        p.dma(shared["adn"][:, :], pT[O_RW + 1280:O_RW + 1408, t0:t0 + SB])
                        p.act(shared["th"][:, :], shared["wdn"][:, :], AF.Tanh)
                        for hp in range(3):
                            G = lambda nm, hp=hp: pt[(hp, nm)]
                            for wi, nm in enumerate(("r", "k", "v")):
                                row0 = O_RW + wi * 384 + hp * 128
                                p.dma(G(nm)[:, :], pT[row0:row0 + 128, t0:t0 + SB])
                            if d == 1:
                                p.dma(G("ob")[:, :], yrw[hp * 128:(hp + 1) * 128, t0:t0 + SB])
                            lora_sig(G("lw"), hp, d, 0, shared["th"], vec[:, hp, 5 + d:6 + d])
                            p.ts("dve", G("lw")[:, :], G("lw")[:, :], -0.6065306597, ALU.mult)
                            lora_sig(G("ic"), hp, d, 1, shared["adn"], vec[:, hp, 7 + d:8 + d])
                            p.ts("dve", G("kk")[:, :], G("k")[:, :], vec[:, hp, 0:1], ALU.mult)
                            p.tt("pool", tmp[:, :], G("kk")[:, :], G("kk")[:, :], ALU.mult)
                            def kkfin(pv, c0, nb, G=G):
                                p.act(tmp[:, c0:c0 + nb], pv, AF.Sqrt, bias=epsr[:, 0:1])
                            blocksum(kkfin, tmp, hp)
                            p.recip(tmp[:, :], tmp[:, :])
                            p.tt("dve", G("kk")[:, :], G("kk")[:, :], tmp[:, :], ALU.mult)
                            p.scan(ordv(G("lp")[:, :]), ordv(rmask[:, :]) if d == 0 else rmask[:, :], ordv(G("lw")[:, :]), 0.0)
                            p.ts("dve", G("E")[:, :], G("lp")[:, :], 1.0, ALU.mult)
                            p.act(G("E")[:, :], G("E")[:, :], AF.Exp)
                            p.act(G("Ei")[:, :], G("lp")[:, :], AF.Exp, scale=-1.0)
                            p.tt("pool", tmp[:, :], G("lp")[:, :], G("lw")[:, :], ALU.subtract)
                            p.act(tmp[:, :], tmp[:, :], AF.Exp)
                            p.tt("dve", G("A")[:, :], G("kk")[:, :], tmp[:, :], ALU.mult)
                            p.tt("pool", G("R")[:, :], G("r")[:, :], G("E")[:, :], ALU.mult)
                            p.tt("pool", tmp[:, :], G("kk")[:, :], G("ic")[:, :], ALU.mult)
                            p.tt("dve", G("B")[:, :], tmp[:, :], G("Ei")[:, :], ALU.mult)
                            p.ts("dve", tmp[:, :], G("ic")[:, :], -1.0, ALU.add, vec[:, hp, 1:2], ALU.mult)
                            p.ts("dve", tmp[:, :], tmp[:, :], 1.0, ALU.add)
                            p.tt("pool", tmp[:, :], tmp[:, :], G("k")[:, :], ALU.mult)
                            p.tt("dve", G("K")[:, :], tmp[:, :], G("Ei")[:, :], ALU.mult)
                            if d == 0:
                                p.memset("pool", G("ob")[:, :], 0.0)
                        ms_, msT, miT = (CLS, CUS, CU) if d == 0 else (CUS, CLS, CL)
                        for n in chunks:
                            j = n - sb * CPS
                            cs = slice(j * 64, j * 64 + 64)
                            lastcol = j * 64 + (63 if d == 0 else 0)
                            units = []
                            for hp in range(3):
                                for hh in range(2):
                                    hs = slice(hh * 64, hh * 64 + 64)
                                    u = hp * 2 + hh
                                    units.append({"u": u, "hp": hp, "hs": hs, "p0": hh * 64, "ps": pslots_u[u],
                                                  "W": (lambda nm, u=u, hs=hs: (wk["G1"][u][hs, G1IDX[nm] * 64:(G1IDX[nm] + 1) * 64]
                                                                                if nm in G1IDX else wk[nm][u][hs, :])),
                                                  "F": (lambda nm, hp=hp, hs=hs, cs=cs: pt[(hp, nm)][hs, cs])})
                            for c in units:
                                W, F, hs, q = c["W"], c["F"], c["hs"], c["ps"]
                                Ih = cm[hs, CI, :]
                                q.group()
                                c["pg1"] = q.getn(8)
                                g1 = c["pg1"]
                                c["pM"], c["pMT"], c["pLak"], c["pLrk"], c["pLrb"] = (g1[:, i_ * 64:(i_ + 1) * 64] for i_ in range(5))
                                c["pV"], c["pK"], c["pB"] = (g1[:, i_ * 64:(i_ + 1) * 64] for i_ in range(5, 8))
                                p.mm(c["pM"], F("A"), F("B"))
                                p.mm(c["pMT"], F("B"), F("A"))
                                p.mm(c["pLak"], F("K"), F("A"))
                                p.mm(c["pLrk"], F("K"), F("R"))
                                p.mm(c["pLrb"], F("B"), F("R"))
                                p.tr(c["pV"], F("v"), Ih)
                                p.tr(c["pK"], F("K"), Ih)
                                p.tr(c["pB"], F("B"), Ih)
                            for c in units:
                                W, hs = c["W"], c["hs"]
                                p.tt("dve", wk["G1"][c["u"]][hs, :], c["pg1"], maskg[hs, :], ALU.mult)
                            for c in units:
                                W, F, q = c["W"], c["F"], c["ps"]
                                q.group()
                                pr_ = q.get()
                                p.mm(pr_, F("A"), W("S"), start=True, stop=False)
                                p.mm(pr_, W("LakT"), W("Vt"), start=False, stop=True)
                                p.copy("dve", W("RHS"), pr_)
                            run_interleaved([neumann_gen(c, c["hs"], c["W"]("M"), c["W"]("MT"), c["W"], c["ps"]) for c in units])
                            for c in units:
                                W, q = c["W"], c["ps"]
                                q.group()
                                pz_ = q.get()
                                p.mm(pz_, c["TT"], W("RHS"))
                                p.ts("dve", W("nZ"), pz_, -1.0, ALU.mult)
                            for c in units:
                                W, F, q, hs, hp = c["W"], c["F"], c["ps"], c["hs"], c["hp"]
                                q.group()
                                po, pS = q.get(), q.get()
                                p.mm(po, W("S"), F("R"), start=True, stop=False)
                                p.mm(po, W("Vt"), W("LrkT"), start=False, stop=False)
                                p.mm(po, W("nZ"), W("LrbT"), start=False, stop=True)
                                p.mm(pS, W("Kt"), W("Vt"), start=True, stop=False)
                                p.mm(pS, W("Bt"), W("nZ"), start=False, stop=True)
                                obv = pt[(hp, "ob")].k((c["u"], n))[hs, cs]
                                p.tt("dve", obv, obv, po, ALU.add)
                                p.tt("dve", W("S"), W("S"), pS, ALU.add)
                                p.ts("dve", W("S"), W("S"), pt[(hp, "E")][hs, lastcol:lastcol + 1], ALU.mult)
                        for hp in range(3):
                            lo, hi = min(chunks) * 64 - t0, (max(chunks) + 1) * 64 - t0
                            p.dma(yrw.k((hp, sb, lo))[hp * 128:(hp + 1) * 128, t0 + lo:t0 + hi], pt[(hp, "ob")][:, lo:hi])

                rwo16 = [p.sbuf("r_o16_%d" % i, [128, SB], BF16) for i in range(2)]
                gdn_t = shared["wdn"]
                sgd = shared["th"]
                for sb in range(NSB):
                    t0 = sb * SB
                    p.dma(gdn_t[:, :], pT[O_RW + 1408:O_RW + 1536, t0:t0 + SB])
                    p.dma(shared["adn"][:, :], pT[O_RW + 1280:O_RW + 1408, t0:t0 + SB])
                    p.act(sgd[:, :], gdn_t[:, :], AF.Sigmoid)
                    for hp in range(3):
                        G = lambda nm, hp=hp: pt[(hp, nm)]
                        for wi, nm in enumerate(("r", "k", "v")):
                            row0 = O_RW + wi * 384 + hp * 128
                            p.dma(G(nm)[:, :], pT[row0:row0 + 128, t0:t0 + SB])
                        y = G("ob")
                        p.dma(y[:, :], yrw[hp * 128:(hp + 1) * 128, t0:t0 + SB])
                        def mfin(pv, c0, nb, y=y, G=G):
                            p.stt("dve", G("A")[:, c0:c0 + nb], pv, -1.0 / 64.0, y[:, c0:c0 + nb], ALU.mult, ALU.add)
                        blocksum(mfin, y, hp)
                        p.tt("pool", tmp[:, :], G("A")[:, :], G("A")[:, :], ALU.mult)
                        def vfin(pv, c0, nb, G=G):
                            p.act(G("B")[:, c0:c0 + nb], pv, AF.Sqrt, bias=epsr[:, 1:2], scale=1.0 / 64.0)
                        blocksum(vfin, tmp, hp)
                        p.recip(G("B")[:, :], G("B")[:, :])
                        p.tt("dve", G("A")[:, :], G("A")[:, :], G("B")[:, :], ALU.mult)
                        p.ts("dve", G("A")[:, :], G("A")[:, :], vec[:, hp, 3:4], ALU.mult, vec[:, hp, 4:5], ALU.add)
                        lora_sig(G("ic"), hp, 0, 1, shared["adn"], vec[:, hp, 7:8])
                        lora_sig(G("E"), hp, 1, 1, shared["adn"], vec[:, hp, 8:9])
                        p.tt("dve", G("ic")[:, :], G("ic")[:, :], G("E")[:, :], ALU.add)
                        p.ts("dve", G("ic")[:, :], G("ic")[:, :], -2.0, ALU.add, vec[:, hp, 1:2], ALU.mult)
                        p.ts("dve", G("ic")[:, :], G("ic")[:, :], 2.0, ALU.add)
                        p.tt("pool", tmp[:, :], G("r")[:, :], G("k")[:, :], ALU.mult)
                        p.stt("dve", tmp[:, :], tmp[:, :], vec[:, hp, 2:3], G("ic")[:, :], ALU.mult, ALU.mult)
                        def bfin(pv, c0, nb, G=G):
                            p.tt("dve", G("K")[:, c0:c0 + nb], pv, G("v")[:, c0:c0 + nb], ALU.mult)
                        blocksum(bfin, tmp, hp)
                        p.tt("dve", G("A")[:, :], G("A")[:, :], G("K")[:, :], ALU.add)
                        ob16 = G("R")
                        for bi_, (c0, nb) in enumerate(colblocks):
                            pz_ = ps[bank_sets[hp][bi_ % len(bank_sets[hp])]]
                            p.mm(pz_[:, 0:nb], ups[:, 2, hp * 128:(hp + 1) * 128], sgd[:, c0:c0 + nb])
                            p.tt("dve", G("B")[:, c0:c0 + nb], G("A")[:, c0:c0 + nb], pz_[:, 0:nb], ALU.mult)
                        o16 = rwo16[hp % 2]
                        p.copy("pool", o16[:, :], G("B")[:, :])
                        p.dma(yT.k(("rw", hp, sb))[640 + hp * 128:640 + (hp + 1) * 128, t0:t0 + SB], o16[:, :])

        for l in range(debug.get("layers", DEPTH)):
            with p.scope():
                win_sb = p.sbuf("win_sb", [128, 8, INC], BF16)
                with p.scope():
                    stg = [p.sbuf("stg_a", [128, 8, 512], F32), p.sbuf("stg_b", [128, 8, 512], F32)]
                    wsrc = w_in.h.ap()[l].rearrange("(k p) n -> p k n", p=128)
                    load_cast(win_sb, lambda c0, c1: wsrc[:, :, c0:c1], INC, w_in, stg)
                xbs = [p.sbuf("xb%d" % i, [128, 8, 512], F32) for i in range(2)]
                xns = [p.sbuf("xn%d" % i, [128, 8, 512], BF16) for i in range(2)]
                sq = p.sbuf("sq", [128, 8, 512], BF16)
                rstd = p.sbuf("rstd", [128, 512], F32)
                evs = [p.sbuf("ev%d" % i, [128, 512], F32) for i in range(4)]
                ntmp = [p.sbuf("ntmp%d" % i, [128, 512], F32) for i in range(2)]
                ei = 0
                for bi, (s, n, isctx) in enumerate(token_blocks(512)):
                    xb, xn = xbs[bi % 2], xns[bi % 2]
                    w = 1 if isctx else 0
                    p.dma(xb[:, :, 0:n], View(xres, xres_v(s, n), None))
                    norm_block(xb, xn, n, coef[:, l, w, 0, :], coef[:, l, w, 1, :], sq, rstd, ps[7], ntmp)
                    for ti, (c0, m) in enumerate(IN_TILES):
                        pst = ps[ti % 6]
                        for k in range(8):
                            p.mm(pst[0:m, 0:n], win_sb[:, k, c0:c0 + m], xn[:, k, 0:n], start=(k == 0), stop=(k == 7))
                        ev = evs[ei % 4]
                        ei += 1
                        p.copy("act" if ti % 2 == 0 else "dve", ev[0:m, 0:n], pst[0:m, 0:n])
                        p.dma(pT.k(("A", bi, ti))[c0:c0 + m, s:s + n], ev[0:m, 0:n])
                    for tq in range(n // 128):
                        pst = ps[6]
                        for k in range(8):
                            p.mm(pst[:, 0:24], xn[:, k, tq * 128:(tq + 1) * 128], win_sb[:, k, 1792:1816], start=(k == 0), stop=(k == 7))
                        ev = evs[ei % 4]
                        ei += 1
                        p.copy("dve", ev[:, 0:24], pst[:, 0:24])
                        p.dma(pBA.k(("A", bi, tq))[s + tq * 128:s + (tq + 1) * 128, :], ev[:, 0:24])


            if debug.get("s5"):
                s5_mixer(l)
            if debug.get("gdn"):
                gdn_mixer(l)
            if debug.get("rwkv"):
                rwkv_mixer(l)
            if debug and debug.get("zero_mix"):
                with p.scope():
                    zb = p.sbuf("zb", [128, 8, 512], BF16)
                    p.memset("pool", zb[:], 0.0)
                    for (s, n, isctx) in token_blocks(512):
                        for (flag, ka, kb) in (("s5", 0, 2), ("gdn", 2, 5), ("rwkv", 5, 8)):
                            if not debug.get(flag):
                                p.dma(View(yT, yT_v(s, n)[:, ka:kb, :], ("z", s, ka)), zb[:, ka:kb, 0:n])

            with p.scope():
                wo_sb = p.sbuf("wo_sb", [128, 8, D], BF16)
                w1_sb = p.sbuf("w1_sb", [128, 8, 4 * D], BF16)
                w2_sb = p.sbuf("w2_sb", [128, 32, D], BF16)
                with p.scope():
                    stg = [p.sbuf("stg_a", [128, 8, 512], F32), p.sbuf("stg_b", [128, 8, 512], F32)]
                    s0 = w_out.h.ap()[l].rearrange("(k p) n -> p k n", p=128)
                    load_cast(wo_sb, lambda c0, c1: s0[:, :, c0:c1], D, w_out, stg)
                    s1 = w1.h.ap()[l].rearrange("(k p) n -> p k n", p=128)
                    load_cast(w1_sb, lambda c0, c1: s1[:, :, c0:c1], 4 * D, w1, stg)
                    s2 = w2.h.ap()[l].rearrange("(k p) n -> p k n", p=128)
                    for kq in range(4):
                        i = 0
                        for c0 in range(0, D, 512):
                            sgb = stg[i % 2]
                            i += 1
                            p.dma(sgb[:], View(w2, s2[:, kq * 8:(kq + 1) * 8, c0:c0 + 512], None))
                            p.copy("dve" if i % 2 else "pool", w2_sb[:, kq * 8:(kq + 1) * 8, c0:c0 + 512], sgb[:])
                NB = 256
                xbs = [p.sbuf("xb%d" % i, [128, 8, NB], F32) for i in range(2)]
                ybs = [p.sbuf("yb%d" % i, [128, 8, NB], BF16) for i in range(2)]
                hn = p.sbuf("hn", [128, 8, NB], BF16)
                sq = p.sbuf("sq", [128, 8, NB], BF16)
                rstd = p.sbuf("rstd", [128, NB], F32)
                hid = p.sbuf("hid", [128, 32, NB], BF16)
                rls = [p.sbuf("rl%d" % i, [128, NB], F32) for i in range(2)]
                ntmp = [p.sbuf("ntmp%d" % i, [128, NB], F32) for i in range(2)]
                last = (l == DEPTH - 1)
                for bi, (s, n, isctx) in enumerate(token_blocks(NB)):
                    if last and isctx:
                        continue
                    w = 1 if isctx else 0
                    xb, yb = xbs[bi % 2], ybs[bi % 2]
                    p.dma(xb[:, :, 0:n], View(xres, xres_v(s, n), None))
                    p.dma(yb[:, :, 0:n], View(yT, yT_v(s, n), None))
                    for dt_ in range(8):
                        pst = ps[dt_ % 4]
                        for k in range(8):
                            p.mm(pst[:, 0:n], wo_sb[:, k, dt_ * 128:(dt_ + 1) * 128], yb[:, k, 0:n], start=(k == 0), stop=(k == 7))
                        p.stt("dve", xb[:, dt_, 0:n], pst[:, 0:n], coef[:, l, w, 2, dt_:dt_ + 1], xb[:, dt_, 0:n], ALU.mult, ALU.add)
                    norm_block(xb, hn, n, coef[:, l, w, 3, :], coef[:, l, w, 4, :], sq, rstd, ps[7], ntmp)
                    for ft in range(32):
                        pst = ps[4 + ft % 3]
                        for k in range(8):
                            p.mm(pst[:, 0:n], w1_sb[:, k, ft * 128:(ft + 1) * 128], hn[:, k, 0:n], start=(k == 0), stop=(k == 7))
                        rl = rls[ft % 2]
                        p.act(rl[:, 0:n], pst[:, 0:n], AF.Relu)
                        p.tt("pool" if ft % 2 == 0 else "dve", hid[:, ft, 0:n], rl[:, 0:n], rl[:, 0:n], ALU.mult)
                    for dt_ in range(8):
                        pst = ps[dt_ % 4]
                        for ft in range(32):
                            p.mm(pst[:, 0:n], w2_sb[:, ft, dt_ * 128:(dt_ + 1) * 128], hid[:, ft, 0:n], start=(ft == 0), stop=(ft == 31))
                        p.stt("dve", xb[:, dt_, 0:n], pst[:, 0:n], coef[:, l, w, 5, dt_:dt_ + 1], xb[:, dt_, 0:n], ALU.mult, ALU.add)
                    if not last:
                        p.dma(View(xres, xres_v(s, n), None), xb[:, :, 0:n])
                    else:
                        for k in range(8):
                            p.act(sq[:, k, 0:n], xb[:, k, 0:n], AF.Square)
                        for k in range(8):
                            p.mm(ps[7][:, 0:n], ones_bf[:], sq[:, k, 0:n], start=(k == 0), stop=(k == 7))
                        rstd_from(rstd[:, 0:n], ps[7][:, 0:n])
                        for k in range(8):
                            p.stt("dve", xb[:, k, 0:n], xb[:, k, 0:n], nrm_sb[:, 4, k:k + 1], rstd[:, 0:n], ALU.mult, ALU.mult)
                        p.dma(View(outT, outT.h.ap().rearrange("(k p) t -> p k t", p=128)[:, :, s - LC:s - LC + n], ("o", s)), xb[:, :, 0:n])

        if debug and debug.get("tap"):
            debug["tap"](p, locals())
        p.wait_all("sp", [outT[:, :]] + ([dbg[:]] if dbg is not None else []))
        p.emit()
    return nc, p


def host_layout(inputs, b):
    f = np.float32
    x = np.asarray(inputs["x"], f)
    ctx = np.asarray(inputs["ctx"], f)
    c = np.asarray(inputs["c"], f)
    c_ctx = np.asarray(inputs["c_ctx"], f)
    fm = lambda v: np.ascontiguousarray(v.reshape(-1, 128).T)
    m = {}
    m["xT"] = np.ascontiguousarray(np.concatenate([ctx[b].T, x[b].T], axis=1))
    m["cT"] = np.ascontiguousarray(np.stack([fm(c[b]), fm(c_ctx)], axis=-1))
    m["mod_w"] = np.asarray(inputs["mod_w"], f)
    mb = np.asarray(inputs["mod_b"], f)
    m["mod_b"] = np.ascontiguousarray(np.stack([fm(mb[l]) for l in range(DEPTH)], axis=1))
    nm = [np.asarray(inputs["norm_mix"], f)[0], np.asarray(inputs["norm_mix"], f)[1],
          np.asarray(inputs["norm_mlp"], f)[0], np.asarray(inputs["norm_mlp"], f)[1], np.asarray(inputs["norm_final"], f)]
    m["nrm"] = np.ascontiguousarray(np.stack([fm(v) for v in nm], axis=1))
    G, P_, Cg = 16, 64, 16
    s5c = np.zeros((DEPTH, 128, 3, 16), f)
    s5B = np.zeros((DEPTH, 128, 2, 8, 128), f)
    s5C = np.zeros((DEPTH, 128, 2, 8, 128), f)
    are, aim, ldt = (np.asarray(inputs[k], f) for k in ("s5_a_re", "s5_a_im", "s5_log_dt"))
    bre, bim, cre, cim = (np.asarray(inputs[k], f) for k in ("s5_b_re", "s5_b_im", "s5_c_re", "s5_c_im"))
    for l in range(DEPTH):
        for d in range(2):
            for j in range(8):
                for gl in range(2):
                    g = 2 * j + gl
                    s5c[l, gl * 64:(gl + 1) * 64, 0, d * 8 + j] = are[l, d, g]
                    s5c[l, gl * 64:(gl + 1) * 64, 1, d * 8 + j] = aim[l, d, g]
                    s5c[l, gl * 64:(gl + 1) * 64, 2, d * 8 + j] = ldt[l, d, g]
        for j in range(8):
            for gl in range(2):
                g = 2 * j + gl
                r0 = 32 * (j % 4) + 16 * gl
                s5B[l, r0:r0 + 16, 0, j, gl * 64:(gl + 1) * 64] = bre[l, g].T
                s5B[l, r0:r0 + 16, 1, j, gl * 64:(gl + 1) * 64] = bim[l, g].T
                s5C[l, gl * 64:(gl + 1) * 64, 0, j, r0:r0 + 16] = cre[l, g].T
                s5C[l, gl * 64:(gl + 1) * 64, 1, j, r0:r0 + 16] = cim[l, g].T
    m["s5c"], m["s5B"], m["s5C"] = s5c, s5B, s5C
    sd, gb = np.asarray(inputs["s5_d"], f), np.asarray(inputs["s5_glu_b"], f)
    m["s5v"] = np.ascontiguousarray(np.stack([np.stack([fm(sd[l]), fm(gb[l])], axis=-1) for l in range(DEPTH)], axis=0))
    m["s5g"] = np.asarray(inputs["s5_glu_w"], f)
    cmat = np.zeros((128, 8, 64), f)
    ii = np.arange(64)
    mats = [np.eye(64), (ii[None, :] <= ii[:, None]), (ii[None, :] < ii[:, None]), (ii[None, :] >= ii[:, None]),
            (ii[None, :] > ii[:, None]), np.ones((64, 64)), np.repeat((ii == 63)[:, None], 64, 1), np.repeat((ii == 0)[:, None], 64, 1)]
    for i_, mt in enumerate(mats):
        cmat[0:64, i_, :] = mt.astype(f)
        cmat[64:128, i_, :] = mt.astype(f)
    m["cmat"] = cmat
    gcv = np.asarray(inputs["gdn_conv"], f)
    m["gdn_cw"] = np.ascontiguousarray(gcv.reshape(DEPTH, 5, 9, 128).transpose(3, 0, 2, 1))
    gal, gdb = np.asarray(inputs["gdn_a_log"], f), np.asarray(inputs["gdn_dt_bias"], f)
    gsc = np.stack([gal.reshape(DEPTH, 12), gdb.reshape(DEPTH, 12)], axis=1)
    m["gdn_sc"] = np.ascontiguousarray(np.broadcast_to(gsc[None], (128, DEPTH, 2, 12)))
    gnw = np.asarray(inputs["gdn_norm"], f)
    m["gdn_nw"] = np.ascontiguousarray(np.concatenate([gnw.T, gnw.T], axis=0))
    g_ = lambda k: np.asarray(inputs[k], f)
    m["rw_mu"] = np.ascontiguousarray(np.stack([fm(g_("rwkv_mu")[l]) for l in range(DEPTH)], axis=1))
    vecs = np.zeros((128, DEPTH, 3, 9), f)
    for l in range(DEPTH):
        srcs = [g_("rwkv_k_k")[l], g_("rwkv_k_a")[l], g_("rwkv_r_k")[l].reshape(-1), g_("rwkv_ln_w")[l], g_("rwkv_ln_b")[l],
                g_("rwkv_w0")[l, 0], g_("rwkv_w0")[l, 1], g_("rwkv_a0")[l, 0], g_("rwkv_a0")[l, 1]]
        for i_, v_ in enumerate(srcs):
            vecs[:, l, :, i_] = fm(v_)
    m["rw_vec"] = vecs
    ups = np.zeros((128, DEPTH, 3, 384), f)
    for l in range(DEPTH):
        ups[:, l, 0] = g_("rwkv_w_up")[l].reshape(128, 384)
        ups[:, l, 1] = g_("rwkv_a_up")[l].reshape(128, 384)
        ups[:, l, 2] = g_("rwkv_g_up")[l]
    m["rw_up"] = ups
    pp = np.arange(128)
    m["cmask"] = np.stack([(pp % 4 == 0), (pp % 4 == 1), (pp % 4 == 2), (pp % 4 == 3), (pp % 2 == 0), (pp % 2 == 1)], axis=1).astype(f)
    m["w_in"] = np.asarray(inputs["w_in"], f)
    m["w_out"] = np.asarray(inputs["w_out"], f)
    m["mlp_w1"] = np.asarray(inputs["mlp_w1"], f)
    m["mlp_w2"] = np.asarray(inputs["mlp_w2"], f)
    return m


def kernel(**inputs):
    nc, _ = build_program()
    in_maps = [host_layout(inputs, core // 2) for core in range(8)]
    res = run_bass_kernel_spmd(nc, in_maps, core_ids=list(range(8)))
    out = np.stack([np.asarray(res.results[2 * b]["outT"]).T for b in range(4)], axis=0)
    return np.ascontiguousarray(out.astype(np.float32))
```
